# Optimizing a Trainium2 kernel written in Bass

```python
import math, functools
import jax, jax.numpy as jnp
from jax import lax
import numpy as np

D_MODEL = 4096
BATCH = 4
SEQ = 2048
DEPTH = 2
DEC_BATCH = 8
DEC_SEQ = 4
PAST_LEN = 16384
PAGE_SIZE = 128

HEAD_DIM = 128
A_CHUNK = 128
A_GROUPS = 8
A_WIDTH = A_GROUPS * HEAD_DIM
B_HEADS = 12
B_WIDTH = B_HEADS * HEAD_DIM
CONV_W = 4
DN_CHUNK = 64
C_HEADS = 12
C_KV_HEADS = 4
C_Q = C_HEADS * HEAD_DIM
C_KV = C_KV_HEADS * HEAD_DIM
IDX_HEADS = 16
IDX_DIM = 128
TOPK_MAX = 256
Q_BLOCK = 128
N_BRANCH = 3
MIX_WIDTH = A_WIDTH + B_WIDTH + C_Q
D_FF = -(-8 * D_MODEL // (3 * 256)) * 256
IN_SPLITS = (A_WIDTH, A_WIDTH, 3 * B_WIDTH, B_WIDTH, B_HEADS, B_HEADS,
             C_Q, C_KV, C_KV, IDX_HEADS * IDX_DIM, IDX_HEADS, IDX_DIM, N_BRANCH * D_MODEL)
IN_WIDTH = sum(IN_SPLITS)
RMS_EPS = 1e-6
LN_EPS = 1e-5
F32 = jnp.float32

kernel_name = 'hybrid_gmlp_gdn_dsa_step'


def rmsnorm(x, g):
    xf = x.astype(F32)
    y = xf * lax.rsqrt(jnp.mean(xf * xf, -1, keepdims=True) + RMS_EPS)
    return (y * g.astype(F32)).astype(x.dtype)


def layernorm(x, g, b):
    xf = x.astype(F32)
    mu = jnp.mean(xf, -1, keepdims=True)
    var = jnp.mean(jnp.square(xf - mu), -1, keepdims=True)
    return ((xf - mu) * lax.rsqrt(var + LN_EPS) * g.astype(F32) + b.astype(F32)).astype(x.dtype)


def l2norm(x):
    return x * lax.rsqrt(jnp.sum(x * x, -1, keepdims=True) + 1e-6)


def split_cols(h):
    out, start = [], 0
    for w in IN_SPLITS:
        out.append(h[..., start:start + w])
        start += w
    return out


def causal_dwconv(x_ext, w):
    T = x_ext.shape[1] - (CONV_W - 1)
    out = x_ext[:, 0:T] * w[0]
    for j in range(1, CONV_W):
        out = out + x_ext[:, j:j + T] * w[j]
    return out


def chunk_mlp(u, v, w_s, b_s):
    Bt, T, _ = u.shape
    C = min(T, A_CHUNK)
    n = T // C
    vr = v.reshape(Bt, n, C, A_GROUPS, HEAD_DIM)
    ws = jnp.tril(w_s[:, :C, :C])
    s = jnp.einsum('gts,bnsgc->bntgc', ws, vr) + b_s[:, :C].T[None, None, :, :, None]
    return u * s.reshape(Bt, T, A_WIDTH)


def gated_delta_chunked(q, k, v, g, beta, S0, chunk):
    Bt, T, H, D = q.shape
    n = T // chunk

    def blk(x):
        x = x.reshape((Bt, n, chunk, H) + x.shape[3:])
        return jnp.moveaxis(x, (1, 3), (0, 2))

    q, k, v, g, beta = blk(q), blk(k), blk(v), blk(g), blk(beta)
    gc = jnp.cumsum(g, -1)
    pos = jnp.arange(chunk)
    incl = pos[:, None] >= pos[None, :]
    strict = pos[:, None] > pos[None, :]
    decay = jnp.exp(jnp.where(incl, gc[..., :, None] - gc[..., None, :], -jnp.inf))
    kb = k * beta[..., None]
    lmat = jnp.where(strict, jnp.einsum('nbhid,nbhjd->nbhij', kb, k) * decay, 0.0)
    eye = jnp.eye(chunk, dtype=F32)
    rhs = jnp.concatenate([v * beta[..., None], kb * jnp.exp(gc)[..., None]], -1)
    sol = lax.linalg.triangular_solve(lmat + eye, rhs, left_side=True, lower=True, unit_diagonal=True)
    value, kcd = sol[..., :D], sol[..., D:]
    attn = jnp.where(incl, jnp.einsum('nbhid,nbhjd->nbhij', q, k) * decay, 0.0)
    qg = q * jnp.exp(gc)[..., None]
    kend = k * jnp.exp(gc[..., -1:] - gc)[..., None]
    gend = jnp.exp(gc[..., -1])

    def step(S, xs):
        val_c, kcd_c, attn_c, qg_c, kend_c, gend_c = xs
        vn = val_c - jnp.einsum('bhcd,bhde->bhce', kcd_c, S)
        o = jnp.einsum('bhcd,bhde->bhce', qg_c, S) + jnp.einsum('bhij,bhje->bhie', attn_c, vn)
        S = S * gend_c[..., None, None] + jnp.einsum('bhcd,bhce->bhde', kend_c, vn)
        return S, o

    S, o = lax.scan(step, S0, (value, kcd, attn, qg, kend, gend))
    o = jnp.moveaxis(o, (0, 2), (1, 3)).reshape(Bt, T, H, D)
    return o, S


def gated_delta_branch(qkv, z, a, b, a_log, dt_bias, o_g, S0, chunk):
    Bt, T, _ = qkv.shape
    q, k, v = [t.reshape(Bt, T, B_HEADS, HEAD_DIM).astype(F32) for t in jnp.split(qkv, 3, -1)]
    q = l2norm(q) * (HEAD_DIM ** -0.5)
    k = l2norm(k)
    beta = jax.nn.sigmoid(b.astype(F32))
    g = -jnp.exp(a_log.astype(F32)) * jax.nn.softplus(a.astype(F32) + dt_bias.astype(F32))
    o, S = gated_delta_chunked(q, k, v, g, beta, S0.astype(F32), chunk)
    o = rmsnorm(o, o_g) * jax.nn.silu(z.reshape(Bt, T, B_HEADS, HEAD_DIM).astype(F32))
    return o.reshape(Bt, T, B_WIDTH).astype(qkv.dtype), S.astype(S0.dtype)


def indexer_scores(iq, iw, ik, mask):
    sc = jax.nn.relu(jnp.einsum('bthd,bsd->bths', iq.astype(F32), ik))
    w = iw.astype(F32) * (IDX_HEADS ** -0.5 * IDX_DIM ** -0.5)
    return jnp.where(mask, jnp.einsum('bth,bths->bts', w, sc), -jnp.inf)


def sparse_attend(q, ks, vs, valid):
    Bt, T, H, D = q.shape
    qg = q.reshape(Bt, T, C_KV_HEADS, H // C_KV_HEADS, D).astype(F32)
    s = jnp.einsum('btkgd,btskd->btkgs', qg, ks.astype(F32)) * (D ** -0.5)
    s = jnp.where(valid[:, :, None, None, :], s, -jnp.inf)
    p = jax.nn.softmax(s, -1)
    o = jnp.einsum('btkgs,btskd->btkgd', p, vs.astype(F32))
    return o.reshape(Bt, T, H * D).astype(q.dtype)


def dsa_prompt(q, k, v, iq, iw, ik):
    Bt, S, H, D = q.shape
    nblk = S // Q_BLOCK
    k_sel = min(TOPK_MAX, S // 4)
    key_pos = jnp.arange(S)
    bidx = jnp.arange(Bt)[:, None, None]
    ikf = ik.astype(F32)

    def to_blocks(a):
        return jnp.moveaxis(a.reshape((Bt, nblk, Q_BLOCK) + a.shape[2:]), 1, 0)

    def one_block(args):
        qb, iqb, iwb, start = args
        qpos = start + jnp.arange(Q_BLOCK)
        mask = (key_pos[None, :] <= qpos[:, None])[None]
        top_val, top_idx = lax.top_k(indexer_scores(iqb, iwb, ikf, mask), k_sel)
        return sparse_attend(qb, k[bidx, top_idx], v[bidx, top_idx], jnp.isfinite(top_val))

    out = lax.map(one_block, (to_blocks(q), to_blocks(iq), to_blocks(iw), jnp.arange(nblk) * Q_BLOCK))
    return jnp.moveaxis(out, 0, 1).reshape(Bt, S, H * D)


def dsa_sample(q, k, v, iq, iw, ik, k_pool, v_pool, kidx_pool, page_table):
    Bt, T, H, D = q.shape
    past = page_table.shape[1] * PAGE_SIZE
    L = past + T
    k_sel = min(TOPK_MAX, L // 4)
    ik_past = kidx_pool[page_table].reshape(Bt, past, IDX_DIM)
    ik_all = jnp.concatenate([ik_past.astype(F32), ik.astype(F32)], 1)
    mask = (jnp.arange(L)[None, :] <= past + jnp.arange(T)[:, None])[None]
    top_val, top_idx = lax.top_k(indexer_scores(iq, iw, ik_all, mask), k_sel)
    bidx = jnp.arange(Bt)[:, None, None]
    is_new = (top_idx >= past)[..., None, None]
    pidx = jnp.minimum(top_idx, past - 1)
    phys = page_table[bidx, pidx // PAGE_SIZE]
    off = pidx % PAGE_SIZE
    nidx = jnp.clip(top_idx - past, 0, T - 1)
    ks = jnp.where(is_new, k[bidx, nidx], k_pool[phys, off])
    vs = jnp.where(is_new, v[bidx, nidx], v_pool[phys, off])
    return sparse_attend(q, ks, vs, jnp.isfinite(top_val))


def trunk_layer(x, p, conv_prev, S0, dn_chunk, attend_c):
    (ln1, w_in, a_ln_g, a_ln_b, a_ws, a_bs, b_conv_w, b_a_log, b_dt_bias, b_out_g,
     w_br, w_o, ln2, w1, w3, w2) = p
    Bt, T, _ = x.shape
    xn = rmsnorm(x, ln1)
    (a_u, a_v, b_qkv, b_z, b_a, b_b, c_q, c_k, c_v, c_iq, c_iw, c_ik, gate_pre) = split_cols(xn @ w_in)
    a_v = layernorm(jax.nn.gelu(a_v), a_ln_g, a_ln_b)
    y_a = chunk_mlp(jax.nn.gelu(a_u), a_v, a_ws, a_bs)
    conv_ext = jnp.concatenate([conv_prev.astype(b_qkv.dtype), b_qkv], 1)
    new_conv = conv_ext[:, -(CONV_W - 1):]
    y_b, S_new = gated_delta_branch(jax.nn.silu(causal_dwconv(conv_ext, b_conv_w)), b_z, b_a, b_b,
                                    b_a_log, b_dt_bias, b_out_g, S0, dn_chunk)
    c_k = c_k.reshape(Bt, T, C_KV_HEADS, HEAD_DIM)
    c_v = c_v.reshape(Bt, T, C_KV_HEADS, HEAD_DIM)
    y_c = attend_c(c_q.reshape(Bt, T, C_HEADS, HEAD_DIM), c_k, c_v,
                   c_iq.reshape(Bt, T, IDX_HEADS, IDX_DIM), c_iw, c_ik)
    gates = jax.nn.sigmoid(gate_pre.astype(F32)).astype(x.dtype).reshape(Bt, T, N_BRANCH, D_MODEL)
    m = (gates[:, :, 0] * (y_a @ w_br[:A_WIDTH])
         + gates[:, :, 1] * (y_b @ w_br[A_WIDTH:A_WIDTH + B_WIDTH])
         + gates[:, :, 2] * (y_c @ w_br[A_WIDTH + B_WIDTH:]))
    x = x + m @ w_o
    hn = rmsnorm(x, ln2)
    x = x + (jax.nn.silu(hn @ w1) * (hn @ w3)) @ w2
    return x, a_v, new_conv, S_new, c_k, c_v, c_ik


def setup_inputs(seed: int = 0) -> dict:
    key = jax.random.key(seed)
    k = jax.random.split(key, 25)

    def nrm(kk, shape, scale):
        return scale * jax.random.normal(kk, shape, F32)

    n_pages = PAST_LEN // PAGE_SIZE
    n_phys = (DEC_BATCH * n_pages * 5) // 4
    page_table = jax.random.permutation(k[7], n_phys)[:DEC_BATCH * n_pages].reshape(DEC_BATCH, n_pages).astype(jnp.int32)
    dt = jnp.exp(jax.random.uniform(k[16], (DEPTH, B_HEADS), F32, math.log(1e-3), math.log(1e-1)))
    return {
        'x_prompt': nrm(k[0], (BATCH, SEQ, D_MODEL), 1.0),
        'x_sample': nrm(k[1], (DEC_BATCH, DEC_SEQ, D_MODEL), 1.0),
        'cache_k': nrm(k[2], (DEPTH, n_phys, PAGE_SIZE, C_KV_HEADS, HEAD_DIM), 1.0),
        'cache_v': nrm(k[3], (DEPTH, n_phys, PAGE_SIZE, C_KV_HEADS, HEAD_DIM), 1.0),
        'cache_kidx': nrm(k[4], (DEPTH, n_phys, PAGE_SIZE, IDX_DIM), 1.0),
        'state_conv': nrm(k[5], (DEPTH, DEC_BATCH, CONV_W - 1, 3 * B_WIDTH), 1.0),
        'state_delta': nrm(k[6], (DEPTH, DEC_BATCH, B_HEADS, HEAD_DIM, HEAD_DIM), 0.05),
        'page_table': page_table,
        'ln1': 1.0 + nrm(k[8], (DEPTH, D_MODEL), 0.02),
        'w_in': nrm(k[9], (DEPTH, D_MODEL, IN_WIDTH), D_MODEL ** -0.5),
        'a_ln_g': 1.0 + nrm(k[10], (DEPTH, A_WIDTH), 0.02),
        'a_ln_b': nrm(k[11], (DEPTH, A_WIDTH), 0.02),
        'a_ws': nrm(k[12], (DEPTH, A_GROUPS, A_CHUNK, A_CHUNK), 0.5 * A_CHUNK ** -0.5),
        'a_bs': 1.0 + nrm(k[13], (DEPTH, A_GROUPS, A_CHUNK), 0.1),
        'b_conv_w': nrm(k[14], (DEPTH, CONV_W, 3 * B_WIDTH), CONV_W ** -0.5),
        'b_a_log': jnp.log(jax.random.uniform(k[15], (DEPTH, B_HEADS), F32, 1.0, 16.0)),
        'b_dt_bias': dt + jnp.log(-jnp.expm1(-dt)),
        'b_out_g': 1.0 + nrm(k[17], (DEPTH, HEAD_DIM), 0.02),
        'w_br': nrm(k[18], (DEPTH, MIX_WIDTH, D_MODEL), (MIX_WIDTH // N_BRANCH) ** -0.5),
        'w_o': nrm(k[19], (DEPTH, D_MODEL, D_MODEL), D_MODEL ** -0.5),
        'ln2': 1.0 + nrm(k[20], (DEPTH, D_MODEL), 0.02),
        'ffn_w1': nrm(k[21], (DEPTH, D_MODEL, D_FF), D_MODEL ** -0.5),
        'ffn_w3': nrm(k[22], (DEPTH, D_MODEL, D_FF), D_MODEL ** -0.5),
        'ffn_w2': nrm(k[23], (DEPTH, D_FF, D_MODEL), D_FF ** -0.5),
        'ln_f': 1.0 + nrm(k[24], (D_MODEL,), 0.02),
    }


def reference(x_prompt, x_sample, cache_k, cache_v, cache_kidx, state_conv, state_delta, page_table,
              ln1, w_in, a_ln_g, a_ln_b, a_ws, a_bs, b_conv_w, b_a_log, b_dt_bias, b_out_g,
              w_br, w_o, ln2, ffn_w1, ffn_w3, ffn_w2, ln_f):
    xp, xs = x_prompt, x_sample
    Bp = xp.shape[0]
    pk, pv, pik, pconv, pdelta = [], [], [], [], []
    sk, sv, sik, sconv, sdelta, schunk = [], [], [], [], [], []
    for l in range(DEPTH):
        p = (ln1[l], w_in[l], a_ln_g[l], a_ln_b[l], a_ws[l], a_bs[l], b_conv_w[l], b_a_log[l],
             b_dt_bias[l], b_out_g[l], w_br[l], w_o[l], ln2[l], ffn_w1[l], ffn_w3[l], ffn_w2[l])
        conv0 = jnp.zeros((Bp, CONV_W - 1, 3 * B_WIDTH), xp.dtype)
        S0 = jnp.zeros((Bp, B_HEADS, HEAD_DIM, HEAD_DIM), F32)
        xp, _, c_new, S_new, k_new, v_new, ik_new = trunk_layer(
            xp, p, conv0, S0, min(DN_CHUNK, xp.shape[1]), dsa_prompt)
        pk.append(k_new); pv.append(v_new); pik.append(ik_new); pconv.append(c_new); pdelta.append(S_new)
        attend_s = functools.partial(dsa_sample, k_pool=cache_k[l], v_pool=cache_v[l],
                                     kidx_pool=cache_kidx[l], page_table=page_table)
        xs, av_new, c_new, S_new, k_new, v_new, ik_new = trunk_layer(
            xs, p, state_conv[l], state_delta[l], xs.shape[1], attend_s)
        sk.append(k_new); sv.append(v_new); sik.append(ik_new); sconv.append(c_new)
        sdelta.append(S_new); schunk.append(av_new)
    y_prompt = rmsnorm(xp, ln_f)
    y_sample = rmsnorm(xs, ln_f)
    return (y_prompt, y_sample,
            jnp.stack(pk), jnp.stack(pv), jnp.stack(pik), jnp.stack(pconv), jnp.stack(pdelta),
            jnp.stack(sk), jnp.stack(sv), jnp.stack(sik), jnp.stack(sconv), jnp.stack(sdelta),
            jnp.stack(schunk))
```

```python
import numpy as np
from contextlib import ExitStack
import concourse.bass as bass
import concourse.mybir as mybir
from concourse.bass_utils import run_bass_kernel_spmd

F32 = mybir.dt.float32
BF16 = mybir.dt.bfloat16
I32 = mybir.dt.int32
AF = mybir.ActivationFunctionType
ALU = mybir.AluOpType
AX = mybir.AxisListType

REAL_CFG = dict(D=4096, T=2048, AG=8, BH=12, CH=12, CKV=4, IH=16, DFF=11008,
                PAGES=128, NPHYS=1280, DEPTH=2, TOPK=256, NB=4, NSB=8)
NEG = -30000.0


def derive(cfg):
    c = dict(cfg)
    c['AW'] = c['AG'] * 128
    c['BW'] = c['BH'] * 128
    c['CQ'] = c['CH'] * 128
    c['CKVW'] = c['CKV'] * 128
    c['IQW'] = c['IH'] * 128
    c['MIX'] = c['AW'] + c['BW'] + c['CQ']
    c['KC'] = c['D'] // 128
    c['NTP'] = c['T'] // 128
    c['NTT'] = c['NTP'] + 1
    c['NT'] = c['T'] + 128
    o = {}
    off = 0
    for name, w in (('au', c['AW']), ('av', c['AW']), ('qkv', 3 * c['BW']), ('z', c['BW']),
                    ('a', c['BH']), ('b', c['BH']), ('cq', c['CQ']), ('ck', c['CKVW']),
                    ('cv', c['CKVW']), ('iq', c['IQW']), ('iw', c['IH']), ('ik', 128),
                    ('gate', 3 * c['D'])):
        o[name] = off
        off += w
    c['off'] = o
    c['INW'] = off
    groups = []
    t = 0
    while t < c['T']:
        g = min(512, c['T'] - t)
        groups.append((t, g))
        t += g
    groups.append((c['T'], 128))
    c['groups'] = groups
    c['KSEL'] = min(c['TOPK'], c['T'] // 4)
    c['PAST'] = c['PAGES'] * 128
    c['KSELS'] = min(c['TOPK'], (c['PAST'] + 4) // 4)
    return c


class Arena:
    def __init__(self, t, nbytes):
        self.t = t
        self.n = nbytes
        self.off = 0
        self.persist = 0

    def alloc(self, shape, dt):
        esz = 2 if dt == BF16 else 4
        n = int(np.prod(shape[1:])) * esz
        n_al = (n + 63) // 64 * 64
        off = self.off
        self.off += n_al
        assert self.off <= self.n, ("arena overflow", self.off, self.n)
        ap = self.t[:, off // 2:(off + n) // 2]
        if dt != BF16:
            ap = ap.bitcast(dt)
        if len(shape) == 3:
            ap = ap.rearrange("p (a b) -> p a b", a=shape[1])
        elif len(shape) == 4:
            ap = ap.rearrange("p (a b c) -> p a b c", a=shape[1], b=shape[2])
        if shape[0] != 128:
            ap = ap[0:shape[0]]
        return ap

    def mark_persist(self):
        self.persist = self.off

    def reset(self):
        self.off = self.persist


class KB:
    def __init__(self, nc):
        self.nc = nc
        self.es = ExitStack()
        self.E = dict(pe=nc.tensor, act=nc.scalar, dve=nc.vector, pool=nc.gpsimd, sp=nc.sync)
        self.esem = {}
        self.ecnt = {}
        for e in ('pe', 'act', 'dve', 'pool'):
            self.esem[e] = self.es.enter_context(nc.semaphore('sem_' + e))
            self.ecnt[e] = 0
        self.known = {e: {} for e in self.E}
        self.res = {}
        self.dsem = {}
        self.nsem = 0
        self.out_toks = []
        self.free_sems = []

    def new_sem(self):
        self.nsem += 1
        return self.es.enter_context(self.nc.semaphore('dsem%d' % self.nsem))

    def _deps(self, reads, writes):
        deps = []
        for r in reads:
            st = self.res.get(r)
            if st and st['w']:
                deps.append(st['w'])
        for w in writes:
            st = self.res.get(w)
            if st:
                if st['w']:
                    deps.append(st['w'])
                deps.extend(st['r'].values())
        return deps

    def _wait(self, e, deps, skip_sem=None):
        for (sem, val, src) in deps:
            if e == 'pe' and src == 'pe':
                continue
            if skip_sem is not None and sem is skip_sem:
                continue
            k = id(sem)
            if self.known[e].get(k, 0) >= val:
                continue
            self.E[e].wait_ge(sem, val)
            self.known[e][k] = val

    def _record(self, tok, reads, writes):
        for r in reads:
            st = self.res.setdefault(r, dict(w=None, r={}))
            st['r'][id(tok[0])] = tok
        for w in writes:
            st = self.res.setdefault(w, dict(w=None, r={}))
            st['w'] = tok
            st['r'] = {}

    def op(self, e, fn, reads=(), writes=()):
        self._wait(e, self._deps(reads, writes))
        ins = fn()
        self.ecnt[e] += 1
        ins.then_inc(self.esem[e], 1)
        tok = (self.esem[e], self.ecnt[e], e)
        self._record(tok, reads, writes)
        return tok

    def dma(self, q, pairs, reads, writes, is_out=False):
        key = writes[0]
        if key not in self.dsem:
            self.dsem[key] = self.free_sems.pop() if self.free_sems else [self.new_sem(), 0]
        ds = self.dsem[key]
        self._wait(q, self._deps(reads, writes), skip_sem=ds[0])
        for p in pairs:
            kw = p[2] if len(p) > 2 else {}
            self.E[q].dma_start(out=p[0], in_=p[1], **kw).then_inc(ds[0], 16)
            ds[1] += 16
        tok = (ds[0], ds[1], 'dma')
        for r in reads:
            st = self.res.setdefault(r, dict(w=None, r={}))
            st['r'][id(tok[0])] = tok
        for w in writes:
            st = self.res.setdefault(w, dict(w=None, r={}))
            st['w'] = tok
            st['r'] = {}
        if is_out:
            self.out_toks.append(key)
        return tok

    def dma_custom(self, q, fn, reads, writes):
        key = writes[0]
        if key not in self.dsem:
            self.dsem[key] = self.free_sems.pop() if self.free_sems else [self.new_sem(), 0]
        ds = self.dsem[key]
        self._wait(q, self._deps(reads, writes), skip_sem=ds[0])
        fn().then_inc(ds[0], 16)
        ds[1] += 16
        tok = (ds[0], ds[1], 'dma')
        for r in reads:
            st = self.res.setdefault(r, dict(w=None, r={}))
            st['r'][id(tok[0])] = tok
        for w in writes:
            st = self.res.setdefault(w, dict(w=None, r={}))
            st['w'] = tok
            st['r'] = {}
        return tok

    def barrier(self):
        toks = [(self.esem[e], self.ecnt[e], e) for e in ('pe', 'act', 'dve', 'pool') if self.ecnt[e] > 0]
        toks += [(ds[0], ds[1], 'dma') for ds in self.dsem.values() if ds[1] > 0]
        for e in self.E:
            for (sem, val, src) in toks:
                if src == e and e == 'pe':
                    continue
                k = id(sem)
                if self.known[e].get(k, 0) >= val:
                    continue
                self.E[e].wait_ge(sem, val)
                self.known[e][k] = val
        self.res = {}
        self.free_sems.extend(self.dsem.values())
        self.dsem = {}

    def finish(self):
        self.barrier()


def build(cfg, stop_after=None, debug_outs=()):
    c = derive(cfg)
    D, T, NT, KC = c['D'], c['T'], c['NT'], c['KC']
    AW, BW, CQ, CKVW, IQW, MIX, DFF = c['AW'], c['BW'], c['CQ'], c['CKVW'], c['IQW'], c['MIX'], c['DFF']
    AG, BH, CH, CKV, IH = c['AG'], c['BH'], c['CH'], c['CKV'], c['IH']
    NTP, NTT, DEPTH = c['NTP'], c['NTT'], c['DEPTH']
    INW = c['INW']
    off = c['off']
    groups = c['groups']
    KF = DFF // 128
    nc = bass.Bass("TRN2", target_bir_lowering=False, dynamic_dma_scratch_size=32768)
    kb = KB(nc)

    def din(name, shape, dt=F32):
        return nc.dram_tensor(name, list(shape), dt, kind="ExternalInput").ap()

    def dout(name, shape, dt=F32):
        return nc.dram_tensor(name, list(shape), dt, kind="ExternalOutput").ap()

    def dscr(name, shape, dt):
        return nc.dram_tensor(name, list(shape), dt).ap()

    xp = din("xp", [T, D])
    xs = din("xs", [4, D])
    cache_k = din("cache_k", [DEPTH, c['NPHYS'] * 128, CKVW])
    cache_v = din("cache_v", [DEPTH, c['NPHYS'] * 128, CKVW])
    cache_kidx = din("cache_kidx", [DEPTH, c['NPHYS'] * 128, 128])
    sconv = din("sconv", [DEPTH, 3, 3 * BW])
    sdelta = din("sdelta", [DEPTH, BH, 128, 128])
    ptab = din("ptab", [1, c['PAGES']], I32)
    ln1 = din("ln1", [DEPTH, D])
    w_in = din("w_in", [DEPTH, D, INW])
    a_ln_g = din("a_ln_g", [DEPTH, AW])
    a_ln_b = din("a_ln_b", [DEPTH, AW])
    a_ws = din("a_ws", [DEPTH, AG, 128, 128])
    a_bs = din("a_bs", [DEPTH, AG, 128])
    b_conv_w = din("b_conv_w", [DEPTH, 4, 3 * BW])
    b_a_log = din("b_a_log", [DEPTH, BH])
    b_dt_bias = din("b_dt_bias", [DEPTH, BH])
    b_out_g = din("b_out_g", [DEPTH, 128])
    w_br = din("w_br", [DEPTH, MIX, D])
    w_o = din("w_o", [DEPTH, D, D])
    ln2 = din("ln2", [DEPTH, D])
    ffn_w1 = din("ffn_w1", [DEPTH, D, DFF])
    ffn_w3 = din("ffn_w3", [DEPTH, D, DFF])
    ffn_w2 = din("ffn_w2", [DEPTH, DFF, D])
    ln_f = din("ln_f", [1, D])
    consts = din("consts", [8, 128, 128])
    consts2 = din("consts2", [128, 512])

    yp = dout("yp", [T, D])
    ys = dout("ys", [4, D])
    okp = dout("okp", [DEPTH, T, CKVW])
    ovp = dout("ovp", [DEPTH, T, CKVW])
    oikp = dout("oikp", [DEPTH, T, 128])
    oconvp = dout("oconvp", [DEPTH, 3, 3 * BW])
    odeltap = dout("odeltap", [DEPTH, BH, 128, 128])
    oks = dout("oks", [DEPTH, 4, CKVW])
    ovs = dout("ovs", [DEPTH, 4, CKVW])
    oiks = dout("oiks", [DEPTH, 4, 128])
    oconvs = dout("oconvs", [DEPTH, 3, 3 * BW])
    odeltas = dout("odeltas", [DEPTH, BH, 128, 128])
    ochunk = dout("ochunk", [DEPTH, 4, AW])

    xresA = dscr("xresA", [D, NT], F32)
    xresB = dscr("xresB", [D, NT], F32)
    gauT = dscr("gauT", [AW, NT], BF16)
    qkvT = dscr("qkvT", [3 * BW, NT], F32)
    qT = dscr("qT", [CQ, NT], BF16)
    kT = dscr("kT", [CKVW, NT], BF16)
    iqT = dscr("iqT", [IQW, NT], BF16)
    ikT = dscr("ikT", [128, NT], BF16)
    gT = dscr("gT", [3 * D, NT], BF16)
    avg = dscr("avg", [NT, AW], F32)
    szTM = dscr("szTM", [NT, BW], BF16)
    smallTM = dscr("smallTM", [NT, 64], F32)
    vTM = dscr("vTM", [NT, CKVW], BF16)
    ymixT = dscr("ymixT", [MIX, NT], BF16)
    mT = dscr("mT", [D, NT], BF16)
    hT = dscr("hT", [DFF, NT], BF16)
    dbg = {}
    for name, shape in debug_outs:
        dbg[name] = dout("dbg_" + name, shape)

    ARENA_BYTES = 188 * 1024
    arena_t = kb.es.enter_context(nc.sbuf_tensor("arena", [128, ARENA_BYTES // 2], BF16))
    ar = Arena(arena_t, ARENA_BYTES)
    PS = [kb.es.enter_context(nc.psum_tensor("ps%d" % i, [128, 512], F32)) for i in range(8)]
    bank_ctr = [0]
    SL = []

    def next_bank(lo=0, hi=8):
        b = lo + bank_ctr[0] % (hi - lo)
        bank_ctr[0] += 1
        return b

    ident_f = ar.alloc([128, 128], F32)
    ident_b = ar.alloc([128, 128], BF16)
    ones_f = ar.alloc([128, 128], F32)
    ones_b = ar.alloc([128, 128], BF16)
    cmask = ar.alloc([128, 8, 128], F32)
    kb.dma('sp', [(cmask, consts.rearrange("k p n -> p k n"))], [], ['cmask'])
    cmask2 = ar.alloc([128, 512], F32)
    kb.dma('sp', [(cmask2, consts2)], [], ['cmask2'])
    kb.op('dve', lambda: nc.vector.tensor_copy(out=ident_f, in_=cmask[:, 0, :]), ['cmask'], ['ident_f'])
    kb.op('dve', lambda: nc.vector.tensor_copy(out=ident_b, in_=cmask[:, 0, :]), ['cmask'], ['ident_b'])
    kb.op('dve', lambda: nc.vector.memset(ones_f, 1.0), [], ['ones_f'])
    kb.op('dve', lambda: nc.vector.memset(ones_b, 1.0), [], ['ones_b'])
    ar.mark_persist()
    kb.barrier()

    xres = [xresA, xresB]

    def phase_loadx():
        ar.reset()
        xt = [ar.alloc([128, D], F32) for _ in range(2)]
        st = [ar.alloc([128, 4, 128], F32) for _ in range(2)]
        si = 0
        for tt in range(NTT):
            b = tt % 2
            if tt < NTP:
                kb.dma('sp', [(xt[b], xp[tt * 128:(tt + 1) * 128, :])], [], [('xt', b)])
            else:
                kb.op('dve', lambda: nc.vector.memset(xt[b], 0.0), [], [('xt', b)])
                kb.dma('sp', [(xt[b][0:4, :], xs)], [], [('xt', b)])
            for k4 in range(KC // 4):
                bank = next_bank()
                for j in range(4):
                    kc = k4 * 4 + j
                    kb.op('pe', lambda: nc.tensor.transpose(PS[bank][:, j * 128:(j + 1) * 128],
                                                            xt[b][:, kc * 128:(kc + 1) * 128], ident_f),
                          [('xt', b), 'ident_f'], [('ps', bank)])
                s = si % 2
                si += 1
                kb.op('act', lambda: nc.scalar.copy(out=st[s], in_=PS[bank][:].rearrange("p (a b) -> p a b", a=4)),
                      [('ps', bank)], [('st', s)])
                dst = xresA[k4 * 512:(k4 + 1) * 512, tt * 128:(tt + 1) * 128].rearrange("(a p) t -> p a t", p=128)
                kb.dma('sp', [(dst, st[s])], [('st', s)], ['xresA'])
        kb.barrier()

    def alloc_actb():
        return ar.alloc([128, KC, NT], BF16)

    def phase_norm(src, lnw_row, ACTB):
        xc = [ar.alloc([128, NT], F32) for _ in range(2)]
        sq = [ar.alloc([128, NT], F32) for _ in range(2)]
        rstd = ar.alloc([128, NT], F32)
        lncol = ar.alloc([128, KC], F32)
        kb.dma('sp', [(lncol, lnw_row.rearrange("o (k p) -> p (o k)", p=128), dict(allow_slow_non_contiguous=True))],
               [], ['lncol'])
        ng = len(groups)
        for kc in range(KC):
            b = kc % 2
            kb.dma('sp', [(xc[b], src[kc * 128:(kc + 1) * 128, :])], [src.name], [('xc', b)])
            kb.op('act', lambda: nc.scalar.activation(out=sq[b], in_=xc[b], func=AF.Square),
                  [('xc', b)], [('sq', b)])
            for gi, (g0, gs) in enumerate(groups):
                kb.op('pe', lambda: nc.tensor.matmul(PS[gi][:, :gs], lhsT=ones_f, rhs=sq[b][:, g0:g0 + gs],
                                                     start=(kc == 0), stop=(kc == KC - 1)),
                      [('sq', b), 'ones_f'], [('ps', gi)])
        for gi, (g0, gs) in enumerate(groups):
            kb.op('act', lambda: nc.scalar.activation(out=rstd[:, g0:g0 + gs], in_=PS[gi][:, :gs], func=AF.Sqrt,
                                                      scale=1.0 / D, bias=1e-6),
                  [('ps', gi)], ['rstd'])
        kb.op('dve', lambda: nc.vector.reciprocal(out=rstd, in_=rstd), ['rstd'], ['rstd'])
        for kc in range(KC):
            b = kc % 2
            kb.dma('sp', [(xc[b], src[kc * 128:(kc + 1) * 128, :])], [src.name], [('xc', b)])
            kb.op('dve', lambda: nc.vector.scalar_tensor_tensor(out=ACTB[:, kc, :], in0=xc[b], scalar=lncol[:, kc:kc + 1],
                                                                in1=rstd, op0=ALU.mult, op1=ALU.mult),
                  [('xc', b), 'rstd', 'lncol'], [('actb', kc)])

    slab_state = dict(i=0)

    def alloc_slabs(nbuf, kcn, cw):
        return [ar.alloc([128, kcn, cw], BF16) for _ in range(nbuf)]

    def load_slab(slabs, wap, r0, kcn, c0, cw):
        b = slab_state['i'] % len(slabs)
        slab_state['i'] += 1
        src = wap[r0 * 128:(r0 + kcn) * 128, c0:c0 + cw].rearrange("(k p) n -> p k n", p=128)
        kb.dma('pool', [(slabs[b][:, 0:kcn, 0:cw], src)], [], [('slab', b)])
        return b

    def run_items(slabs, items):
        def do_load(it):
            return [load_slab(slabs, *la) for la in it[0]]
        if not items:
            return
        nxt = do_load(items[0])
        for i, it in enumerate(items):
            cur_b = nxt
            if i + 1 < len(items):
                nxt = do_load(items[i + 1])
            it[1](cur_b)

    def dense_fm_items(ACTB, wap, c0, ncols, kcn, epi, grp=None, cw=256):
        grp = grp or groups
        items = []
        for s0 in range(0, ncols, cw):
            w = min(cw, ncols - s0)

            def compute(bufs, s0=s0, w=w):
                b = bufs[0]
                for mc in range(w // 128):
                    for gi, (g0, gs) in enumerate(grp):
                        bank = next_bank()
                        for kc in range(kcn):
                            kb.op('pe', lambda: nc.tensor.matmul(PS[bank][:, :gs], lhsT=SL[b][:, kc, mc * 128:(mc + 1) * 128],
                                                                 rhs=ACTB[:, kc, g0:g0 + gs], start=(kc == 0), stop=(kc == kcn - 1)),
                                  [('slab', b), ('actb', kc)], [('ps', bank)])
                        epi(bank, (s0 // 128) + mc, gi, g0, gs)
            items.append(([(wap, 0, kcn, c0 + s0, w)], compute))
        return items

    def dense_tm_items(ACTB, wap, c0, ncols, kcn, epi, tiles, cw=256):
        items = []
        for s0 in range(0, ncols, cw):
            w = min(cw, ncols - s0)

            def compute(bufs, s0=s0, w=w):
                b = bufs[0]
                for tt in tiles:
                    bank = next_bank()
                    for kc in range(kcn):
                        kb.op('pe', lambda: nc.tensor.matmul(PS[bank][:, :w], lhsT=ACTB[:, kc, tt * 128:(tt + 1) * 128],
                                                             rhs=SL[b][:, kc, 0:w], start=(kc == 0), stop=(kc == kcn - 1)),
                              [('slab', b), ('actb', kc)], [('ps', bank)])
                    epi(bank, s0, w, tt)
            items.append(([(wap, 0, kcn, c0 + s0, w)], compute))
        return items

    def phase_win(l, ACTB):
        slabs = alloc_slabs(2, KC, 256)
        SL[:] = slabs
        items = []
        stf = [ar.alloc([128, 512], F32) for _ in range(3)]
        stb = [ar.alloc([128, 512], BF16) for _ in range(3)]
        ctr = dict(f=0, b=0)
        W = w_in[l]

        def fm_store(dst, func=AF.Copy, scale=1.0, fp32=False):
            def epi(bank, m, gi, g0, gs):
                if fp32:
                    s = ctr['f'] % 3
                    ctr['f'] += 1
                    buf, key = stf[s], ('stf', s)
                else:
                    s = ctr['b'] % 3
                    ctr['b'] += 1
                    buf, key = stb[s], ('stb', s)
                kb.op('act', lambda: nc.scalar.activation(out=buf[:, :gs], in_=PS[bank][:, :gs], func=func, scale=scale),
                      [('ps', bank)], [key])
                kb.dma('sp', [(dst[m * 128:(m + 1) * 128, g0:g0 + gs], buf[:, :gs])], [key], [dst.name])
            return epi

        all_tiles = list(range(NTT))
        items += dense_fm_items(ACTB, W, off['au'], AW, KC, fm_store(gauT, AF.Gelu_apprx_tanh))
        items += dense_fm_items(ACTB, W, off['qkv'], 3 * BW, KC, fm_store(qkvT, fp32=True))
        items += dense_fm_items(ACTB, W, off['cq'], CQ, KC, fm_store(qT, scale=128 ** -0.5))
        items += dense_fm_items(ACTB, W, off['ck'], CKVW, KC, fm_store(kT))
        items += dense_fm_items(ACTB, W, off['iq'], IQW, KC, fm_store(iqT))
        items += dense_fm_items(ACTB, W, off['ik'], 128, KC, fm_store(ikT))
        items += dense_fm_items(ACTB, W, off['gate'], 3 * D, KC, fm_store(gT, AF.Sigmoid))

        def tm_epi(func, dst, dcol0, fp32, extra=None):
            def epi(bank, s0, w, tt):
                if fp32:
                    s = ctr['f'] % 3
                    ctr['f'] += 1
                    buf, key = stf[s], ('stf', s)
                else:
                    s = ctr['b'] % 3
                    ctr['b'] += 1
                    buf, key = stb[s], ('stb', s)
                kb.op('act', lambda: nc.scalar.activation(out=buf[:, :w], in_=PS[bank][:, :w], func=func),
                      [('ps', bank)], [key])
                if dst is not None:
                    kb.dma('sp', [(dst[tt * 128:(tt + 1) * 128, dcol0 + s0:dcol0 + s0 + w], buf[:, :w])], [key], [dst.name])
                if extra is not None:
                    extra(buf, key, s0, w, tt)
            return epi

        items += dense_tm_items(ACTB, W, off['av'], AW, KC, tm_epi(AF.Gelu_apprx_tanh, avg, 0, True), all_tiles)
        items += dense_tm_items(ACTB, W, off['z'], BW, KC, tm_epi(AF.Silu, szTM, 0, False), all_tiles)
        items += dense_tm_items(ACTB, W, off['a'], 2 * BH, KC, tm_epi(AF.Copy, smallTM, 0, True), all_tiles)
        items += dense_tm_items(ACTB, W, off['iw'], IH, KC, tm_epi(AF.Copy, smallTM, 32, True), all_tiles)

        def kv_extra(buf, key, s0, w, tt):
            for (lo, hi, outp, outs) in ((0, CKVW, okp, oks), (CKVW, 2 * CKVW, ovp, ovs)):
                a = max(lo, s0)
                e = min(hi, s0 + w)
                if a >= e:
                    continue
                if tt < NTP:
                    kb.dma('sp', [(outp[l, tt * 128:(tt + 1) * 128, a - lo:e - lo], buf[:, a - s0:e - s0])], [key], [outp.name], is_out=True)
                else:
                    kb.dma('sp', [(outs[l, :, a - lo:e - lo], buf[0:4, a - s0:e - s0])], [key], [outs.name], is_out=True)
                if lo == CKVW:
                    s = ctr['b'] % 3
                    ctr['b'] += 1
                    kb.op('dve', lambda: nc.vector.tensor_copy(out=stb[s][:, :e - a], in_=buf[:, a - s0:e - s0]), [key], [('stb', s)])
                    kb.dma('sp', [(vTM[tt * 128:(tt + 1) * 128, a - lo:e - lo], stb[s][:, :e - a])], [('stb', s)], [vTM.name])
        items += dense_tm_items(ACTB, W, off['ck'], 2 * CKVW, KC, tm_epi(AF.Copy, None, 0, True, kv_extra), all_tiles)

        def ik_extra(buf, key, s0, w, tt):
            if tt < NTP:
                kb.dma('sp', [(oikp[l, tt * 128:(tt + 1) * 128, :], buf[:, :128])], [key], [oikp.name], is_out=True)
            else:
                kb.dma('sp', [(oiks[l, :, :], buf[0:4, :128])], [key], [oiks.name], is_out=True)
        items += dense_tm_items(ACTB, W, off['ik'], 128, KC, tm_epi(AF.Copy, None, 0, True, ik_extra), all_tiles)

        def conv_extra(buf, key, s0, w, tt):
            if tt < NTP:
                kb.dma('sp', [(oconvp[l, :, s0:s0 + w], buf[125:128, :w])], [key], [oconvp.name], is_out=True)
            else:
                kb.dma('sp', [(oconvs[l, :, s0:s0 + w], buf[1:4, :w])], [key], [oconvs.name], is_out=True)
        items += dense_tm_items(ACTB, W, off['qkv'], 3 * BW, KC, tm_epi(AF.Copy, None, 0, True, conv_extra), [NTP - 1, NTP])
        run_items(slabs, items)
        kb.barrier()

    def phase_wbr(l):
        ar.reset()
        MK = MIX // 128
        YB = ar.alloc([128, MK, NT], BF16)
        for kc in range(MK):
            kb.dma('sp', [(YB[:, kc, :], ymixT[kc * 128:(kc + 1) * 128, :])], [ymixT.name], [('actb', kc)])
        slabs = alloc_slabs(2, MK, 128)
        SL[:] = slabs
        gt = [ar.alloc([128, 3, 512], BF16) for _ in range(2)]
        t0 = [ar.alloc([128, 512], F32) for _ in range(2)]
        t1 = [ar.alloc([128, 512], F32) for _ in range(2)]
        ob = [ar.alloc([128, 512], BF16) for _ in range(2)]
        br = [(0, AW // 128), (AW // 128, (AW + BW) // 128), ((AW + BW) // 128, MK)]
        ctr = [0]
        items = []
        gTv = gT.rearrange("(i m p) t -> m p i t", i=3, p=128)
        for s0 in range(0, D, 128):
            w = min(128, D - s0)

            def compute(bufs, s0=s0, w=w):
                b = bufs[0]
                for mc in range(w // 128):
                    m = s0 // 128 + mc
                    for gi, (g0, gs) in enumerate(groups):
                        s = ctr[0] % 2
                        ctr[0] += 1
                        kb.dma('sp', [(gt[s][:, :, :gs], gTv[m, :, :, g0:g0 + gs])], [gT.name], [('gt', s)])
                        banks = [next_bank() for _ in range(3)]
                        for bi, (k0, k1) in enumerate(br):
                            for kc in range(k0, k1):
                                kb.op('pe', lambda: nc.tensor.matmul(PS[banks[bi]][:, :gs], lhsT=SL[b][:, kc, mc * 128:(mc + 1) * 128],
                                                                     rhs=YB[:, kc, g0:g0 + gs], start=(kc == k0), stop=(kc == k1 - 1)),
                                      [('slab', b), ('actb', kc)], [('ps', banks[bi])])
                        kb.op('dve', lambda: nc.vector.tensor_tensor(out=t0[s][:, :gs], in0=PS[banks[0]][:, :gs], in1=gt[s][:, 0, :gs], op=ALU.mult),
                              [('ps', banks[0]), ('gt', s)], [('t0', s)])
                        kb.op('dve', lambda: nc.vector.tensor_tensor(out=t1[s][:, :gs], in0=PS[banks[1]][:, :gs], in1=gt[s][:, 1, :gs], op=ALU.mult),
                              [('ps', banks[1]), ('gt', s)], [('t1', s)])
                        kb.op('pool', lambda: nc.gpsimd.tensor_tensor(out=t0[s][:, :gs], in0=t0[s][:, :gs], in1=t1[s][:, :gs], op=ALU.add),
                              [('t0', s), ('t1', s)], [('t0', s)])
                        kb.op('dve', lambda: nc.vector.tensor_tensor(out=t1[s][:, :gs], in0=PS[banks[2]][:, :gs], in1=gt[s][:, 2, :gs], op=ALU.mult),
                              [('ps', banks[2]), ('gt', s)], [('t1', s)])
                        kb.op('pool', lambda: nc.gpsimd.tensor_tensor(out=ob[s][:, :gs], in0=t0[s][:, :gs], in1=t1[s][:, :gs], op=ALU.add),
                              [('t0', s), ('t1', s)], [('ob', s)])
                        kb.dma('sp', [(mT[m * 128:(m + 1) * 128, g0:g0 + gs], ob[s][:, :gs])], [('ob', s)], [mT.name])
            items.append(([(w_br[l], 0, MK, s0, w)], compute))
        run_items(slabs, items)
        kb.barrier()

    def resid_epi(srcx, dstx, xt, xo, ctr):
        def epi(bank, m, gi, g0, gs):
            s = ctr[0] % 2
            ctr[0] += 1
            kb.dma('sp', [(xt[s][:, :gs], srcx[m * 128:(m + 1) * 128, g0:g0 + gs])], [srcx.name], [('xt', s)])
            kb.op('dve', lambda: nc.vector.tensor_tensor(out=xo[s][:, :gs], in0=PS[bank][:, :gs], in1=xt[s][:, :gs], op=ALU.add),
                  [('ps', bank), ('xt', s)], [('xo', s)])
            kb.dma('sp', [(dstx[m * 128:(m + 1) * 128, g0:g0 + gs], xo[s][:, :gs])], [('xo', s)], [dstx.name])
        return epi

    def phase_wo(l, srcx, dstx):
        ar.reset()
        MB = ar.alloc([128, KC, NT], BF16)
        for kc in range(KC):
            kb.dma('sp', [(MB[:, kc, :], mT[kc * 128:(kc + 1) * 128, :])], [mT.name], [('actb', kc)])
        slabs = alloc_slabs(2, KC, 256)
        SL[:] = slabs
        xt = [ar.alloc([128, 512], F32) for _ in range(2)]
        xo = [ar.alloc([128, 512], F32) for _ in range(2)]
        items = dense_fm_items(MB, w_o[l], 0, D, KC, resid_epi(srcx, dstx, xt, xo, [0]))
        run_items(slabs, items)
        kb.barrier()

    def phase_w13(l, ACTB):
        slabs = alloc_slabs(4, KC, 128)
        SL[:] = slabs
        sg = [ar.alloc([128, 512], F32) for _ in range(2)]
        hb = [ar.alloc([128, 512], BF16) for _ in range(2)]
        ctr = [0]
        items = []
        for m in range(KF):
            def compute(bufs, m=m):
                b1, b3 = bufs
                for gi, (g0, gs) in enumerate(groups):
                    k1 = next_bank()
                    k3 = next_bank()
                    for (bb, bank) in ((b1, k1), (b3, k3)):
                        for kc in range(KC):
                            kb.op('pe', lambda: nc.tensor.matmul(PS[bank][:, :gs], lhsT=SL[bb][:, kc, 0:128],
                                                                 rhs=ACTB[:, kc, g0:g0 + gs], start=(kc == 0), stop=(kc == KC - 1)),
                                  [('slab', bb), ('actb', kc)], [('ps', bank)])
                    s = ctr[0] % 2
                    ctr[0] += 1
                    kb.op('act', lambda: nc.scalar.activation(out=sg[s][:, :gs], in_=PS[k1][:, :gs], func=AF.Silu),
                          [('ps', k1)], [('sg', s)])
                    kb.op('dve', lambda: nc.vector.tensor_tensor(out=hb[s][:, :gs], in0=PS[k3][:, :gs], in1=sg[s][:, :gs], op=ALU.mult),
                          [('ps', k3), ('sg', s)], [('hb', s)])
                    kb.dma('sp', [(hT[m * 128:(m + 1) * 128, g0:g0 + gs], hb[s][:, :gs])], [('hb', s)], [hT.name])
            items.append(([(ffn_w1[l], 0, KC, m * 128, 128), (ffn_w3[l], 0, KC, m * 128, 128)], compute))
        run_items(slabs, items)
        kb.barrier()

    def phase_w2(l, srcx, dstx):
        ar.reset()
        sgs = []
        cur_g = []
        tot = 0
        for g in groups:
            if tot + g[1] > 640:
                sgs.append(cur_g)
                cur_g, tot = [], 0
            cur_g.append(g)
            tot += g[1]
        sgs.append(cur_g)
        HB = ar.alloc([128, KF, 640], BF16)
        slabs = alloc_slabs(2, KF, 128)
        SL[:] = slabs
        xt = [ar.alloc([128, 512], F32) for _ in range(2)]
        xo = [ar.alloc([128, 512], F32) for _ in range(2)]
        epi = resid_epi(srcx, dstx, xt, xo, [0])
        items = []
        for si, sg_ in enumerate(sgs):
            base = sg_[0][0]
            width = sum(g[1] for g in sg_)
            for m in range(KC):
                def compute(bufs, m=m, sg_=sg_, base=base, width=width, first=(m == 0)):
                    b = bufs[0]
                    if first:
                        for kc in range(KF):
                            kb.dma('sp', [(HB[:, kc, 0:width], hT[kc * 128:(kc + 1) * 128, base:base + width])], [hT.name], [('hbres', kc)])
                    for gi, (g0, gs) in enumerate(sg_):
                        bank = next_bank()
                        for kc in range(KF):
                            kb.op('pe', lambda: nc.tensor.matmul(PS[bank][:, :gs], lhsT=SL[b][:, kc, 0:128],
                                                                 rhs=HB[:, kc, g0 - base:g0 - base + gs], start=(kc == 0), stop=(kc == KF - 1)),
                                  [('slab', b), ('hbres', kc)], [('ps', bank)])
                        epi(bank, m, gi, g0, gs)
                items.append(([(ffn_w2[l], 0, KF, m * 128, 128)], compute))
        run_items(slabs, items)
        kb.barrier()

    def phase_final(srcx, tmpx):
        ar.reset()
        xc = [ar.alloc([128, NT], F32) for _ in range(2)]
        sq = [ar.alloc([128, NT], F32) for _ in range(2)]
        yo = [ar.alloc([128, NT], F32) for _ in range(2)]
        rstd = ar.alloc([128, NT], F32)
        lncol = ar.alloc([128, KC], F32)
        kb.dma('sp', [(lncol, ln_f.rearrange("o (k p) -> p (o k)", p=128), dict(allow_slow_non_contiguous=True))], [], ['lncol'])
        for kc in range(KC):
            b = kc % 2
            kb.dma('sp', [(xc[b], srcx[kc * 128:(kc + 1) * 128, :])], [srcx.name], [('xc', b)])
            kb.op('act', lambda: nc.scalar.activation(out=sq[b], in_=xc[b], func=AF.Square), [('xc', b)], [('sq', b)])
            for gi, (g0, gs) in enumerate(groups):
                kb.op('pe', lambda: nc.tensor.matmul(PS[gi][:, :gs], lhsT=ones_f, rhs=sq[b][:, g0:g0 + gs],
                                                     start=(kc == 0), stop=(kc == KC - 1)),
                      [('sq', b), 'ones_f'], [('ps', gi)])
        for gi, (g0, gs) in enumerate(groups):
            kb.op('act', lambda: nc.scalar.activation(out=rstd[:, g0:g0 + gs], in_=PS[gi][:, :gs], func=AF.Sqrt,
                                                      scale=1.0 / D, bias=1e-6), [('ps', gi)], ['rstd'])
        kb.op('dve', lambda: nc.vector.reciprocal(out=rstd, in_=rstd), ['rstd'], ['rstd'])
        for kc in range(KC):
            b = kc % 2
            kb.dma('sp', [(xc[b], srcx[kc * 128:(kc + 1) * 128, :])], [srcx.name], [('xc', b)])
            kb.op('dve', lambda: nc.vector.scalar_tensor_tensor(out=yo[b], in0=xc[b], scalar=lncol[:, kc:kc + 1],
                                                                in1=rstd, op0=ALU.mult, op1=ALU.mult),
                  [('xc', b), 'rstd', 'lncol'], [('yo', b)])
            kb.dma('sp', [(tmpx[kc * 128:(kc + 1) * 128, :], yo[b])], [('yo', b)], [tmpx.name])
        kb.barrier()
        ar.reset()
        blk = [ar.alloc([128, KC, 128], F32) for _ in range(2)]
        row = [ar.alloc([128, D], F32) for _ in range(2)]
        for tt in range(NTT):
            b = tt % 2
            kb.dma('sp', [(blk[b], tmpx[:, tt * 128:(tt + 1) * 128].rearrange("(k p) t -> p k t", p=128))], [tmpx.name], [('blk', b)])
            for k4 in range(KC // 4):
                bank = next_bank()
                for j in range(4):
                    kc = k4 * 4 + j
                    kb.op('pe', lambda: nc.tensor.transpose(PS[bank][:, j * 128:(j + 1) * 128], blk[b][:, kc, :], ident_f),
                          [('blk', b), 'ident_f'], [('ps', bank)])
                kb.op('act', lambda: nc.scalar.copy(out=row[b][:, k4 * 512:(k4 + 1) * 512], in_=PS[bank][:]),
                      [('ps', bank)], [('row', b)])
            if tt < NTP:
                kb.dma('sp', [(yp[tt * 128:(tt + 1) * 128, :], row[b])], [('row', b)], [yp.name], is_out=True)
            else:
                kb.dma('sp', [(ys, row[b][0:4, :])], [('row', b)], [ys.name], is_out=True)
        kb.barrier()

    def mix_A(l):
        ar.reset()
        WsT = ar.alloc([128, AG, 128], BF16)
        bsf = ar.alloc([1, AG, 128], F32)
        bsrow = ar.alloc([1, AG, 128], BF16)
        lng = ar.alloc([128, AW], F32)
        lnb = ar.alloc([128, AW], F32)
        wtmp = ar.alloc([128, AG, 128], F32)
        kb.dma('sp', [(wtmp, a_ws[l].rearrange("g t s -> t g s"))], [], ['wtmp'])
        kb.dma('sp', [(bsf, a_bs[l:l + 1])], [], ['bsf'])
        kb.dma('sp', [(lng, a_ln_g[l:l + 1, :].to_broadcast([128, AW]))], [], ['lng'])
        kb.dma('sp', [(lnb, a_ln_b[l:l + 1, :].to_broadcast([128, AW]))], [], ['lnb'])
        for g in range(AG):
            bank = next_bank()
            kb.op('pe', lambda: nc.tensor.transpose(PS[bank][:, 0:128], wtmp[:, g, :], ident_f), ['wtmp', 'ident_f'], [('ps', bank)])
            kb.op('dve', lambda: nc.vector.tensor_tensor(out=WsT[:, g, :], in0=PS[bank][:, 0:128], in1=cmask[:, 5, :], op=ALU.mult),
                  [('ps', bank)], ['WsT'])
        kb.op('dve', lambda: nc.vector.tensor_copy(out=bsrow, in_=bsf), ['bsf'], ['bsrow'])
        cs = min(512, AW)
        nch = AW // cs
        avt = [ar.alloc([128, AW], F32) for _ in range(2)]
        avb = [ar.alloc([128, AW], BF16) for _ in range(2)]
        gau = [ar.alloc([128, AG, 128], BF16) for _ in range(2)]
        ya = [ar.alloc([128, AG, 128], BF16) for _ in range(2)]
        st = [ar.alloc([128, nch, 6], F32) for _ in range(2)]
        mv = [ar.alloc([128, 2], F32) for _ in range(2)]
        rs = [ar.alloc([128, 1], F32) for _ in range(2)]
        gauv = gauT.rearrange("(g c) t -> c g t", c=128)
        ymv = ymixT[0:AW, :].rearrange("(g c) t -> c g t", c=128)
        for tt in range(NTT):
            b = tt % 2
            kb.dma('sp', [(avt[b], avg[tt * 128:(tt + 1) * 128, :])], [avg.name], [('avt', b)])
            kb.dma('sp', [(gau[b], gauv[:, :, tt * 128:(tt + 1) * 128])], [gauT.name], [('gau', b)])
            for ci in range(nch):
                kb.op('dve', lambda: nc.vector.bn_stats(out=st[b][:, ci, :], in_=avt[b][:, ci * cs:(ci + 1) * cs]), [('avt', b)], [('st', b)])
            kb.op('dve', lambda: nc.vector.bn_aggr(out=mv[b], in_=st[b].rearrange("p a b -> p (a b)")), [('st', b)], [('mv', b)])
            kb.op('act', lambda: nc.scalar.activation(out=rs[b], in_=mv[b][:, 1:2], func=AF.Sqrt, bias=1e-5, scale=1.0), [('mv', b)], [('rs', b)])
            kb.op('dve', lambda: nc.vector.reciprocal(out=rs[b], in_=rs[b]), [('rs', b)], [('rs', b)])
            kb.op('dve', lambda: nc.vector.tensor_scalar(out=avt[b], in0=avt[b], scalar1=mv[b][:, 0:1], scalar2=rs[b][:, 0:1],
                                                         op0=ALU.subtract, op1=ALU.mult), [('avt', b), ('mv', b), ('rs', b)], [('avt', b)])
            kb.op('pool', lambda: nc.gpsimd.tensor_tensor(out=avt[b], in0=avt[b], in1=lng, op=ALU.mult), [('avt', b), 'lng'], [('avt', b)])
            kb.op('pool', lambda: nc.gpsimd.tensor_tensor(out=avt[b], in0=avt[b], in1=lnb, op=ALU.add), [('avt', b), 'lnb'], [('avt', b)])
            if tt == NTP:
                kb.dma('sp', [(ochunk[l], avt[b][0:4, :])], [('avt', b)], [ochunk.name], is_out=True)
            kb.op('act', lambda: nc.scalar.copy(out=avb[b], in_=avt[b]), [('avt', b)], [('avb', b)])
            for g0 in range(0, AG, 4):
                gn = min(4, AG - g0)
                bank = next_bank()
                for j in range(gn):
                    g = g0 + j
                    kb.op('pe', lambda: nc.tensor.matmul(PS[bank][:, j * 128:(j + 1) * 128], lhsT=avb[b][:, g * 128:(g + 1) * 128],
                                                         rhs=WsT[:, g, :], start=True, stop=False),
                          [('avb', b), 'WsT'], [('ps', bank)])
                    kb.op('pe', lambda: nc.tensor.matmul(PS[bank][:, j * 128:(j + 1) * 128], lhsT=ones_b[0:1, :],
                                                         rhs=bsrow[0:1, g, :], start=False, stop=True),
                          ['bsrow', 'ones_b'], [('ps', bank)])
                kb.op('dve', lambda: nc.vector.tensor_tensor(out=ya[b][:, g0:g0 + gn, :],
                                                             in0=PS[bank][:, 0:gn * 128].rearrange("p (a b) -> p a b", a=gn),
                                                             in1=gau[b][:, g0:g0 + gn, :], op=ALU.mult),
                      [('ps', bank), ('gau', b)], [('ya', b)])
            kb.dma('sp', [(ymv[:, :, tt * 128:(tt + 1) * 128], ya[b])], [('ya', b)], [ymixT.name])
        kb.barrier()

    def mix_C_prompt(l):
        ar.reset()
        G = CH // CKV
        KSEL = c['KSEL']
        ikR = ar.alloc([128, NT], BF16)
        kR = ar.alloc([128, CKV, NT], BF16)
        vR = ar.alloc([128, NTP, CKVW], BF16)
        kb.dma('sp', [(ikR, ikT)], [ikT.name], ['ikR'])
        kb.dma('sp', [(kR, kT.rearrange("(h d) t -> d h t", d=128))], [kT.name], ['kR'])
        kb.dma('sp', [(vR, vTM[0:T, :].rearrange("(n p) c -> p n c", p=128))], [vTM.name], ['vR'])
        iq = [ar.alloc([128, IH, 128], BF16) for _ in range(2)]
        qq = [ar.alloc([128, CH, 128], BF16) for _ in range(2)]
        iw = [ar.alloc([128, IH], F32) for _ in range(2)]
        acc = ar.alloc([128, T], F32)
        work = ar.alloc([128, T], F32)
        sel = ar.alloc([128, T], BF16)
        selT = ar.alloc([128, NTP, 128], BF16)
        tmp = [ar.alloc([128, 512], F32) for _ in range(2)]
        m8 = ar.alloc([128, 8], F32)
        thr = ar.alloc([128, 1], F32)
        pT = [ar.alloc([128, G, 128], BF16) for _ in range(2)]
        pTm = [ar.alloc([128, G, 128], BF16) for _ in range(2)]
        rden = ar.alloc([128, G * 128], F32)
        yc = [ar.alloc([128, G, 128], BF16) for _ in range(2)]
        iqv = iqT.rearrange("(h d) t -> d h t", d=128)
        qv = qT.rearrange("(h d) t -> d h t", d=128)
        ycv = ymixT[AW + BW:MIX, :].rearrange("(h d) t -> d h t", d=128)
        wscale = float(IH ** -0.5 * 128 ** -0.5)
        tctr = [0]
        pctr = [0]
        yctr = [0]
        for qt in range(NTP):
            b = qt % 2
            nk = (qt + 1) * 128
            nkb = qt + 1
            cols = slice(qt * 128, (qt + 1) * 128)
            kb.dma('sp', [(iq[b], iqv[:, :, cols])], [iqT.name], [('iq', b)])
            kb.dma('sp', [(qq[b], qv[:, :, cols])], [qT.name], [('qq', b)])
            kb.dma('sp', [(iw[b], smallTM[qt * 128:(qt + 1) * 128, 32:32 + IH])], [smallTM.name], [('iw', b)])
            kb.op('dve', lambda: nc.vector.tensor_scalar(out=iw[b], in0=iw[b], scalar1=wscale, scalar2=None, op0=ALU.mult),
                  [('iw', b)], [('iw', b)])
            for k0 in range(0, nk, 512):
                kw = min(512, nk - k0)
                for h in range(IH):
                    bank = next_bank(0, 2)
                    kb.op('pe', lambda: nc.tensor.matmul(PS[bank][:, :kw], lhsT=iq[b][:, h, :], rhs=ikR[:, k0:k0 + kw], start=True, stop=True),
                          [('iq', b), 'ikR'], [('ps', bank)])
                    s = tctr[0] % 2
                    tctr[0] += 1
                    kb.op('act', lambda: nc.scalar.activation(out=tmp[s][:, :kw], in_=PS[bank][:, :kw], func=AF.Relu),
                          [('ps', bank)], [('tmp', s)])
                    if h == 0:
                        kb.op('dve', lambda: nc.vector.tensor_scalar(out=acc[:, k0:k0 + kw], in0=tmp[s][:, :kw], scalar1=iw[b][:, 0:1],
                                                                     scalar2=None, op0=ALU.mult), [('tmp', s), ('iw', b)], ['acc'])
                    else:
                        kb.op('dve', lambda: nc.vector.scalar_tensor_tensor(out=acc[:, k0:k0 + kw], in0=tmp[s][:, :kw], scalar=iw[b][:, h:h + 1],
                                                                            in1=acc[:, k0:k0 + kw], op0=ALU.mult, op1=ALU.add),
                              [('tmp', s), ('iw', b), 'acc'], ['acc'])
            kb.op('dve', lambda: nc.vector.tensor_tensor(out=acc[:, cols], in0=acc[:, cols], in1=cmask[:, 4, :], op=ALU.add), ['acc'], ['acc'])
            if nk > KSEL:
                kb.op('act', lambda: nc.scalar.copy(out=work[:, :nk], in_=acc[:, :nk]), ['acc'], ['work'])
                for r in range(KSEL // 8):
                    kb.op('dve', lambda: nc.vector.max(out=m8, in_=work[:, :nk]), ['work'], ['m8'])
                    if r < KSEL // 8 - 1:
                        kb.op('dve', lambda: nc.vector.match_replace(out=work[:, :nk], in_to_replace=m8, in_values=work[:, :nk], imm_value=-3.0e38),
                              ['work', 'm8'], ['work'])
                kb.op('dve', lambda: nc.vector.tensor_scalar(out=thr, in0=m8[:, 7:8], scalar1=-1.0e29, scalar2=None, op0=ALU.max), ['m8'], ['thr'])
                kb.op('dve', lambda: nc.vector.tensor_scalar(out=sel[:, :nk], in0=acc[:, :nk], scalar1=thr[:, 0:1], scalar2=None, op0=ALU.is_ge),
                      ['acc', 'thr'], ['sel'])
            else:
                kb.op('dve', lambda: nc.vector.tensor_scalar(out=sel[:, :nk], in0=acc[:, :nk], scalar1=-1.0e29, scalar2=None, op0=ALU.is_ge),
                      ['acc'], ['sel'])
            for kb0 in range(0, nkb, 8):
                kn = min(8, nkb - kb0)
                bank = 2
                pbf = PS[bank][:].bitcast(BF16)
                for j in range(kn):
                    kb.op('pe', lambda: nc.tensor.transpose(pbf[:, j * 128:(j + 1) * 128], sel[:, (kb0 + j) * 128:(kb0 + j + 1) * 128], ident_b),
                          ['sel', 'ident_b'], [('ps', bank)])
                kb.op('act', lambda: nc.scalar.copy(out=selT[:, kb0:kb0 + kn, :], in_=pbf[:, 0:kn * 128].rearrange("p (a b) -> p a b", a=kn)),
                      [('ps', bank)], ['selT'])
            for kh in range(CKV):
                OT, DEN = 6, 7
                for kbi in range(nkb):
                    sbank = next_bank(3, 6)
                    kb.op('pe', lambda: nc.tensor.matmul(PS[sbank][:, :G * 128], lhsT=kR[:, kh, kbi * 128:(kbi + 1) * 128],
                                                         rhs=qq[b][:, kh * G:(kh + 1) * G, :].rearrange("p a b -> p (a b)"), start=True, stop=True),
                          ['kR', ('qq', b)], [('ps', sbank)])
                    s = pctr[0] % 2
                    pctr[0] += 1
                    kb.op('act', lambda: nc.scalar.activation(out=pT[s].rearrange("p a b -> p (a b)"), in_=PS[sbank][:, :G * 128], func=AF.Exp),
                          [('ps', sbank)], [('pT', s)])
                    for j in range(G):
                        kb.op('dve', lambda: nc.vector.tensor_tensor(out=pTm[s][:, j, :], in0=pT[s][:, j, :], in1=selT[:, kbi, :], op=ALU.mult),
                              [('pT', s), 'selT'], [('pTm', s)])
                    kb.op('pe', lambda: nc.tensor.matmul(PS[OT][:, :G * 128], lhsT=vR[:, kbi, kh * 128:(kh + 1) * 128],
                                                         rhs=pTm[s].rearrange("p a b -> p (a b)"), start=(kbi == 0), stop=(kbi == nkb - 1)),
                          ['vR', ('pTm', s)], [('ps', OT)])
                    kb.op('pe', lambda: nc.tensor.matmul(PS[DEN][:, :G * 128], lhsT=ones_b, rhs=pTm[s].rearrange("p a b -> p (a b)"),
                                                         start=(kbi == 0), stop=(kbi == nkb - 1)),
                          ['ones_b', ('pTm', s)], [('ps', DEN)])
                kb.op('dve', lambda: nc.vector.reciprocal(out=rden, in_=PS[DEN][:, :G * 128]), [('ps', DEN)], ['rden'])
                s = yctr[0] % 2
                yctr[0] += 1
                kb.op('dve', lambda: nc.vector.tensor_tensor(out=yc[s].rearrange("p a b -> p (a b)"), in0=PS[OT][:, :G * 128], in1=rden, op=ALU.mult),
                      [('ps', OT), 'rden'], [('yc', s)])
                kb.dma('sp', [(ycv[:, kh * G:(kh + 1) * G, cols], yc[s])], [('yc', s)], [ymixT.name])
        kb.barrier()

    qnT = dscr("qnT", [BW, NT], F32)
    knT = dscr("knT", [BW, NT], F32)
    vcT = dscr("vcT", [BW, NT], F32)

    def mix_B_conv(l):
        ar.reset()
        NCC = 3 * BH
        CBW = 6 + T + 128
        wc = ar.alloc([128, NCC, 4], F32)
        hs = ar.alloc([128, NCC, 3], F32)
        kb.dma('sp', [(wc[:, :, j], b_conv_w[l, j:j + 1, :].rearrange("o (n c) -> c (o n)", c=128), dict(allow_slow_non_contiguous=True)) for j in range(4)], [], ['wc'])
        kb.dma('sp', [(hs[:, :, j], sconv[l, j:j + 1, :].rearrange("o (n c) -> c (o n)", c=128), dict(allow_slow_non_contiguous=True)) for j in range(3)], [], ['hs'])
        cb = [ar.alloc([128, CBW], F32) for _ in range(2)]
        co = [ar.alloc([128, NT], F32) for _ in range(2)]
        sq = [ar.alloc([128, NT], F32) for _ in range(2)]
        rs = [ar.alloc([128, NT], F32) for _ in range(2)]
        for b in range(2):
            kb.op('dve', lambda: nc.vector.memset(cb[b][:, 0:3], 0.0), [], [('cb', b)])
        i = 0
        for part, dst in ((0, qnT), (1, knT), (2, vcT)):
            for h in range(BH):
                b = i % 2
                i += 1
                n = part * BH + h
                r0 = n * 128
                kb.dma('sp', [(cb[b][:, 3:3 + T], qkvT[r0:r0 + 128, 0:T]), (cb[b][:, 6 + T:6 + T + 128], qkvT[r0:r0 + 128, T:T + 128])],
                       [qkvT.name], [('cb', b)])
                kb.op('act', lambda: nc.scalar.copy(out=cb[b][:, 3 + T:6 + T], in_=hs[:, n, :]), ['hs'], [('cb', b)])
                for (o0, i0, wd) in ((0, 0, T), (T, 3 + T, 128)):
                    kb.op('dve', lambda: nc.vector.tensor_scalar(out=co[b][:, o0:o0 + wd], in0=cb[b][:, i0:i0 + wd], scalar1=wc[:, n, 0:1],
                                                                 scalar2=None, op0=ALU.mult), [('cb', b), 'wc'], [('co', b)])
                    for j in range(1, 4):
                        kb.op('dve', lambda: nc.vector.scalar_tensor_tensor(out=co[b][:, o0:o0 + wd], in0=cb[b][:, i0 + j:i0 + j + wd],
                                                                            scalar=wc[:, n, j:j + 1], in1=co[b][:, o0:o0 + wd],
                                                                            op0=ALU.mult, op1=ALU.add), [('cb', b), 'wc', ('co', b)], [('co', b)])
                kb.op('act', lambda: nc.scalar.activation(out=co[b], in_=co[b], func=AF.Silu), [('co', b)], [('co', b)])
                if part < 2:
                    kb.op('act', lambda: nc.scalar.activation(out=sq[b], in_=co[b], func=AF.Square), [('co', b)], [('sq', b)])
                    for gi, (g0, gs) in enumerate(groups):
                        bank = next_bank()
                        kb.op('pe', lambda: nc.tensor.matmul(PS[bank][:, :gs], lhsT=ones_f, rhs=sq[b][:, g0:g0 + gs], start=True, stop=True),
                              [('sq', b), 'ones_f'], [('ps', bank)])
                        kb.op('act', lambda: nc.scalar.activation(out=rs[b][:, g0:g0 + gs], in_=PS[bank][:, :gs], func=AF.Sqrt, bias=1e-6, scale=1.0),
                              [('ps', bank)], [('rs', b)])
                    kb.op('dve', lambda: nc.vector.reciprocal(out=rs[b], in_=rs[b]), [('rs', b)], [('rs', b)])
                    sc = 128 ** -0.5 if part == 0 else 1.0
                    kb.op('dve', lambda: nc.vector.scalar_tensor_tensor(out=co[b], in0=co[b], scalar=sc, in1=rs[b], op0=ALU.mult, op1=ALU.mult),
                          [('co', b), ('rs', b)], [('co', b)])
                kb.dma('sp', [(dst[h * 128:(h + 1) * 128, :], co[b])], [('co', b)], [dst.name])
        kb.barrier()

    def mix_B_scan(l):
        import os
        LIM = int(os.environ.get('MIXB_LIM', '99'))
        LIMH = int(os.environ.get('LIMH', '0'))
        ar.reset()
        Uf = cmask[:, 1, :]
        NEGI = cmask[:, 2, :]
        STRICT = cmask[:, 3, :]
        vcol = cmask[:, 6, 0:1]
        S = [ar.alloc([128, BH, 128], F32) for _ in range(2)]
        kb.op('dve', lambda: nc.vector.memset(S[0], 0.0), [], [('S', 0, h) for h in range(BH)])
        kb.dma('sp', [(S[1], sdelta[l].rearrange("h k v -> k h v"))], [], [('S', 1, h) for h in range(BH)])
        nalog = ar.alloc([128, BH], F32)
        dtb = ar.alloc([128, BH], F32)
        ogb = ar.alloc([128, 128], F32)
        kb.dma('sp', [(nalog, b_a_log[l:l + 1, :].to_broadcast([128, BH]))], [], ['nalog'])
        kb.dma('sp', [(dtb, b_dt_bias[l:l + 1, :].to_broadcast([128, BH]))], [], ['dtb'])
        kb.dma('sp', [(ogb, b_out_g[l:l + 1, :].to_broadcast([128, 128]))], [], ['ogb'])
        kb.op('act', lambda: nc.scalar.activation(out=nalog, in_=nalog, func=AF.Exp), ['nalog'], ['nalog'])
        kb.op('dve', lambda: nc.vector.tensor_scalar(out=nalog, in0=nalog, scalar1=-1.0, scalar2=None, op0=ALU.mult), ['nalog'], ['nalog'])
        qt_ = [ar.alloc([128, BH, 128], F32) for _ in range(2)]
        kt_ = [ar.alloc([128, BH, 128], F32) for _ in range(2)]
        vt_ = [ar.alloc([128, BH, 128], F32) for _ in range(2)]
        szt = [ar.alloc([128, BH, 128], BF16) for _ in range(2)]
        ab = [ar.alloc([128, 2 * BH], F32) for _ in range(2)]
        g_ = ar.alloc([128, BH], F32)
        beta = ar.alloc([128, BH], F32)
        nbeta = ar.alloc([128, BH], F32)
        egc = ar.alloc([128, BH], F32)
        gendb = ar.alloc([128, BH], F32)
        ekend = ar.alloc([128, BH], F32)
        bg = ar.alloc([128, BH], F32)
        GU = ar.alloc([128, BH, 128], F32)
        GBn = ar.alloc([128, BH, 128], F32)
        oall = ar.alloc([128, BH, 128], F32)
        osq = ar.alloc([128, BH, 128], F32)
        ss = ar.alloc([128, BH], F32)
        ybf = ar.alloc([128, BH, 128], BF16)
        ybT = ar.alloc([128, BH, 128], BF16)
        NSET = int(os.environ.get("NSET", "2"))

        def hset():
            d = {}
            for nm in ('decay', 'P1T', 'attn', 'attnT', 'kend', 'kcdT', 'vn', 'tq'):
                d[nm] = ar.alloc([128, 128], F32)
            d['X'] = ar.alloc([128, 256], F32)
            d['P'] = [ar.alloc([128, 128], F32) for _ in range(7)]
            d['PT'] = [ar.alloc([128, 128], F32) for _ in range(6)]
            return d
        HS = [hset() for _ in range(NSET)]
        qv_ = qnT.rearrange("(h d) t -> d h t", d=128)
        kv_ = knT.rearrange("(h d) t -> d h t", d=128)
        vv_ = vcT.rearrange("(h d) t -> d h t", d=128)
        szv = szTM.rearrange("t (h e) -> t h e", e=128)
        ybv = ymixT[AW:AW + BW, :].rearrange("(h e) t -> e h t", e=128)
        cp = [0]

        def evac(out_ap, ps_ap, reads, writes):
            cp[0] += 1
            if cp[0] % 2:
                kb.op('act', lambda: nc.scalar.copy(out=out_ap, in_=ps_ap), reads, writes)
            else:
                kb.op('dve', lambda: nc.vector.tensor_copy(out=out_ap, in_=ps_ap), reads, writes)

        hctr = [0]
        for tt in range(NTT):
            b = tt % 2
            si = 0 if tt < NTP else 1
            cols = slice(tt * 128, (tt + 1) * 128)
            kb.dma('sp', [(qt_[b], qv_[:, :, cols])], [qnT.name], [('qt', b)])
            kb.dma('sp', [(kt_[b], kv_[:, :, cols])], [knT.name], [('kt', b)])
            kb.dma('sp', [(vt_[b], vv_[:, :, cols])], [vcT.name], [('vt', b)])
            kb.dma('sp', [(szt[b], szv[tt * 128:(tt + 1) * 128])], [szTM.name], [('szt', b)])
            kb.dma('sp', [(ab[b], smallTM[tt * 128:(tt + 1) * 128, 0:2 * BH])], [smallTM.name], [('ab', b)])
            kb.op('act', lambda: nc.scalar.activation(out=beta, in_=ab[b][:, BH:2 * BH], func=AF.Sigmoid), [('ab', b)], ['beta'])
            kb.op('dve', lambda: nc.vector.tensor_tensor(out=g_, in0=ab[b][:, 0:BH], in1=dtb, op=ALU.add), [('ab', b), 'dtb'], ['g'])
            kb.op('act', lambda: nc.scalar.activation(out=g_, in_=g_, func=AF.Exp), ['g'], ['g'])
            kb.op('act', lambda: nc.scalar.activation(out=g_, in_=g_, func=AF.Ln, bias=1.0, scale=1.0), ['g'], ['g'])
            kb.op('dve', lambda: nc.vector.tensor_tensor(out=g_, in0=g_, in1=nalog, op=ALU.mult), ['g', 'nalog'], ['g'])
            if si == 1:
                kb.op('dve', lambda: nc.vector.tensor_scalar(out=g_, in0=g_, scalar1=vcol, scalar2=None, op0=ALU.mult), ['g'], ['g'])
                kb.op('dve', lambda: nc.vector.tensor_scalar(out=beta, in0=beta, scalar1=vcol, scalar2=None, op0=ALU.mult), ['beta'], ['beta'])
            kb.op('dve', lambda: nc.vector.tensor_scalar(out=nbeta, in0=beta, scalar1=-1.0, scalar2=None, op0=ALU.mult), ['beta'], ['nbeta'])
            bk1 = next_bank()
            kb.op('pe', lambda: nc.tensor.matmul(PS[bk1][:, 0:BH], lhsT=Uf, rhs=g_, start=True, stop=True), ['g'], [('ps', bk1)])
            bk2 = next_bank()
            kb.op('pe', lambda: nc.tensor.matmul(PS[bk2][:, 0:BH], lhsT=ones_f, rhs=g_, start=True, stop=True), ['g'], [('ps', bk2)])
            kb.op('act', lambda: nc.scalar.activation(out=egc, in_=PS[bk1][:, 0:BH], func=AF.Exp), [('ps', bk1)], ['egc'])
            kb.op('act', lambda: nc.scalar.activation(out=gendb, in_=PS[bk2][:, 0:BH], func=AF.Exp), [('ps', bk2)], ['gendb'])
            kb.op('dve', lambda: nc.vector.tensor_copy(out=ekend, in_=PS[bk2][:, 0:BH]), [('ps', bk2)], ['ekend'])
            kb.op('dve', lambda: nc.vector.tensor_tensor(out=ekend, in0=ekend, in1=PS[bk1][:, 0:BH], op=ALU.subtract), [('ps', bk1), 'ekend'], ['ekend'])
            kb.op('act', lambda: nc.scalar.activation(out=ekend, in_=ekend, func=AF.Exp), ['ekend'], ['ekend'])
            kb.op('dve', lambda: nc.vector.tensor_tensor(out=bg, in0=beta, in1=egc, op=ALU.mult), ['beta', 'egc'], ['bg'])
            kb.op('dve', lambda: nc.vector.tensor_tensor(out=GU, in0=Uf.unsqueeze(1).to_broadcast([128, BH, 128]),
                                                         in1=g_.unsqueeze(2).to_broadcast([128, BH, 128]), op=ALU.mult), ['g'], ['GU'])
            kb.op('dve', lambda: nc.vector.tensor_scalar(out=GBn, in0=g_.unsqueeze(2).to_broadcast([128, BH, 128]), scalar1=-1.0, scalar2=None,
                                                         op0=ALU.mult), ['g'], ['GBn'])
            if LIM <= 1:
                kb.barrier()
                return
            for h in range(int(os.environ.get('H0', '0')), BH):
                hs_i = hctr[0] % NSET
                hctr[0] += 1
                H = HS[hs_i]
                K_ = lambda nm: ('h', hs_i, nm)
                qh, kh_, vh = qt_[b][:, h, :], kt_[b][:, h, :], vt_[b][:, h, :]
                bd = next_bank()
                kb.op('pe', lambda: nc.tensor.matmul(PS[bd][:, 0:128], lhsT=GU[:, h, :], rhs=ones_f, start=True, stop=False), ['GU'], [('ps', bd)])
                kb.op('pe', lambda: nc.tensor.matmul(PS[bd][:, 0:128], lhsT=GBn[:, h, :], rhs=Uf, start=False, stop=False), ['GBn'], [('ps', bd)])
                kb.op('pe', lambda: nc.tensor.matmul(PS[bd][:, 0:128], lhsT=ident_f, rhs=NEGI, start=False, stop=True), [], [('ps', bd)])
                kb.op('act', lambda: nc.scalar.activation(out=H['decay'], in_=PS[bd][:, 0:128], func=AF.Exp), [('ps', bd)], [K_('decay')])
                if LIM <= 2 and h >= LIMH:
                    kb.barrier()
                    return
                bkk = next_bank()
                kb.op('pe', lambda: nc.tensor.matmul(PS[bkk][:, 0:128], lhsT=kh_, rhs=kh_, start=True, stop=True), [('kt', b)], [('ps', bkk)])
                kb.op('pe', lambda: nc.tensor.matmul(PS[bkk][:, 128:256], lhsT=qh, rhs=kh_, start=True, stop=True), [('kt', b), ('qt', b)], [('ps', bkk)])
                kb.op('dve', lambda: nc.vector.scalar_tensor_tensor(out=H['P1T'], in0=PS[bkk][:, 0:128], scalar=nbeta[:, h:h + 1], in1=H['decay'],
                                                                    op0=ALU.mult, op1=ALU.mult), [('ps', bkk), 'nbeta', K_('decay')], [K_('P1T')])
                kb.op('pool', lambda: nc.gpsimd.tensor_tensor(out=H['PT'][0], in0=H['P1T'], in1=STRICT, op=ALU.mult), [K_('P1T')], [K_('PT0')])
                kb.op('dve', lambda: nc.vector.tensor_tensor(out=H['attn'], in0=PS[bkk][:, 128:256], in1=H['decay'], op=ALU.mult),
                      [('ps', bkk), K_('decay')], [K_('attn')])
                if LIM <= 3 and h >= LIMH:
                    kb.barrier()
                    return
                bt1 = next_bank()
                kb.op('pe', lambda: nc.tensor.transpose(PS[bt1][:, 0:128], H['PT'][0], ident_f), [K_('PT0')], [('ps', bt1)])
                evac(H['P'][0], PS[bt1][:, 0:128], [('ps', bt1)], [K_('P0')])
                bt2 = next_bank()
                kb.op('pe', lambda: nc.tensor.transpose(PS[bt2][:, 0:128], H['attn'], ident_f), [K_('attn')], [('ps', bt2)])
                evac(H['attnT'], PS[bt2][:, 0:128], [('ps', bt2)], [K_('attnT')])
                if LIM <= 4 and h >= LIMH:
                    kb.barrier()
                    return
                bt3 = next_bank()
                kb.op('pe', lambda: nc.tensor.transpose(PS[bt3][:, 0:128], vh, ident_f), [('vt', b)], [('ps', bt3)])
                kb.op('pe', lambda: nc.tensor.transpose(PS[bt3][:, 128:256], kh_, ident_f), [('kt', b)], [('ps', bt3)])
                kb.op('dve', lambda: nc.vector.tensor_scalar(out=H['X'][:, 0:128], in0=PS[bt3][:, 0:128], scalar1=beta[:, h:h + 1], scalar2=None,
                                                             op0=ALU.mult), [('ps', bt3), 'beta'], [K_('X')])
                kb.op('dve', lambda: nc.vector.tensor_scalar(out=H['X'][:, 128:256], in0=PS[bt3][:, 128:256], scalar1=bg[:, h:h + 1], scalar2=None,
                                                             op0=ALU.mult), [('ps', bt3), 'bg'], [K_('X')])
                kb.op('dve', lambda: nc.vector.tensor_scalar(out=H['kend'], in0=PS[bt3][:, 128:256], scalar1=ekend[:, h:h + 1], scalar2=None,
                                                             op0=ALU.mult), [('ps', bt3), 'ekend'], [K_('kend')])
                if LIM <= 5 and h >= LIMH:
                    kb.barrier()
                    return
                for lv in range(6):
                    bs1 = next_bank()
                    kb.op('pe', lambda: nc.tensor.matmul(PS[bs1][:, 0:128], lhsT=H['PT'][lv], rhs=H['P'][lv], start=True, stop=True),
                          [K_('PT%d' % lv), K_('P%d' % lv)], [('ps', bs1)])
                    evac(H['P'][lv + 1], PS[bs1][:, 0:128], [('ps', bs1)], [K_('P%d' % (lv + 1))])
                    if lv < 5:
                        bs2 = next_bank()
                        kb.op('pe', lambda: nc.tensor.matmul(PS[bs2][:, 0:128], lhsT=H['P'][lv], rhs=H['PT'][lv], start=True, stop=True),
                              [K_('PT%d' % lv), K_('P%d' % lv)], [('ps', bs2)])
                        evac(H['PT'][lv + 1], PS[bs2][:, 0:128], [('ps', bs2)], [K_('PT%d' % (lv + 1))])
                for lv in range(7):
                    ba = next_bank()
                    kb.op('pe', lambda: nc.tensor.matmul(PS[ba][:, 0:256], lhsT=H['P'][lv], rhs=H['X'], start=True, stop=True),
                          [K_('P%d' % lv), K_('X')], [('ps', ba)])
                    kb.op('dve', lambda: nc.vector.tensor_tensor(out=H['X'], in0=H['X'], in1=PS[ba][:, 0:256], op=ALU.add),
                          [('ps', ba), K_('X')], [K_('X')])
                bt4 = next_bank()
                kb.op('pe', lambda: nc.tensor.transpose(PS[bt4][:, 0:128], H['X'][:, 128:256], ident_f), [K_('X')], [('ps', bt4)])
                evac(H['kcdT'], PS[bt4][:, 0:128], [('ps', bt4)], [K_('kcdT')])
                if LIM <= 7 and h >= LIMH:
                    kb.barrier()
                    return
                Sk = ('S', si, h)
                Sh = S[si][:, h, :]
                b1 = next_bank()
                kb.op('pe', lambda: nc.tensor.matmul(PS[b1][:, 0:128], lhsT=H['kcdT'], rhs=Sh, start=True, stop=True), [K_('kcdT'), Sk], [('ps', b1)])
                kb.op('dve', lambda: nc.vector.tensor_tensor(out=H['vn'], in0=H['X'][:, 0:128], in1=PS[b1][:, 0:128], op=ALU.subtract),
                      [('ps', b1), K_('X')], [K_('vn')])
                b2 = next_bank()
                kb.op('pe', lambda: nc.tensor.matmul(PS[b2][:, 0:128], lhsT=qh, rhs=Sh, start=True, stop=True), [('qt', b), Sk], [('ps', b2)])
                kb.op('pe', lambda: nc.tensor.matmul(PS[b2][:, 128:256], lhsT=H['attnT'], rhs=H['vn'], start=True, stop=True),
                      [K_('attnT'), K_('vn')], [('ps', b2)])
                kb.op('dve', lambda: nc.vector.tensor_scalar(out=H['tq'], in0=PS[b2][:, 0:128], scalar1=egc[:, h:h + 1], scalar2=None,
                                                             op0=ALU.mult), [('ps', b2), 'egc'], [K_('tq')])
                kb.op('dve', lambda: nc.vector.tensor_tensor(out=oall[:, h, :], in0=H['tq'], in1=PS[b2][:, 128:256], op=ALU.add),
                      [('ps', b2), K_('tq')], [('oall', h)])
                b3 = next_bank()
                kb.op('pe', lambda: nc.tensor.matmul(PS[b3][:, 0:128], lhsT=H['kend'], rhs=H['vn'], start=True, stop=True),
                      [K_('kend'), K_('vn')], [('ps', b3)])
                kb.op('dve', lambda: nc.vector.scalar_tensor_tensor(out=Sh, in0=Sh, scalar=gendb[:, h:h + 1], in1=PS[b3][:, 0:128],
                                                                    op0=ALU.mult, op1=ALU.add), [('ps', b3), Sk, 'gendb'], [Sk])
                if LIM <= 8 and h >= LIMH:
                    kb.barrier()
                    return
            if LIM <= 9:
                kb.barrier()
                return
            oall_k = [('oall', h) for h in range(BH)]
            kb.op('act', lambda: nc.scalar.activation(out=osq, in_=oall, func=AF.Square), oall_k, ['osq'])
            kb.op('dve', lambda: nc.vector.tensor_reduce(out=ss, in_=osq, axis=AX.X, op=ALU.add), ['osq'], ['ss'])
            kb.op('act', lambda: nc.scalar.activation(out=ss, in_=ss, func=AF.Sqrt, scale=1.0 / 128, bias=1e-6), ['ss'], ['ss'])
            kb.op('dve', lambda: nc.vector.reciprocal(out=ss, in_=ss), ['ss'], ['ss'])
            if LIM <= 10:
                kb.barrier()
                return
            kb.op('dve', lambda: nc.vector.tensor_tensor(out=osq, in0=oall, in1=ss.unsqueeze(2).to_broadcast([128, BH, 128]), op=ALU.mult),
                  oall_k + ['ss', 'osq'], ['osq'])
            kb.op('pool', lambda: nc.gpsimd.tensor_tensor(out=osq, in0=osq, in1=ogb.unsqueeze(1).to_broadcast([128, BH, 128]), op=ALU.mult),
                  ['osq', 'ogb'], ['osq'])
            kb.op('dve', lambda: nc.vector.tensor_tensor(out=ybf, in0=osq, in1=szt[b], op=ALU.mult), ['osq', ('szt', b)], ['ybf'])
            if LIM <= 11:
                kb.barrier()
                return
            for h0 in range(0, BH, 8):
                hn = min(8, BH - h0)
                bank = next_bank()
                pbf = PS[bank][:].bitcast(BF16)
                for j in range(hn):
                    kb.op('pe', lambda: nc.tensor.transpose(pbf[:, j * 128:(j + 1) * 128], ybf[:, h0 + j, :], ident_b), ['ybf'], [('ps', bank)])
                kb.op('act', lambda: nc.scalar.copy(out=ybT[:, h0:h0 + hn, :], in_=pbf[:, 0:hn * 128].rearrange("p (a b) -> p a b", a=hn)),
                      [('ps', bank)], ['ybT'])
            kb.dma('sp', [(ybv[:, :, cols], ybT)], ['ybT'], [ymixT.name])
            if LIM <= 12:
                kb.barrier()
                return
            if tt == NTP - 1:
                kb.dma('sp', [(odeltap[l].rearrange("h k v -> k h v"), S[0])], [('S', 0, h) for h in range(BH)], [odeltap.name], is_out=True)
            if tt == NTP:
                kb.dma('sp', [(odeltas[l].rearrange("h k v -> k h v"), S[1])], [('S', 1, h) for h in range(BH)], [odeltas.name], is_out=True)
        kb.barrier()

    candD = dscr("candD", [128, 256], F32)

    def mix_C_sample(l):
        ar.reset()
        P = c['PAGES']
        NCH = 32
        M4 = 128
        CW = 4 * P
        KS = c['KSELS']
        K1 = min(KS, CW)
        G = CH // CKV
        R64 = 4 * IH
        Wide = cmask2[0:R64, 0:256]
        SelM = cmask2[0:R64, 256:260]
        Ind = cmask2[0:4, 260:260 + M4]
        wscale = float(IH ** -0.5 * 128 ** -0.5)
        NPH = c['NPHYS']
        pidx = ar.alloc([P, 1], I32)
        kb.dma('sp', [(pidx, ptab.rearrange("o p -> p o"), dict(allow_slow_non_contiguous=True))], [], ['pidx'])
        if l > 0:
            kb.op('dve', lambda: nc.vector.tensor_scalar(out=pidx, in0=pidx, scalar1=float(l * NPH), scalar2=None, op0=ALU.add), ['pidx'], ['pidx'])
        iqs = ar.alloc([128, IH, 4], BF16)
        qs = ar.alloc([128, CH, 4], BF16)
        ikn = ar.alloc([128, 4], BF16)
        ktn = ar.alloc([128, CKV, 4], BF16)
        vn_ = ar.alloc([4, CKVW], BF16)
        wcol = ar.alloc([R64, 1], F32)
        kb.dma('sp', [(iqs, iqT.rearrange("(h d) t -> d h t", d=128)[:, :, T:T + 4])], [iqT.name], ['iqs'])
        kb.dma('sp', [(qs, qT.rearrange("(h d) t -> d h t", d=128)[:, :, T:T + 4])], [qT.name], ['qs'])
        kb.dma('sp', [(ikn, ikT[:, T:T + 4])], [ikT.name], ['ikn'])
        kb.dma('sp', [(ktn, kT.rearrange("(h d) t -> d h t", d=128)[:, :, T:T + 4])], [kT.name], ['ktn'])
        kb.dma('sp', [(vn_, vTM[T:T + 4, :])], [vTM.name], ['vn_'])
        kb.dma('sp', [(wcol[h * 4:(h + 1) * 4, :], smallTM[T:T + 4, 32 + h:33 + h], dict(allow_slow_non_contiguous=True)) for h in range(IH)], [smallTM.name], ['wcol'])
        kb.op('dve', lambda: nc.vector.tensor_scalar(out=wcol, in0=wcol, scalar1=wscale, scalar2=None, op0=ALU.mult), ['wcol'], ['wcol'])
        iqs2 = iqs.rearrange("p a b -> p (a b)")

        idxI = ar.alloc([P, 8], I32)
        idxK = ar.alloc([P, 16], I32)
        for ch in range(8):
            kb.op('dve', lambda: nc.vector.tensor_scalar(out=idxI[:, ch:ch + 1], in0=pidx, scalar1=8.0, scalar2=float(ch), op0=ALU.mult, op1=ALU.add),
                  ['pidx'], ['idxI'])
        for ch in range(16):
            kb.op('dve', lambda: nc.vector.tensor_scalar(out=idxK[:, ch:ch + 1], in0=pidx, scalar1=16.0, scalar2=float(ch), op0=ALU.mult, op1=ALU.add),
                  ['pidx'], ['idxK'])

        def gather(dst, srcap, nch, ch, idxt, ikey, key):
            def fn():
                flat = srcap.rearrange("l (n c s) w -> (l n c) (s w)", c=nch, s=128 // nch)
                return nc.gpsimd.indirect_dma_start(out=dst.rearrange("p a b -> p (a b)"), out_offset=None, in_=flat,
                                                    in_offset=bass.IndirectOffsetOnAxis(ap=idxt[:, ch:ch + 1], axis=0),
                                                    oob_is_err=False)
            kb.dma_custom('pool', fn, [ikey], [key])

        KI = ar.alloc([P, 128, 128], BF16)
        for s0 in range(0, 128, 16):
            gather(KI[:, s0:s0 + 16, :], cache_kidx, 8, s0 // 16, idxI, 'idxI', 'KI')
        ikc = [ar.alloc([128, CW], BF16) for _ in range(2)]
        rr = [ar.alloc([R64, CW], F32) for _ in range(2)]
        ACC = 7
        for cidx in range(NCH):
            cb_ = cidx % 2
            bank = next_bank(0, 2)
            pbf = PS[bank][:].bitcast(BF16)
            for jb in range(4):
                s = cidx * 4 + jb
                kb.op('pe', lambda: nc.tensor.transpose(pbf[:, jb * P:(jb + 1) * P], KI[:, s, :], ident_b[0:P, 0:P]), ['KI'], [('ps', bank)])
            kb.op('act', lambda: nc.scalar.copy(out=ikc[cb_], in_=pbf[:, 0:CW]), [('ps', bank)], [('ikc', cb_)])
            b2_ = next_bank(2, 4)
            kb.op('pe', lambda: nc.tensor.matmul(PS[b2_][0:R64, 0:CW], lhsT=iqs2, rhs=ikc[cb_], start=True, stop=True), ['iqs', ('ikc', cb_)], [('ps', b2_)])
            kb.op('dve', lambda: nc.vector.tensor_scalar(out=rr[cb_], in0=PS[b2_][0:R64, 0:CW], scalar1=0.0, scalar2=wcol[:, 0:1], op0=ALU.max, op1=ALU.mult),
                  [('ps', b2_), 'wcol'], [('rr', cb_)])
            kb.op('pe', lambda: nc.tensor.matmul(PS[ACC][0:M4, 0:CW], lhsT=Wide[:, NCH - 1 - cidx:NCH - 1 - cidx + M4], rhs=rr[cb_],
                                                 start=(cidx == 0), stop=(cidx == NCH - 1)), [('rr', cb_)], [('ps', ACC)])
        acc = ar.alloc([M4, CW], F32)
        work = ar.alloc([M4, CW], F32)
        kb.op('act', lambda: nc.scalar.copy(out=acc, in_=PS[ACC][0:M4, 0:CW]), [('ps', ACC)], ['acc'])
        kb.op('dve', lambda: nc.vector.tensor_copy(out=work, in_=acc), ['acc'], ['work'])
        rn = ar.alloc([R64, 4], F32)
        accn = ar.alloc([4, 4], F32)
        bn = next_bank(2, 4)
        kb.op('pe', lambda: nc.tensor.matmul(PS[bn][0:R64, 0:4], lhsT=iqs2, rhs=ikn, start=True, stop=True), ['iqs', 'ikn'], [('ps', bn)])
        kb.op('dve', lambda: nc.vector.tensor_scalar(out=rn, in0=PS[bn][0:R64, 0:4], scalar1=0.0, scalar2=wcol[:, 0:1], op0=ALU.max, op1=ALU.mult),
              [('ps', bn), 'wcol'], ['rn'])
        bn2 = next_bank(2, 4)
        kb.op('pe', lambda: nc.tensor.matmul(PS[bn2][0:4, 0:4], lhsT=SelM, rhs=rn, start=True, stop=True), ['rn'], [('ps', bn2)])
        kb.op('dve', lambda: nc.vector.tensor_tensor(out=accn, in0=PS[bn2][0:4, 0:4], in1=cmask[0:4, 4, 0:4], op=ALU.add), [('ps', bn2)], ['accn'])
        cand1 = ar.alloc([M4, K1], F32)
        m8 = ar.alloc([128, 8], F32)
        for r in range(K1 // 8):
            kb.op('dve', lambda: nc.vector.max(out=cand1[:, r * 8:(r + 1) * 8], in_=work), ['work'], ['cand1'])
            if r < K1 // 8 - 1:
                kb.op('dve', lambda: nc.vector.match_replace(out=work, in_to_replace=cand1[:, r * 8:(r + 1) * 8], in_values=work, imm_value=-3.0e38),
                      ['work', 'cand1'], ['work'])
        W2 = NCH * K1 + 8
        cand2 = ar.alloc([4, W2], F32)
        cdv = candD.rearrange("a b -> (a b)")[0:M4 * K1]
        kb.dma('sp', [(cdv.rearrange("(m k) -> m k", k=K1), cand1)], ['cand1'], [candD.name])
        kb.dma('sp', [(cand2[:, 0:NCH * K1], cdv.rearrange("(t x) -> t x", t=4))], [candD.name], ['cand2'])
        kb.op('dve', lambda: nc.vector.tensor_copy(out=cand2[:, NCH * K1:NCH * K1 + 4], in_=accn), ['accn', 'cand2'], ['cand2'])
        kb.op('dve', lambda: nc.vector.memset(cand2[:, NCH * K1 + 4:W2], -3.0e38), ['cand2'], ['cand2'])
        for r in range(KS // 8):
            kb.op('dve', lambda: nc.vector.max(out=m8[0:4, :], in_=cand2), ['cand2'], ['m8'])
            if r < KS // 8 - 1:
                kb.op('dve', lambda: nc.vector.match_replace(out=cand2, in_to_replace=m8[0:4, :], in_values=cand2, imm_value=-3.0e38),
                      ['cand2', 'm8'], ['cand2'])
        thr4 = ar.alloc([4, 2], F32)
        kb.op('dve', lambda: nc.vector.tensor_copy(out=thr4[:, 0:1], in_=m8[0:4, 7:8]), ['m8'], ['thr4'])
        kb.op('dve', lambda: nc.vector.tensor_copy(out=thr4[:, 1:2], in_=m8[0:4, 7:8]), ['m8', 'thr4'], ['thr4'])
        bt_ = next_bank(2, 4)
        kb.op('pe', lambda: nc.tensor.matmul(PS[bt_][0:M4, 0:2], lhsT=Ind, rhs=thr4, start=True, stop=True), ['thr4'], [('ps', bt_)])
        thrtc = ar.alloc([M4, 2], F32)
        kb.op('act', lambda: nc.scalar.copy(out=thrtc, in_=PS[bt_][0:M4, 0:2]), [('ps', bt_)], ['thrtc'])
        seltc = ar.alloc([M4, CW], BF16)
        kb.op('dve', lambda: nc.vector.tensor_scalar(out=seltc, in0=acc, scalar1=thrtc[:, 0:1], scalar2=None, op0=ALU.is_ge), ['acc', 'thrtc'], ['seltc'])
        seln = ar.alloc([4, 4], F32)
        kb.op('dve', lambda: nc.vector.tensor_scalar(out=seln, in0=accn, scalar1=thr4[:, 0:1], scalar2=None, op0=ALU.is_ge), ['accn', 'thr4'], ['seln'])
        selT = ar.alloc([P, 4, M4], BF16)
        bs_ = next_bank(2, 4)
        pbs = PS[bs_][:].bitcast(BF16)
        for jb in range(4):
            kb.op('pe', lambda: nc.tensor.transpose(pbs[0:P, jb * M4:(jb + 1) * M4], seltc[:, jb * P:(jb + 1) * P], ident_b),
                  ['seltc'], [('ps', bs_)])
        kb.op('act', lambda: nc.scalar.copy(out=selT.rearrange("p a b -> p (a b)"), in_=pbs[0:P, 0:4 * M4]), [('ps', bs_)], ['selT'])
        selnT = ar.alloc([4, 4], BF16)
        bs2 = next_bank(2, 4)
        kb.op('pe', lambda: nc.tensor.transpose(PS[bs2][0:4, 0:4], seln, ident_f[0:4, 0:4]), ['seln'], [('ps', bs2)])
        kb.op('act', lambda: nc.scalar.copy(out=selnT, in_=PS[bs2][0:4, 0:4]), [('ps', bs2)], ['selnT'])
        SC = 8
        kpg = [ar.alloc([P, SC, CKVW], BF16) for _ in range(2)]
        vpg = [ar.alloc([P, SC, CKVW], BF16) for _ in range(2)]
        kpf = [ar.alloc([P, SC, CKVW], F32)] * 2
        vpf = [ar.alloc([P, SC, CKVW], F32)] * 2
        ktj = [ar.alloc([128, CKV, P], BF16) for _ in range(2)]
        pT = [ar.alloc([P, CH, 4], BF16) for _ in range(2)]
        pTm = [ar.alloc([P, CH, 4], BF16) for _ in range(2)]
        OT, DEN = 5, 6
        NQ = CH * 4
        qs2 = qs.rearrange("p a b -> p (a b)")
        first = [True]
        for s in range(128):
            cidx, jb = s // 4, s % 4
            gb = (s // SC) % 2
            sl = s % SC
            if sl == 0:
                gather(kpf[gb], cache_k, 16, s // SC, idxK, 'idxK', ('kpf', 0))
                gather(vpf[gb], cache_v, 16, s // SC, idxK, 'idxK', ('vpf', 0))
                kb.op('act', lambda: nc.scalar.copy(out=kpg[gb], in_=kpf[gb]), [('kpf', 0)], [('kpg', gb)])
                kb.op('dve', lambda: nc.vector.tensor_copy(out=vpg[gb], in_=vpf[gb]), [('vpf', 0)], [('vpg', gb)])
            s2 = s % 2
            bank = next_bank(0, 2)
            pbf = PS[bank][:].bitcast(BF16)
            for kh in range(CKV):
                kb.op('pe', lambda: nc.tensor.transpose(pbf[:, kh * P:(kh + 1) * P], kpg[gb][:, sl, kh * 128:(kh + 1) * 128], ident_b[0:P, 0:P]),
                      [('kpg', gb)], [('ps', bank)])
            kb.op('act', lambda: nc.scalar.copy(out=ktj[s2].rearrange("p a b -> p (a b)"), in_=pbf[:, 0:CKV * P]), [('ps', bank)], [('ktj', s2)])
            sb_ = next_bank(2, 4)
            for kh in range(CKV):
                kb.op('pe', lambda: nc.tensor.matmul(PS[sb_][0:P, kh * G * 4:(kh + 1) * G * 4], lhsT=ktj[s2][:, kh, :], rhs=qs2[:, kh * G * 4:(kh + 1) * G * 4],
                                                     start=True, stop=True), [('ktj', s2), 'qs'], [('ps', sb_)])
            kb.op('act', lambda: nc.scalar.activation(out=pT[s2].rearrange("p a b -> p (a b)"), in_=PS[sb_][0:P, 0:NQ], func=AF.Exp), [('ps', sb_)], [('pT', s2)])
            kb.op('dve', lambda: nc.vector.tensor_tensor(out=pTm[s2], in0=pT[s2],
                                                         in1=selT[:, jb, cidx:M4:NCH].unsqueeze(1).to_broadcast([P, CH, 4]), op=ALU.mult),
                  [('pT', s2), 'selT'], [('pTm', s2)])
            pm2 = pTm[s2].rearrange("p a b -> p (a b)")
            for kh in range(CKV):
                kb.op('pe', lambda: nc.tensor.matmul(PS[OT][:, kh * G * 4:(kh + 1) * G * 4], lhsT=vpg[gb][:, sl, kh * 128:(kh + 1) * 128],
                                                     rhs=pm2[:, kh * G * 4:(kh + 1) * G * 4], start=first[0], stop=False, skip_group_check=True),
                      [('vpg', gb), ('pTm', s2)], [('ps', OT)])
                first[0] = False
            kb.op('pe', lambda: nc.tensor.matmul(PS[DEN][:, 0:NQ], lhsT=ones_b[0:P, :], rhs=pm2, start=(s == 0), stop=False, skip_group_check=True),
                  [('pTm', s2)], [('ps', DEN)])
        sbn = next_bank(2, 4)
        for kh in range(CKV):
            kb.op('pe', lambda: nc.tensor.matmul(PS[sbn][0:4, kh * G * 4:(kh + 1) * G * 4], lhsT=ktn[:, kh, :], rhs=qs2[:, kh * G * 4:(kh + 1) * G * 4],
                                                 start=True, stop=True), ['ktn', 'qs'], [('ps', sbn)])
        pTn = ar.alloc([4, CH, 4], BF16)
        pTnm = ar.alloc([4, CH, 4], BF16)
        kb.op('act', lambda: nc.scalar.activation(out=pTn.rearrange("p a b -> p (a b)"), in_=PS[sbn][0:4, 0:NQ], func=AF.Exp), [('ps', sbn)], ['pTn'])
        kb.op('dve', lambda: nc.vector.tensor_tensor(out=pTnm, in0=pTn, in1=selnT.unsqueeze(1).to_broadcast([4, CH, 4]), op=ALU.mult),
              ['pTn', 'selnT'], ['pTnm'])
        pn2 = pTnm.rearrange("p a b -> p (a b)")
        for kh in range(CKV):
            kb.op('pe', lambda: nc.tensor.matmul(PS[OT][:, kh * G * 4:(kh + 1) * G * 4], lhsT=vn_[0:4, kh * 128:(kh + 1) * 128],
                                                 rhs=pn2[:, kh * G * 4:(kh + 1) * G * 4], start=False, stop=(kh == CKV - 1), skip_group_check=True),
                  ['vn_', 'pTnm'], [('ps', OT)])
        kb.op('pe', lambda: nc.tensor.matmul(PS[DEN][:, 0:NQ], lhsT=ones_b[0:4, :], rhs=pn2, start=False, stop=True, skip_group_check=True),
              ['pTnm'], [('ps', DEN)])
        rden = ar.alloc([128, NQ], F32)
        ycs = ar.alloc([128, CH, 128], BF16)
        kb.op('dve', lambda: nc.vector.reciprocal(out=rden, in_=PS[DEN][:, 0:NQ]), [('ps', DEN)], ['rden'])
        kb.op('pool', lambda: nc.gpsimd.memset(ycs, 0.0), [], ['ycs'])
        kb.op('dve', lambda: nc.vector.tensor_tensor(out=ycs[:, :, 0:4], in0=PS[OT][:, 0:NQ].rearrange("p (a b) -> p a b", b=4),
                                                     in1=rden.rearrange("p (a b) -> p a b", b=4), op=ALU.mult), [('ps', OT), 'rden', 'ycs'], ['ycs'])
        ycv = ymixT[AW + BW:MIX, :].rearrange("(h d) t -> d h t", d=128)
        kb.dma('sp', [(ycv[:, :, T:T + 128], ycs)], ['ycs'], [ymixT.name])
        kb.barrier()

    def phase_mixers(l):
        mix_A(l)
        mix_C_prompt(l)
        import os
        if os.environ.get('MIXCS', '1') == '1':
            mix_C_sample(l)
        if os.environ.get('MIXB', 'all') in ('conv', 'all'):
            mix_B_conv(l)
        if os.environ.get('MIXB', 'all') == 'all':
            mix_B_scan(l)


    phase_loadx()
    done = False
    for l in range(DEPTH):
        ar.reset()
        ACTB = alloc_actb()
        mark = ar.off
        phase_norm(xresA, ln1[l:l + 1, :], ACTB)
        kb.barrier()
        ar.off = mark
        phase_win(l, ACTB)
        if stop_after == 'win':
            done = True
            break
        phase_mixers(l)
        if stop_after == 'mix':
            done = True
            break
        phase_wbr(l)
        phase_wo(l, xresA, xresB)
        ar.reset()
        ACTB = alloc_actb()
        mark = ar.off
        phase_norm(xresB, ln2[l:l + 1, :], ACTB)
        kb.barrier()
        ar.off = mark
        phase_w13(l, ACTB)
        phase_w2(l, xresB, xresA)
        if stop_after == 'l0':
            done = True
            break
    if not done:
        phase_final(xresA, xresB)

    scr = dict(xresA=xresA, gauT=gauT, qkvT=qkvT, qT=qT, kT=kT, iqT=iqT, ikT=ikT, gT=gT, avg=avg, szTM=szTM,
               smallTM=smallTM, vTM=vTM, ymixT=ymixT, mT=mT, hT=hT, xresB=xresB)
    if dbg:
        ar.reset()
        for name, ap in dbg.items():
            src = scr[name]
            rows = src.shape[0]
            for r0 in range(0, rows, 128):
                rr = min(128, rows - r0)
                if src.dtype == BF16:
                    tb = ar.alloc([128, src.shape[1]], BF16)
                    tf = ar.alloc([128, src.shape[1]], F32)
                    kb.dma('sp', [(tb[0:rr], src[r0:r0 + rr, :])], [src.name], ['dbg_tb'])
                    kb.op('dve', lambda: nc.vector.tensor_copy(out=tf[0:rr], in_=tb[0:rr]), ['dbg_tb'], ['dbg_tf'])
                    kb.dma('sp', [(ap[r0:r0 + rr, :], tf[0:rr])], ['dbg_tf'], [ap.name])
                    kb.barrier()
                    ar.reset()
                else:
                    kb.dma('sp', [(ap[r0:r0 + rr, :], src[r0:r0 + rr, :])], [src.name], [ap.name])
    kb.finish()
    return nc, c


def make_consts():
    k = np.zeros((8, 128, 128), np.float32)
    i = np.arange(128)
    k[0] = np.eye(128, dtype=np.float32)
    k[1] = (i[:, None] <= i[None, :]).astype(np.float32)
    k[2] = np.where(i[:, None] >= i[None, :], 0.0, NEG)
    k[3] = (i[:, None] > i[None, :]).astype(np.float32)
    k[4] = np.where(i[None, :] <= i[:, None], 0.0, -1e30)
    k[5] = (i[:, None] <= i[None, :]).astype(np.float32)
    k[6] = (i[:, None] < 4).astype(np.float32) * np.ones((1, 128), np.float32)
    k[7] = i[:, None].astype(np.float32) * np.ones((1, 128), np.float32)
    return k


def make_consts2(c):
    NCH = 32
    IH = c['IH']
    k = np.zeros((128, 512), np.float32)
    for r in range(4 * IH):
        t = r % 4
        k[r, t * NCH + NCH - 1] = 1.0
        k[r, 256 + t] = 1.0
    for t in range(4):
        k[t, 260 + t * NCH:260 + (t + 1) * NCH] = 1.0
    return k


def shard_inputs(cfg, inputs, n_cores=8):
    c = derive(cfg)
    NB, NSB = c['NB'], c['NSB']
    f = lambda a: np.ascontiguousarray(np.asarray(a))
    rep = {}
    DEPTH = c['DEPTH']
    rep['cache_k'] = f(inputs['cache_k']).reshape(DEPTH, -1, c['CKVW'])
    rep['cache_v'] = f(inputs['cache_v']).reshape(DEPTH, -1, c['CKVW'])
    rep['cache_kidx'] = f(inputs['cache_kidx']).reshape(DEPTH, -1, 128)
    for k in ('ln1', 'w_in', 'a_ln_g', 'a_ln_b', 'a_ws', 'a_bs', 'b_conv_w', 'b_a_log', 'b_dt_bias', 'b_out_g',
              'w_br', 'w_o', 'ln2', 'ffn_w1', 'ffn_w3', 'ffn_w2'):
        rep[k] = f(inputs[k])
    rep['ln_f'] = f(inputs['ln_f']).reshape(1, -1)
    rep['consts'] = make_consts()
    rep['consts2'] = make_consts2(c)
    maps = []
    for core in range(n_cores):
        m = dict(rep)
        m['xp'] = f(inputs['x_prompt'][core % NB])
        sb = core % NSB
        m['xs'] = f(inputs['x_sample'][sb])
        m['sconv'] = f(inputs['state_conv'][:, sb])
        m['sdelta'] = f(inputs['state_delta'][:, sb])
        m['ptab'] = f(inputs['page_table'][sb:sb + 1]).astype(np.int32)
        maps.append(m)
    return maps


_CACHE = {}


def kernel(**inputs):
    cfg = REAL_CFG
    if 'nc' not in _CACHE:
        _CACHE['nc'] = build(cfg)
    nc, c = _CACHE['nc']
    maps = shard_inputs(cfg, inputs)
    res = run_bass_kernel_spmd(nc, maps, core_ids=list(range(8)))
    return assemble(c, res.results)


def assemble(c, R):
    NB, NSB, DEPTH = c['NB'], c['NSB'], c['DEPTH']
    T, D = c['T'], c['D']
    stackp = lambda name: np.stack([R[b][name] for b in range(NB)], axis=1)
    stacks = lambda name: np.stack([R[b][name] for b in range(NSB)], axis=1)
    y_prompt = np.stack([R[b]['yp'] for b in range(NB)], 0)
    y_sample = np.stack([R[b]['ys'] for b in range(NSB)], 0)
    return (y_prompt, y_sample,
            stackp('okp').reshape(DEPTH, NB, T, c['CKV'], 128), stackp('ovp').reshape(DEPTH, NB, T, c['CKV'], 128),
            stackp('oikp'), stackp('oconvp'), stackp('odeltap'),
            stacks('oks').reshape(DEPTH, NSB, 4, c['CKV'], 128), stacks('ovs').reshape(DEPTH, NSB, 4, c['CKV'], 128),
            stacks('oiks'), stacks('oconvs'), stacks('odeltas'), stacks('ochunk'))
```

```python
import numpy as np
from contextlib import ExitStack
import concourse.bass as bass
import concourse.mybir as mybir
from concourse.bass_utils import run_bass_kernel_spmd

F32 = mybir.dt.float32
BF16 = mybir.dt.bfloat16
I32 = mybir.dt.int32
AF = mybir.ActivationFunctionType
ALU = mybir.AluOpType
AX = mybir.AxisListType

REAL_CFG = dict(D=4096, T=2048, AG=8, BH=12, CH=12, CKV=4, IH=16, DFF=11008,
                PAGES=128, NPHYS=1280, DEPTH=2, TOPK=256, NB=4, NSB=8)
NEG = -30000.0


def derive(cfg):
    c = dict(cfg)
    c['AW'] = c['AG'] * 128
    c['BW'] = c['BH'] * 128
    c['CQ'] = c['CH'] * 128
    c['CKVW'] = c['CKV'] * 128
    c['IQW'] = c['IH'] * 128
    c['MIX'] = c['AW'] + c['BW'] + c['CQ']
    c['KC'] = c['D'] // 128
    c['NTP'] = c['T'] // 128
    c['NTT'] = c['NTP'] + 1
    c['NT'] = c['T'] + 128
    o = {}
    off = 0
    for name, w in (('au', c['AW']), ('av', c['AW']), ('qkv', 3 * c['BW']), ('z', c['BW']),
                    ('a', c['BH']), ('b', c['BH']), ('cq', c['CQ']), ('ck', c['CKVW']),
                    ('cv', c['CKVW']), ('iq', c['IQW']), ('iw', c['IH']), ('ik', 128),
                    ('gate', 3 * c['D'])):
        o[name] = off
        off += w
    c['off'] = o
    c['INW'] = off
    groups = []
    t = 0
    while t < c['T']:
        g = min(512, c['T'] - t)
        groups.append((t, g))
        t += g
    groups.append((c['T'], 128))
    c['groups'] = groups
    c['KSEL'] = min(c['TOPK'], c['T'] // 4)
    c['PAST'] = c['PAGES'] * 128
    c['KSELS'] = min(c['TOPK'], (c['PAST'] + 4) // 4)
    return c


class Arena:
    def __init__(self, t, nbytes):
        self.t = t
        self.n = nbytes
        self.off = 0
        self.persist = 0

    def alloc(self, shape, dt):
        esz = 2 if dt == BF16 else 4
        n = int(np.prod(shape[1:])) * esz
        n_al = (n + 63) // 64 * 64
        off = self.off
        self.off += n_al
        assert self.off <= self.n, ("arena overflow", self.off, self.n)
        ap = self.t[:, off // 2:(off + n) // 2]
        if dt != BF16:
            ap = ap.bitcast(dt)
        if len(shape) == 3:
            ap = ap.rearrange("p (a b) -> p a b", a=shape[1])
        elif len(shape) == 4:
            ap = ap.rearrange("p (a b c) -> p a b c", a=shape[1], b=shape[2])
        if shape[0] != 128:
            ap = ap[0:shape[0]]
        return ap

    def mark_persist(self):
        self.persist = self.off

    def reset(self):
        self.off = self.persist


class KB:
    def __init__(self, nc):
        self.nc = nc
        self.es = ExitStack()
        self.E = dict(pe=nc.tensor, act=nc.scalar, dve=nc.vector, pool=nc.gpsimd, sp=nc.sync)
        self.esem = {}
        self.ecnt = {}
        for e in ('pe', 'act', 'dve', 'pool'):
            self.esem[e] = self.es.enter_context(nc.semaphore('sem_' + e))
            self.ecnt[e] = 0
        self.known = {e: {} for e in self.E}
        self.res = {}
        self.dsem = {}
        self.nsem = 0
        self.out_toks = []
        self.free_sems = []

    def new_sem(self):
        self.nsem += 1
        return self.es.enter_context(self.nc.semaphore('dsem%d' % self.nsem))

    def _deps(self, reads, writes, eng=None):
        deps = []
        for r in reads:
            st = self.res.get(r)
            if st and st['w']:
                deps.append(st['w'])
            if st and isinstance(r, tuple) and r[0] == 'ps':
                deps.extend(t for t in st['r'].values() if t[2] != eng)
        for w in writes:
            st = self.res.get(w)
            if st:
                if st['w']:
                    deps.append(st['w'])
                deps.extend(st['r'].values())
        return deps

    def _wait(self, e, deps, skip_sem=None):
        for (sem, val, src) in deps:
            if e == 'pe' and src == 'pe':
                continue
            if skip_sem is not None and sem is skip_sem:
                continue
            k = id(sem)
            if self.known[e].get(k, 0) >= val:
                continue
            self.E[e].wait_ge(sem, val)
            self.known[e][k] = val

    def _record(self, tok, reads, writes):
        for r in reads:
            st = self.res.setdefault(r, dict(w=None, r={}))
            st['r'][id(tok[0])] = tok
        for w in writes:
            st = self.res.setdefault(w, dict(w=None, r={}))
            st['w'] = tok
            st['r'] = {}

    def op(self, e, fn, reads=(), writes=()):
        self._wait(e, self._deps(reads, writes, e))
        ins = fn()
        self.ecnt[e] += 1
        ins.then_inc(self.esem[e], 1)
        tok = (self.esem[e], self.ecnt[e], e)
        self._record(tok, reads, writes)
        return tok

    def dma(self, q, pairs, reads, writes, is_out=False):
        key = writes[0]
        if key not in self.dsem:
            self.dsem[key] = self.free_sems.pop() if self.free_sems else [self.new_sem(), 0]
        ds = self.dsem[key]
        self._wait(q, self._deps(reads, writes), skip_sem=ds[0])
        for p in pairs:
            kw = p[2] if len(p) > 2 else {}
            self.E[q].dma_start(out=p[0], in_=p[1], **kw).then_inc(ds[0], 16)
            ds[1] += 16
        tok = (ds[0], ds[1], 'dma')
        for r in reads:
            st = self.res.setdefault(r, dict(w=None, r={}))
            st['r'][id(tok[0])] = tok
        for w in writes:
            st = self.res.setdefault(w, dict(w=None, r={}))
            st['w'] = tok
            st['r'] = {}
        if is_out:
            self.out_toks.append(key)
        return tok

    def dma_custom(self, q, fn, reads, writes):
        key = writes[0]
        if key not in self.dsem:
            self.dsem[key] = self.free_sems.pop() if self.free_sems else [self.new_sem(), 0]
        ds = self.dsem[key]
        self._wait(q, self._deps(reads, writes), skip_sem=ds[0])
        fn().then_inc(ds[0], 16)
        ds[1] += 16
        tok = (ds[0], ds[1], 'dma')
        for r in reads:
            st = self.res.setdefault(r, dict(w=None, r={}))
            st['r'][id(tok[0])] = tok
        for w in writes:
            st = self.res.setdefault(w, dict(w=None, r={}))
            st['w'] = tok
            st['r'] = {}
        return tok

    def barrier(self):
        toks = [(self.esem[e], self.ecnt[e], e) for e in ('pe', 'act', 'dve', 'pool') if self.ecnt[e] > 0]
        toks += [(ds[0], ds[1], 'dma') for ds in self.dsem.values() if ds[1] > 0]
        for e in self.E:
            for (sem, val, src) in toks:
                if src == e and e == 'pe':
                    continue
                k = id(sem)
                if self.known[e].get(k, 0) >= val:
                    continue
                self.E[e].wait_ge(sem, val)
                self.known[e][k] = val
        self.res = {}
        self.free_sems.extend(self.dsem.values())
        self.dsem = {}

    def finish(self):
        self.barrier()


def build(cfg, stop_after=None, debug_outs=()):
    c = derive(cfg)
    D, T, NT, KC = c['D'], c['T'], c['NT'], c['KC']
    AW, BW, CQ, CKVW, IQW, MIX, DFF = c['AW'], c['BW'], c['CQ'], c['CKVW'], c['IQW'], c['MIX'], c['DFF']
    AG, BH, CH, CKV, IH = c['AG'], c['BH'], c['CH'], c['CKV'], c['IH']
    NTP, NTT, DEPTH = c['NTP'], c['NTT'], c['DEPTH']
    INW = c['INW']
    off = c['off']
    groups = c['groups']
    KF = DFF // 128
    nc = bass.Bass("TRN2", target_bir_lowering=False, dynamic_dma_scratch_size=32768)
    kb = KB(nc)

    def din(name, shape, dt=F32):
        return nc.dram_tensor(name, list(shape), dt, kind="ExternalInput").ap()

    def dout(name, shape, dt=F32):
        return nc.dram_tensor(name, list(shape), dt, kind="ExternalOutput").ap()

    def dscr(name, shape, dt):
        return nc.dram_tensor(name, list(shape), dt).ap()

    xp = din("xp", [T, D])
    xs = din("xs", [4, D])
    cache_k = din("cache_k", [DEPTH, c['NPHYS'] * 128, CKVW])
    cache_v = din("cache_v", [DEPTH, c['NPHYS'] * 128, CKVW])
    cache_kidx = din("cache_kidx", [DEPTH, c['NPHYS'] * 128, 128])
    sconv = din("sconv", [DEPTH, 3, 3 * BW])
    sdelta = din("sdelta", [DEPTH, BH, 128, 128])
    ptab = din("ptab", [1, c['PAGES']], I32)
    ln1 = din("ln1", [DEPTH, D])
    w_in = din("w_in", [DEPTH, D, INW])
    a_ln_g = din("a_ln_g", [DEPTH, AW])
    a_ln_b = din("a_ln_b", [DEPTH, AW])
    a_ws = din("a_ws", [DEPTH, AG, 128, 128])
    a_bs = din("a_bs", [DEPTH, AG, 128])
    b_conv_w = din("b_conv_w", [DEPTH, 4, 3 * BW])
    b_a_log = din("b_a_log", [DEPTH, BH])
    b_dt_bias = din("b_dt_bias", [DEPTH, BH])
    b_out_g = din("b_out_g", [DEPTH, 128])
    w_br = din("w_br", [DEPTH, MIX, D])
    w_o = din("w_o", [DEPTH, D, D])
    ln2 = din("ln2", [DEPTH, D])
    ffn_w1 = din("ffn_w1", [DEPTH, D, DFF])
    ffn_w3 = din("ffn_w3", [DEPTH, D, DFF])
    ffn_w2 = din("ffn_w2", [DEPTH, DFF, D])
    ln_f = din("ln_f", [1, D])
    consts = din("consts", [8, 128, 128])
    consts2 = din("consts2", [128, 512])

    yp = dout("yp", [T, D])
    ys = dout("ys", [4, D])
    okp = dout("okp", [DEPTH, T, CKVW])
    ovp = dout("ovp", [DEPTH, T, CKVW])
    oikp = dout("oikp", [DEPTH, T, 128])
    oconvp = dout("oconvp", [DEPTH, 3, 3 * BW])
    odeltap = dout("odeltap", [DEPTH, BH, 128, 128])
    oks = dout("oks", [DEPTH, 4, CKVW])
    ovs = dout("ovs", [DEPTH, 4, CKVW])
    oiks = dout("oiks", [DEPTH, 4, 128])
    oconvs = dout("oconvs", [DEPTH, 3, 3 * BW])
    odeltas = dout("odeltas", [DEPTH, BH, 128, 128])
    ochunk = dout("ochunk", [DEPTH, 4, AW])

    xresA = dscr("xresA", [D, NT], F32)
    xresB = dscr("xresB", [D, NT], F32)
    gauT = dscr("gauT", [AW, NT], BF16)
    qkvT = dscr("qkvT", [3 * BW, NT], F32)
    qT = dscr("qT", [CQ, NT], BF16)
    kT = dscr("kT", [CKVW, NT], BF16)
    iqT = dscr("iqT", [IQW, NT], BF16)
    ikT = dscr("ikT", [128, NT], BF16)
    gT = dscr("gT", [3 * D, NT], BF16)
    avg = dscr("avg", [NT, AW], F32)
    szTM = dscr("szTM", [NT, BW], BF16)
    smallTM = dscr("smallTM", [NT, 64], F32)
    vTM = dscr("vTM", [NT, CKVW], BF16)
    ymixT = dscr("ymixT", [MIX, NT], BF16)
    mT = dscr("mT", [D, NT], BF16)
    hT = dscr("hT", [DFF, NT], BF16)
    dbg = {}
    for name, shape in debug_outs:
        dbg[name] = dout("dbg_" + name, shape)

    ARENA_BYTES = 188 * 1024
    arena_t = kb.es.enter_context(nc.sbuf_tensor("arena", [128, ARENA_BYTES // 2], BF16))
    ar = Arena(arena_t, ARENA_BYTES)
    PS = [kb.es.enter_context(nc.psum_tensor("ps%d" % i, [128, 512], F32)) for i in range(8)]
    bank_ctr = [0]
    SL = []

    def next_bank(lo=0, hi=8):
        b = lo + bank_ctr[0] % (hi - lo)
        bank_ctr[0] += 1
        return b

    ident_f = ar.alloc([128, 128], F32)
    ident_b = ar.alloc([128, 128], BF16)
    ones_f = ar.alloc([128, 128], F32)
    ones_b = ar.alloc([128, 128], BF16)
    cmask = ar.alloc([128, 8, 128], F32)
    kb.dma('sp', [(cmask, consts.rearrange("k p n -> p k n"))], [], ['cmask'])
    cmask2 = ar.alloc([128, 512], F32)
    kb.dma('sp', [(cmask2, consts2)], [], ['cmask2'])
    kb.op('dve', lambda: nc.vector.tensor_copy(out=ident_f, in_=cmask[:, 0, :]), ['cmask'], ['ident_f'])
    kb.op('dve', lambda: nc.vector.tensor_copy(out=ident_b, in_=cmask[:, 0, :]), ['cmask'], ['ident_b'])
    kb.op('dve', lambda: nc.vector.memset(ones_f, 1.0), [], ['ones_f'])
    kb.op('dve', lambda: nc.vector.memset(ones_b, 1.0), [], ['ones_b'])
    ar.mark_persist()
    kb.barrier()

    xres = [xresA, xresB]

    def phase_loadx():
        ar.reset()
        xt = [ar.alloc([128, D], F32) for _ in range(2)]
        st = [ar.alloc([128, 4, 128], F32) for _ in range(2)]
        si = 0
        for tt in range(NTT):
            b = tt % 2
            if tt < NTP:
                kb.dma('sp', [(xt[b], xp[tt * 128:(tt + 1) * 128, :])], [], [('xt', b)])
            else:
                kb.op('dve', lambda: nc.vector.memset(xt[b], 0.0), [], [('xt', b)])
                kb.dma('sp', [(xt[b][0:4, :], xs)], [], [('xt', b)])
            for k4 in range(KC // 4):
                bank = next_bank()
                for j in range(4):
                    kc = k4 * 4 + j
                    kb.op('pe', lambda: nc.tensor.transpose(PS[bank][:, j * 128:(j + 1) * 128],
                                                            xt[b][:, kc * 128:(kc + 1) * 128], ident_f),
                          [('xt', b), 'ident_f'], [('ps', bank)])
                s = si % 2
                si += 1
                kb.op('act', lambda: nc.scalar.copy(out=st[s], in_=PS[bank][:].rearrange("p (a b) -> p a b", a=4)),
                      [('ps', bank)], [('st', s)])
                dst = xresA[k4 * 512:(k4 + 1) * 512, tt * 128:(tt + 1) * 128].rearrange("(a p) t -> p a t", p=128)
                kb.dma('sp', [(dst, st[s])], [('st', s)], ['xresA'])
        kb.barrier()

    def alloc_actb():
        return ar.alloc([128, KC, NT], BF16)

    def phase_norm(src, lnw_row, ACTB):
        xc = [ar.alloc([128, NT], F32) for _ in range(2)]
        sq = [ar.alloc([128, NT], F32) for _ in range(2)]
        rstd = ar.alloc([128, NT], F32)
        lncol = ar.alloc([128, KC], F32)
        kb.dma('sp', [(lncol, lnw_row.rearrange("o (k p) -> p (o k)", p=128), dict(allow_slow_non_contiguous=True))],
               [], ['lncol'])
        ng = len(groups)
        for kc in range(KC):
            b = kc % 2
            kb.dma('sp', [(xc[b], src[kc * 128:(kc + 1) * 128, :])], [src.name], [('xc', b)])
            kb.op('act', lambda: nc.scalar.activation(out=sq[b], in_=xc[b], func=AF.Square),
                  [('xc', b)], [('sq', b)])
            for gi, (g0, gs) in enumerate(groups):
                kb.op('pe', lambda: nc.tensor.matmul(PS[gi][:, :gs], lhsT=ones_f, rhs=sq[b][:, g0:g0 + gs],
                                                     start=(kc == 0), stop=(kc == KC - 1)),
                      [('sq', b), 'ones_f'], [('ps', gi)])
        for gi, (g0, gs) in enumerate(groups):
            kb.op('act', lambda: nc.scalar.activation(out=rstd[:, g0:g0 + gs], in_=PS[gi][:, :gs], func=AF.Sqrt,
                                                      scale=1.0 / D, bias=1e-6),
                  [('ps', gi)], ['rstd'])
        kb.op('dve', lambda: nc.vector.reciprocal(out=rstd, in_=rstd), ['rstd'], ['rstd'])
        for kc in range(KC):
            b = kc % 2
            kb.dma('sp', [(xc[b], src[kc * 128:(kc + 1) * 128, :])], [src.name], [('xc', b)])
            kb.op('dve', lambda: nc.vector.scalar_tensor_tensor(out=ACTB[:, kc, :], in0=xc[b], scalar=lncol[:, kc:kc + 1],
                                                                in1=rstd, op0=ALU.mult, op1=ALU.mult),
                  [('xc', b), 'rstd', 'lncol'], [('actb', kc)])

    slab_state = dict(i=0)

    def alloc_slabs(nbuf, kcn, cw):
        return [ar.alloc([128, kcn, cw], BF16) for _ in range(nbuf)]

    def load_slab(slabs, wap, r0, kcn, c0, cw):
        b = slab_state['i'] % len(slabs)
        slab_state['i'] += 1
        src = wap[r0 * 128:(r0 + kcn) * 128, c0:c0 + cw].rearrange("(k p) n -> p k n", p=128)
        kb.dma('pool', [(slabs[b][:, 0:kcn, 0:cw], src)], [], [('slab', b)])
        return b

    def run_items(slabs, items):
        def do_load(it):
            return [load_slab(slabs, *la) for la in it[0]]
        if not items:
            return
        nxt = do_load(items[0])
        for i, it in enumerate(items):
            cur_b = nxt
            if i + 1 < len(items):
                nxt = do_load(items[i + 1])
            it[1](cur_b)

    def dense_fm_items(ACTB, wap, c0, ncols, kcn, epi, grp=None, cw=256):
        grp = grp or groups
        items = []
        for s0 in range(0, ncols, cw):
            w = min(cw, ncols - s0)

            def compute(bufs, s0=s0, w=w):
                b = bufs[0]
                for mc in range(w // 128):
                    for gi, (g0, gs) in enumerate(grp):
                        bank = next_bank()
                        for kc in range(kcn):
                            kb.op('pe', lambda: nc.tensor.matmul(PS[bank][:, :gs], lhsT=SL[b][:, kc, mc * 128:(mc + 1) * 128],
                                                                 rhs=ACTB[:, kc, g0:g0 + gs], start=(kc == 0), stop=(kc == kcn - 1)),
                                  [('slab', b), ('actb', kc)], [('ps', bank)])
                        epi(bank, (s0 // 128) + mc, gi, g0, gs)
            items.append(([(wap, 0, kcn, c0 + s0, w)], compute))
        return items

    def dense_tm_items(ACTB, wap, c0, ncols, kcn, epi, tiles, cw=256):
        items = []
        for s0 in range(0, ncols, cw):
            w = min(cw, ncols - s0)

            def compute(bufs, s0=s0, w=w):
                b = bufs[0]
                for tt in tiles:
                    bank = next_bank()
                    for kc in range(kcn):
                        kb.op('pe', lambda: nc.tensor.matmul(PS[bank][:, :w], lhsT=ACTB[:, kc, tt * 128:(tt + 1) * 128],
                                                             rhs=SL[b][:, kc, 0:w], start=(kc == 0), stop=(kc == kcn - 1)),
                              [('slab', b), ('actb', kc)], [('ps', bank)])
                    epi(bank, s0, w, tt)
            items.append(([(wap, 0, kcn, c0 + s0, w)], compute))
        return items

    def phase_win(l, ACTB):
        slabs = alloc_slabs(2, KC, 256)
        SL[:] = slabs
        items = []
        stf = [ar.alloc([128, 512], F32) for _ in range(3)]
        stb = [ar.alloc([128, 512], BF16) for _ in range(3)]
        ctr = dict(f=0, b=0)
        W = w_in[l]

        def fm_store(dst, func=AF.Copy, scale=1.0, fp32=False):
            def epi(bank, m, gi, g0, gs):
                if fp32:
                    s = ctr['f'] % 3
                    ctr['f'] += 1
                    buf, key = stf[s], ('stf', s)
                else:
                    s = ctr['b'] % 3
                    ctr['b'] += 1
                    buf, key = stb[s], ('stb', s)
                kb.op('act', lambda: nc.scalar.activation(out=buf[:, :gs], in_=PS[bank][:, :gs], func=func, scale=scale),
                      [('ps', bank)], [key])
                kb.dma('sp', [(dst[m * 128:(m + 1) * 128, g0:g0 + gs], buf[:, :gs])], [key], [dst.name])
            return epi

        all_tiles = list(range(NTT))
        items += dense_fm_items(ACTB, W, off['au'], AW, KC, fm_store(gauT, AF.Gelu_apprx_tanh))
        items += dense_fm_items(ACTB, W, off['qkv'], 3 * BW, KC, fm_store(qkvT, fp32=True))
        items += dense_fm_items(ACTB, W, off['cq'], CQ, KC, fm_store(qT, scale=128 ** -0.5))
        items += dense_fm_items(ACTB, W, off['ck'], CKVW, KC, fm_store(kT))
        items += dense_fm_items(ACTB, W, off['iq'], IQW, KC, fm_store(iqT))
        items += dense_fm_items(ACTB, W, off['ik'], 128, KC, fm_store(ikT))
        items += dense_fm_items(ACTB, W, off['gate'], 3 * D, KC, fm_store(gT, AF.Sigmoid))

        def tm_epi(func, dst, dcol0, fp32, extra=None):
            def epi(bank, s0, w, tt):
                if fp32:
                    s = ctr['f'] % 3
                    ctr['f'] += 1
                    buf, key = stf[s], ('stf', s)
                else:
                    s = ctr['b'] % 3
                    ctr['b'] += 1
                    buf, key = stb[s], ('stb', s)
                kb.op('act', lambda: nc.scalar.activation(out=buf[:, :w], in_=PS[bank][:, :w], func=func),
                      [('ps', bank)], [key])
                if dst is not None:
                    kb.dma('sp', [(dst[tt * 128:(tt + 1) * 128, dcol0 + s0:dcol0 + s0 + w], buf[:, :w])], [key], [dst.name])
                if extra is not None:
                    extra(buf, key, s0, w, tt)
            return epi

        items += dense_tm_items(ACTB, W, off['av'], AW, KC, tm_epi(AF.Gelu_apprx_tanh, avg, 0, True), all_tiles)
        items += dense_tm_items(ACTB, W, off['z'], BW, KC, tm_epi(AF.Silu, szTM, 0, False), all_tiles)
        items += dense_tm_items(ACTB, W, off['a'], 2 * BH, KC, tm_epi(AF.Copy, smallTM, 0, True), all_tiles)
        items += dense_tm_items(ACTB, W, off['iw'], IH, KC, tm_epi(AF.Copy, smallTM, 32, True), all_tiles)

        def kv_extra(buf, key, s0, w, tt):
            for (lo, hi, outp, outs) in ((0, CKVW, okp, oks), (CKVW, 2 * CKVW, ovp, ovs)):
                a = max(lo, s0)
                e = min(hi, s0 + w)
                if a >= e:
                    continue
                if tt < NTP:
                    kb.dma('sp', [(outp[l, tt * 128:(tt + 1) * 128, a - lo:e - lo], buf[:, a - s0:e - s0])], [key], [outp.name], is_out=True)
                else:
                    kb.dma('sp', [(outs[l, :, a - lo:e - lo], buf[0:4, a - s0:e - s0])], [key], [outs.name], is_out=True)
                if lo == CKVW:
                    s = ctr['b'] % 3
                    ctr['b'] += 1
                    kb.op('dve', lambda: nc.vector.tensor_copy(out=stb[s][:, :e - a], in_=buf[:, a - s0:e - s0]), [key], [('stb', s)])
                    kb.dma('sp', [(vTM[tt * 128:(tt + 1) * 128, a - lo:e - lo], stb[s][:, :e - a])], [('stb', s)], [vTM.name])
        items += dense_tm_items(ACTB, W, off['ck'], 2 * CKVW, KC, tm_epi(AF.Copy, None, 0, True, kv_extra), all_tiles)

        def ik_extra(buf, key, s0, w, tt):
            if tt < NTP:
                kb.dma('sp', [(oikp[l, tt * 128:(tt + 1) * 128, :], buf[:, :128])], [key], [oikp.name], is_out=True)
            else:
                kb.dma('sp', [(oiks[l, :, :], buf[0:4, :128])], [key], [oiks.name], is_out=True)
        items += dense_tm_items(ACTB, W, off['ik'], 128, KC, tm_epi(AF.Copy, None, 0, True, ik_extra), all_tiles)

        def conv_extra(buf, key, s0, w, tt):
            if tt < NTP:
                kb.dma('sp', [(oconvp[l, :, s0:s0 + w], buf[125:128, :w])], [key], [oconvp.name], is_out=True)
            else:
                kb.dma('sp', [(oconvs[l, :, s0:s0 + w], buf[1:4, :w])], [key], [oconvs.name], is_out=True)
        items += dense_tm_items(ACTB, W, off['qkv'], 3 * BW, KC, tm_epi(AF.Copy, None, 0, True, conv_extra), [NTP - 1, NTP])
        run_items(slabs, items)
        kb.barrier()

    def phase_wbr(l):
        ar.reset()
        MK = MIX // 128
        YB = ar.alloc([128, MK, NT], BF16)
        for kc in range(MK):
            kb.dma('sp', [(YB[:, kc, :], ymixT[kc * 128:(kc + 1) * 128, :])], [ymixT.name], [('actb', kc)])
        slabs = alloc_slabs(2, MK, 128)
        SL[:] = slabs
        gt = [ar.alloc([128, 3, 512], BF16) for _ in range(2)]
        t0 = [ar.alloc([128, 512], F32) for _ in range(2)]
        t1 = [ar.alloc([128, 512], F32) for _ in range(2)]
        ob = [ar.alloc([128, 512], BF16) for _ in range(2)]
        br = [(0, AW // 128), (AW // 128, (AW + BW) // 128), ((AW + BW) // 128, MK)]
        ctr = [0]
        items = []
        gTv = gT.rearrange("(i m p) t -> m p i t", i=3, p=128)
        for s0 in range(0, D, 128):
            w = min(128, D - s0)

            def compute(bufs, s0=s0, w=w):
                b = bufs[0]
                for mc in range(w // 128):
                    m = s0 // 128 + mc
                    for gi, (g0, gs) in enumerate(groups):
                        s = ctr[0] % 2
                        ctr[0] += 1
                        kb.dma('sp', [(gt[s][:, :, :gs], gTv[m, :, :, g0:g0 + gs])], [gT.name], [('gt', s)])
                        banks = [next_bank() for _ in range(3)]
                        for bi, (k0, k1) in enumerate(br):
                            for kc in range(k0, k1):
                                kb.op('pe', lambda: nc.tensor.matmul(PS[banks[bi]][:, :gs], lhsT=SL[b][:, kc, mc * 128:(mc + 1) * 128],
                                                                     rhs=YB[:, kc, g0:g0 + gs], start=(kc == k0), stop=(kc == k1 - 1)),
                                      [('slab', b), ('actb', kc)], [('ps', banks[bi])])
                        kb.op('dve', lambda: nc.vector.tensor_tensor(out=t0[s][:, :gs], in0=PS[banks[0]][:, :gs], in1=gt[s][:, 0, :gs], op=ALU.mult),
                              [('ps', banks[0]), ('gt', s)], [('t0', s)])
                        kb.op('dve', lambda: nc.vector.tensor_tensor(out=t1[s][:, :gs], in0=PS[banks[1]][:, :gs], in1=gt[s][:, 1, :gs], op=ALU.mult),
                              [('ps', banks[1]), ('gt', s)], [('t1', s)])
                        kb.op('pool', lambda: nc.gpsimd.tensor_tensor(out=t0[s][:, :gs], in0=t0[s][:, :gs], in1=t1[s][:, :gs], op=ALU.add),
                              [('t0', s), ('t1', s)], [('t0', s)])
                        kb.op('dve', lambda: nc.vector.tensor_tensor(out=t1[s][:, :gs], in0=PS[banks[2]][:, :gs], in1=gt[s][:, 2, :gs], op=ALU.mult),
                              [('ps', banks[2]), ('gt', s)], [('t1', s)])
                        kb.op('pool', lambda: nc.gpsimd.tensor_tensor(out=ob[s][:, :gs], in0=t0[s][:, :gs], in1=t1[s][:, :gs], op=ALU.add),
                              [('t0', s), ('t1', s)], [('ob', s)])
                        kb.dma('sp', [(mT[m * 128:(m + 1) * 128, g0:g0 + gs], ob[s][:, :gs])], [('ob', s)], [mT.name])
            items.append(([(w_br[l], 0, MK, s0, w)], compute))
        run_items(slabs, items)
        kb.barrier()

    def resid_epi(srcx, dstx, xt, xo, ctr):
        def epi(bank, m, gi, g0, gs):
            s = ctr[0] % 2
            ctr[0] += 1
            kb.dma('sp', [(xt[s][:, :gs], srcx[m * 128:(m + 1) * 128, g0:g0 + gs])], [srcx.name], [('xt', s)])
            kb.op('dve', lambda: nc.vector.tensor_tensor(out=xo[s][:, :gs], in0=PS[bank][:, :gs], in1=xt[s][:, :gs], op=ALU.add),
                  [('ps', bank), ('xt', s)], [('xo', s)])
            kb.dma('sp', [(dstx[m * 128:(m + 1) * 128, g0:g0 + gs], xo[s][:, :gs])], [('xo', s)], [dstx.name])
        return epi

    def phase_wo(l, srcx, dstx):
        ar.reset()
        MB = ar.alloc([128, KC, NT], BF16)
        for kc in range(KC):
            kb.dma('sp', [(MB[:, kc, :], mT[kc * 128:(kc + 1) * 128, :])], [mT.name], [('actb', kc)])
        slabs = alloc_slabs(2, KC, 256)
        SL[:] = slabs
        xt = [ar.alloc([128, 512], F32) for _ in range(2)]
        xo = [ar.alloc([128, 512], F32) for _ in range(2)]
        items = dense_fm_items(MB, w_o[l], 0, D, KC, resid_epi(srcx, dstx, xt, xo, [0]))
        run_items(slabs, items)
        kb.barrier()

    def phase_w13(l, ACTB):
        slabs = alloc_slabs(4, KC, 128)
        SL[:] = slabs
        sg = [ar.alloc([128, 512], F32) for _ in range(2)]
        hb = [ar.alloc([128, 512], BF16) for _ in range(2)]
        ctr = [0]
        items = []
        for m in range(KF):
            def compute(bufs, m=m):
                b1, b3 = bufs
                for gi, (g0, gs) in enumerate(groups):
                    k1 = next_bank()
                    k3 = next_bank()
                    for (bb, bank) in ((b1, k1), (b3, k3)):
                        for kc in range(KC):
                            kb.op('pe', lambda: nc.tensor.matmul(PS[bank][:, :gs], lhsT=SL[bb][:, kc, 0:128],
                                                                 rhs=ACTB[:, kc, g0:g0 + gs], start=(kc == 0), stop=(kc == KC - 1)),
                                  [('slab', bb), ('actb', kc)], [('ps', bank)])
                    s = ctr[0] % 2
                    ctr[0] += 1
                    kb.op('act', lambda: nc.scalar.activation(out=sg[s][:, :gs], in_=PS[k1][:, :gs], func=AF.Silu),
                          [('ps', k1)], [('sg', s)])
                    kb.op('dve', lambda: nc.vector.tensor_tensor(out=hb[s][:, :gs], in0=PS[k3][:, :gs], in1=sg[s][:, :gs], op=ALU.mult),
                          [('ps', k3), ('sg', s)], [('hb', s)])
                    kb.dma('sp', [(hT[m * 128:(m + 1) * 128, g0:g0 + gs], hb[s][:, :gs])], [('hb', s)], [hT.name])
            items.append(([(ffn_w1[l], 0, KC, m * 128, 128), (ffn_w3[l], 0, KC, m * 128, 128)], compute))
        run_items(slabs, items)
        kb.barrier()

    def phase_w2(l, srcx, dstx):
        ar.reset()
        sgs = []
        cur_g = []
        tot = 0
        for g in groups:
            if tot + g[1] > 640:
                sgs.append(cur_g)
                cur_g, tot = [], 0
            cur_g.append(g)
            tot += g[1]
        sgs.append(cur_g)
        HB = ar.alloc([128, KF, 640], BF16)
        slabs = alloc_slabs(2, KF, 128)
        SL[:] = slabs
        xt = [ar.alloc([128, 512], F32) for _ in range(2)]
        xo = [ar.alloc([128, 512], F32) for _ in range(2)]
        epi = resid_epi(srcx, dstx, xt, xo, [0])
        items = []
        for si, sg_ in enumerate(sgs):
            base = sg_[0][0]
            width = sum(g[1] for g in sg_)
            for m in range(KC):
                def compute(bufs, m=m, sg_=sg_, base=base, width=width, first=(m == 0)):
                    b = bufs[0]
                    if first:
                        for kc in range(KF):
                            kb.dma('sp', [(HB[:, kc, 0:width], hT[kc * 128:(kc + 1) * 128, base:base + width])], [hT.name], [('hbres', kc)])
                    for gi, (g0, gs) in enumerate(sg_):
                        bank = next_bank()
                        for kc in range(KF):
                            kb.op('pe', lambda: nc.tensor.matmul(PS[bank][:, :gs], lhsT=SL[b][:, kc, 0:128],
                                                                 rhs=HB[:, kc, g0 - base:g0 - base + gs], start=(kc == 0), stop=(kc == KF - 1)),
                                  [('slab', b), ('hbres', kc)], [('ps', bank)])
                        epi(bank, m, gi, g0, gs)
                items.append(([(ffn_w2[l], 0, KF, m * 128, 128)], compute))
        run_items(slabs, items)
        kb.barrier()

    def phase_final(srcx, tmpx):
        ar.reset()
        xc = [ar.alloc([128, NT], F32) for _ in range(2)]
        sq = [ar.alloc([128, NT], F32) for _ in range(2)]
        yo = [ar.alloc([128, NT], F32) for _ in range(2)]
        rstd = ar.alloc([128, NT], F32)
        lncol = ar.alloc([128, KC], F32)
        kb.dma('sp', [(lncol, ln_f.rearrange("o (k p) -> p (o k)", p=128), dict(allow_slow_non_contiguous=True))], [], ['lncol'])
        for kc in range(KC):
            b = kc % 2
            kb.dma('sp', [(xc[b], srcx[kc * 128:(kc + 1) * 128, :])], [srcx.name], [('xc', b)])
            kb.op('act', lambda: nc.scalar.activation(out=sq[b], in_=xc[b], func=AF.Square), [('xc', b)], [('sq', b)])
            for gi, (g0, gs) in enumerate(groups):
                kb.op('pe', lambda: nc.tensor.matmul(PS[gi][:, :gs], lhsT=ones_f, rhs=sq[b][:, g0:g0 + gs],
                                                     start=(kc == 0), stop=(kc == KC - 1)),
                      [('sq', b), 'ones_f'], [('ps', gi)])
        for gi, (g0, gs) in enumerate(groups):
            kb.op('act', lambda: nc.scalar.activation(out=rstd[:, g0:g0 + gs], in_=PS[gi][:, :gs], func=AF.Sqrt,
                                                      scale=1.0 / D, bias=1e-6), [('ps', gi)], ['rstd'])
        kb.op('dve', lambda: nc.vector.reciprocal(out=rstd, in_=rstd), ['rstd'], ['rstd'])
        for kc in range(KC):
            b = kc % 2
            kb.dma('sp', [(xc[b], srcx[kc * 128:(kc + 1) * 128, :])], [srcx.name], [('xc', b)])
            kb.op('dve', lambda: nc.vector.scalar_tensor_tensor(out=yo[b], in0=xc[b], scalar=lncol[:, kc:kc + 1],
                                                                in1=rstd, op0=ALU.mult, op1=ALU.mult),
                  [('xc', b), 'rstd', 'lncol'], [('yo', b)])
            kb.dma('sp', [(tmpx[kc * 128:(kc + 1) * 128, :], yo[b])], [('yo', b)], [tmpx.name])
        kb.barrier()
        ar.reset()
        blk = [ar.alloc([128, KC, 128], F32) for _ in range(2)]
        row = [ar.alloc([128, D], F32) for _ in range(2)]
        for tt in range(NTT):
            b = tt % 2
            kb.dma('sp', [(blk[b], tmpx[:, tt * 128:(tt + 1) * 128].rearrange("(k p) t -> p k t", p=128))], [tmpx.name], [('blk', b)])
            for k4 in range(KC // 4):
                bank = next_bank()
                for j in range(4):
                    kc = k4 * 4 + j
                    kb.op('pe', lambda: nc.tensor.transpose(PS[bank][:, j * 128:(j + 1) * 128], blk[b][:, kc, :], ident_f),
                          [('blk', b), 'ident_f'], [('ps', bank)])
                kb.op('act', lambda: nc.scalar.copy(out=row[b][:, k4 * 512:(k4 + 1) * 512], in_=PS[bank][:]),
                      [('ps', bank)], [('row', b)])
            if tt < NTP:
                kb.dma('sp', [(yp[tt * 128:(tt + 1) * 128, :], row[b])], [('row', b)], [yp.name], is_out=True)
            else:
                kb.dma('sp', [(ys, row[b][0:4, :])], [('row', b)], [ys.name], is_out=True)
        kb.barrier()

    def mix_A(l):
        ar.reset()
        WsT = ar.alloc([128, AG, 128], BF16)
        bsf = ar.alloc([1, AG, 128], F32)
        bsrow = ar.alloc([1, AG, 128], BF16)
        lng = ar.alloc([128, AW], F32)
        lnb = ar.alloc([128, AW], F32)
        wtmp = ar.alloc([128, AG, 128], F32)
        kb.dma('sp', [(wtmp, a_ws[l].rearrange("g t s -> t g s"))], [], ['wtmp'])
        kb.dma('sp', [(bsf, a_bs[l:l + 1])], [], ['bsf'])
        kb.dma('sp', [(lng, a_ln_g[l:l + 1, :].to_broadcast([128, AW]))], [], ['lng'])
        kb.dma('sp', [(lnb, a_ln_b[l:l + 1, :].to_broadcast([128, AW]))], [], ['lnb'])
        for g in range(AG):
            bank = next_bank()
            kb.op('pe', lambda: nc.tensor.transpose(PS[bank][:, 0:128], wtmp[:, g, :], ident_f), ['wtmp', 'ident_f'], [('ps', bank)])
            kb.op('dve', lambda: nc.vector.tensor_tensor(out=WsT[:, g, :], in0=PS[bank][:, 0:128], in1=cmask[:, 5, :], op=ALU.mult),
                  [('ps', bank)], ['WsT'])
        kb.op('dve', lambda: nc.vector.tensor_copy(out=bsrow, in_=bsf), ['bsf'], ['bsrow'])
        cs = min(512, AW)
        nch = AW // cs
        avt = [ar.alloc([128, AW], F32) for _ in range(2)]
        avb = [ar.alloc([128, AW], BF16) for _ in range(2)]
        gau = [ar.alloc([128, AG, 128], BF16) for _ in range(2)]
        ya = [ar.alloc([128, AG, 128], BF16) for _ in range(2)]
        st = [ar.alloc([128, nch, 6], F32) for _ in range(2)]
        mv = [ar.alloc([128, 2], F32) for _ in range(2)]
        rs = [ar.alloc([128, 1], F32) for _ in range(2)]
        gauv = gauT.rearrange("(g c) t -> c g t", c=128)
        ymv = ymixT[0:AW, :].rearrange("(g c) t -> c g t", c=128)
        for tt in range(NTT):
            b = tt % 2
            kb.dma('sp', [(avt[b], avg[tt * 128:(tt + 1) * 128, :])], [avg.name], [('avt', b)])
            kb.dma('sp', [(gau[b], gauv[:, :, tt * 128:(tt + 1) * 128])], [gauT.name], [('gau', b)])
            for ci in range(nch):
                kb.op('dve', lambda: nc.vector.bn_stats(out=st[b][:, ci, :], in_=avt[b][:, ci * cs:(ci + 1) * cs]), [('avt', b)], [('st', b)])
            kb.op('dve', lambda: nc.vector.bn_aggr(out=mv[b], in_=st[b].rearrange("p a b -> p (a b)")), [('st', b)], [('mv', b)])
            kb.op('act', lambda: nc.scalar.activation(out=rs[b], in_=mv[b][:, 1:2], func=AF.Sqrt, bias=1e-5, scale=1.0), [('mv', b)], [('rs', b)])
            kb.op('dve', lambda: nc.vector.reciprocal(out=rs[b], in_=rs[b]), [('rs', b)], [('rs', b)])
            kb.op('dve', lambda: nc.vector.tensor_scalar(out=avt[b], in0=avt[b], scalar1=mv[b][:, 0:1], scalar2=rs[b][:, 0:1],
                                                         op0=ALU.subtract, op1=ALU.mult), [('avt', b), ('mv', b), ('rs', b)], [('avt', b)])
            kb.op('pool', lambda: nc.gpsimd.tensor_tensor(out=avt[b], in0=avt[b], in1=lng, op=ALU.mult), [('avt', b), 'lng'], [('avt', b)])
            kb.op('pool', lambda: nc.gpsimd.tensor_tensor(out=avt[b], in0=avt[b], in1=lnb, op=ALU.add), [('avt', b), 'lnb'], [('avt', b)])
            if tt == NTP:
                kb.dma('sp', [(ochunk[l], avt[b][0:4, :])], [('avt', b)], [ochunk.name], is_out=True)
            kb.op('act', lambda: nc.scalar.copy(out=avb[b], in_=avt[b]), [('avt', b)], [('avb', b)])
            for g0 in range(0, AG, 4):
                gn = min(4, AG - g0)
                bank = next_bank()
                for j in range(gn):
                    g = g0 + j
                    kb.op('pe', lambda: nc.tensor.matmul(PS[bank][:, j * 128:(j + 1) * 128], lhsT=avb[b][:, g * 128:(g + 1) * 128],
                                                         rhs=WsT[:, g, :], start=True, stop=False),
                          [('avb', b), 'WsT'], [('ps', bank)])
                    kb.op('pe', lambda: nc.tensor.matmul(PS[bank][:, j * 128:(j + 1) * 128], lhsT=ones_b[0:1, :],
                                                         rhs=bsrow[0:1, g, :], start=False, stop=True),
                          ['bsrow', 'ones_b'], [('ps', bank)])
                kb.op('dve', lambda: nc.vector.tensor_tensor(out=ya[b][:, g0:g0 + gn, :],
                                                             in0=PS[bank][:, 0:gn * 128].rearrange("p (a b) -> p a b", a=gn),
                                                             in1=gau[b][:, g0:g0 + gn, :], op=ALU.mult),
                      [('ps', bank), ('gau', b)], [('ya', b)])
            kb.dma('sp', [(ymv[:, :, tt * 128:(tt + 1) * 128], ya[b])], [('ya', b)], [ymixT.name])
        kb.barrier()

    def mix_C_prompt(l):
        ar.reset()
        G = CH // CKV
        KSEL = c['KSEL']
        ikR = ar.alloc([128, NT], BF16)
        kR = ar.alloc([128, CKV, NT], BF16)
        vR = ar.alloc([128, NTP, CKVW], BF16)
        kb.dma('sp', [(ikR, ikT)], [ikT.name], ['ikR'])
        kb.dma('sp', [(kR, kT.rearrange("(h d) t -> d h t", d=128))], [kT.name], ['kR'])
        kb.dma('sp', [(vR, vTM[0:T, :].rearrange("(n p) c -> p n c", p=128))], [vTM.name], ['vR'])
        iq = [ar.alloc([128, IH, 128], BF16) for _ in range(2)]
        qq = [ar.alloc([128, CH, 128], BF16) for _ in range(2)]
        iw = [ar.alloc([128, IH], F32) for _ in range(2)]
        acc = ar.alloc([128, T], F32)
        work = ar.alloc([128, T], F32)
        sel = ar.alloc([128, T], BF16)
        selT = ar.alloc([128, NTP, 128], BF16)
        tmp = [ar.alloc([128, 512], F32) for _ in range(2)]
        m8 = ar.alloc([128, 8], F32)
        thr = ar.alloc([128, 1], F32)
        pT = [ar.alloc([128, G, 128], BF16) for _ in range(2)]
        pTm = [ar.alloc([128, G, 128], BF16) for _ in range(2)]
        rden = ar.alloc([128, G * 128], F32)
        yc = [ar.alloc([128, G, 128], BF16) for _ in range(2)]
        iqv = iqT.rearrange("(h d) t -> d h t", d=128)
        qv = qT.rearrange("(h d) t -> d h t", d=128)
        ycv = ymixT[AW + BW:MIX, :].rearrange("(h d) t -> d h t", d=128)
        wscale = float(IH ** -0.5 * 128 ** -0.5)
        tctr = [0]
        pctr = [0]
        yctr = [0]
        for qt in range(NTP):
            b = qt % 2
            nk = (qt + 1) * 128
            nkb = qt + 1
            cols = slice(qt * 128, (qt + 1) * 128)
            kb.dma('sp', [(iq[b], iqv[:, :, cols])], [iqT.name], [('iq', b)])
            kb.dma('sp', [(qq[b], qv[:, :, cols])], [qT.name], [('qq', b)])
            kb.dma('sp', [(iw[b], smallTM[qt * 128:(qt + 1) * 128, 32:32 + IH])], [smallTM.name], [('iw', b)])
            kb.op('dve', lambda: nc.vector.tensor_scalar(out=iw[b], in0=iw[b], scalar1=wscale, scalar2=None, op0=ALU.mult),
                  [('iw', b)], [('iw', b)])
            for k0 in range(0, nk, 512):
                kw = min(512, nk - k0)
                for h in range(IH):
                    bank = next_bank(0, 2)
                    kb.op('pe', lambda: nc.tensor.matmul(PS[bank][:, :kw], lhsT=iq[b][:, h, :], rhs=ikR[:, k0:k0 + kw], start=True, stop=True),
                          [('iq', b), 'ikR'], [('ps', bank)])
                    s = tctr[0] % 2
                    tctr[0] += 1
                    kb.op('act', lambda: nc.scalar.activation(out=tmp[s][:, :kw], in_=PS[bank][:, :kw], func=AF.Relu),
                          [('ps', bank)], [('tmp', s)])
                    if h == 0:
                        kb.op('dve', lambda: nc.vector.tensor_scalar(out=acc[:, k0:k0 + kw], in0=tmp[s][:, :kw], scalar1=iw[b][:, 0:1],
                                                                     scalar2=None, op0=ALU.mult), [('tmp', s), ('iw', b)], ['acc'])
                    else:
                        kb.op('dve', lambda: nc.vector.scalar_tensor_tensor(out=acc[:, k0:k0 + kw], in0=tmp[s][:, :kw], scalar=iw[b][:, h:h + 1],
                                                                            in1=acc[:, k0:k0 + kw], op0=ALU.mult, op1=ALU.add),
                              [('tmp', s), ('iw', b), 'acc'], ['acc'])
            kb.op('dve', lambda: nc.vector.tensor_tensor(out=acc[:, cols], in0=acc[:, cols], in1=cmask[:, 4, :], op=ALU.add), ['acc'], ['acc'])
            if nk > KSEL:
                kb.op('act', lambda: nc.scalar.copy(out=work[:, :nk], in_=acc[:, :nk]), ['acc'], ['work'])
                for r in range(KSEL // 8):
                    kb.op('dve', lambda: nc.vector.max(out=m8, in_=work[:, :nk]), ['work'], ['m8'])
                    if r < KSEL // 8 - 1:
                        kb.op('dve', lambda: nc.vector.match_replace(out=work[:, :nk], in_to_replace=m8, in_values=work[:, :nk], imm_value=-3.0e38),
                              ['work', 'm8'], ['work'])
                kb.op('dve', lambda: nc.vector.tensor_scalar(out=thr, in0=m8[:, 7:8], scalar1=-1.0e29, scalar2=None, op0=ALU.max), ['m8'], ['thr'])
                kb.op('dve', lambda: nc.vector.tensor_scalar(out=sel[:, :nk], in0=acc[:, :nk], scalar1=thr[:, 0:1], scalar2=None, op0=ALU.is_ge),
                      ['acc', 'thr'], ['sel'])
            else:
                kb.op('dve', lambda: nc.vector.tensor_scalar(out=sel[:, :nk], in0=acc[:, :nk], scalar1=-1.0e29, scalar2=None, op0=ALU.is_ge),
                      ['acc'], ['sel'])
            for kb0 in range(0, nkb, 8):
                kn = min(8, nkb - kb0)
                bank = 2
                pbf = PS[bank][:].bitcast(BF16)
                for j in range(kn):
                    kb.op('pe', lambda: nc.tensor.transpose(pbf[:, j * 128:(j + 1) * 128], sel[:, (kb0 + j) * 128:(kb0 + j + 1) * 128], ident_b),
                          ['sel', 'ident_b'], [('ps', bank)])
                kb.op('act', lambda: nc.scalar.copy(out=selT[:, kb0:kb0 + kn, :], in_=pbf[:, 0:kn * 128].rearrange("p (a b) -> p a b", a=kn)),
                      [('ps', bank)], ['selT'])
            for kh in range(CKV):
                OT, DEN = 6, 7
                for kbi in range(nkb):
                    sbank = next_bank(3, 6)
                    kb.op('pe', lambda: nc.tensor.matmul(PS[sbank][:, :G * 128], lhsT=kR[:, kh, kbi * 128:(kbi + 1) * 128],
                                                         rhs=qq[b][:, kh * G:(kh + 1) * G, :].rearrange("p a b -> p (a b)"), start=True, stop=True),
                          ['kR', ('qq', b)], [('ps', sbank)])
                    s = pctr[0] % 2
                    pctr[0] += 1
                    kb.op('act', lambda: nc.scalar.activation(out=pT[s].rearrange("p a b -> p (a b)"), in_=PS[sbank][:, :G * 128], func=AF.Exp),
                          [('ps', sbank)], [('pT', s)])
                    for j in range(G):
                        kb.op('dve', lambda: nc.vector.tensor_tensor(out=pTm[s][:, j, :], in0=pT[s][:, j, :], in1=selT[:, kbi, :], op=ALU.mult),
                              [('pT', s), 'selT'], [('pTm', s)])
                    kb.op('pe', lambda: nc.tensor.matmul(PS[OT][:, :G * 128], lhsT=vR[:, kbi, kh * 128:(kh + 1) * 128],
                                                         rhs=pTm[s].rearrange("p a b -> p (a b)"), start=(kbi == 0), stop=(kbi == nkb - 1)),
                          ['vR', ('pTm', s)], [('ps', OT)])
                    kb.op('pe', lambda: nc.tensor.matmul(PS[DEN][:, :G * 128], lhsT=ones_b, rhs=pTm[s].rearrange("p a b -> p (a b)"),
                                                         start=(kbi == 0), stop=(kbi == nkb - 1)),
                          ['ones_b', ('pTm', s)], [('ps', DEN)])
                kb.op('dve', lambda: nc.vector.reciprocal(out=rden, in_=PS[DEN][:, :G * 128]), [('ps', DEN)], ['rden'])
                s = yctr[0] % 2
                yctr[0] += 1
                kb.op('dve', lambda: nc.vector.tensor_tensor(out=yc[s].rearrange("p a b -> p (a b)"), in0=PS[OT][:, :G * 128], in1=rden, op=ALU.mult),
                      [('ps', OT), 'rden'], [('yc', s)])
                kb.dma('sp', [(ycv[:, kh * G:(kh + 1) * G, cols], yc[s])], [('yc', s)], [ymixT.name])
        kb.barrier()

    qnT = dscr("qnT", [BW, NT], F32)
    knT = dscr("knT", [BW, NT], F32)
    vcT = dscr("vcT", [BW, NT], F32)

    def mix_B_conv(l):
        ar.reset()
        NCC = 3 * BH
        CBW = 6 + T + 128
        wc = ar.alloc([128, NCC, 4], F32)
        hs = ar.alloc([128, NCC, 3], F32)
        kb.dma('sp', [(wc[:, :, j], b_conv_w[l, j:j + 1, :].rearrange("o (n c) -> c (o n)", c=128), dict(allow_slow_non_contiguous=True)) for j in range(4)], [], ['wc'])
        kb.dma('sp', [(hs[:, :, j], sconv[l, j:j + 1, :].rearrange("o (n c) -> c (o n)", c=128), dict(allow_slow_non_contiguous=True)) for j in range(3)], [], ['hs'])
        cb = [ar.alloc([128, CBW], F32) for _ in range(2)]
        co = [ar.alloc([128, NT], F32) for _ in range(2)]
        sq = [ar.alloc([128, NT], F32) for _ in range(2)]
        rs = [ar.alloc([128, NT], F32) for _ in range(2)]
        for b in range(2):
            kb.op('dve', lambda: nc.vector.memset(cb[b][:, 0:3], 0.0), [], [('cb', b)])
        i = 0
        for part, dst in ((0, qnT), (1, knT), (2, vcT)):
            for h in range(BH):
                b = i % 2
                i += 1
                n = part * BH + h
                r0 = n * 128
                kb.dma('sp', [(cb[b][:, 3:3 + T], qkvT[r0:r0 + 128, 0:T]), (cb[b][:, 6 + T:6 + T + 128], qkvT[r0:r0 + 128, T:T + 128])],
                       [qkvT.name], [('cb', b)])
                kb.op('act', lambda: nc.scalar.copy(out=cb[b][:, 3 + T:6 + T], in_=hs[:, n, :]), ['hs'], [('cb', b)])
                for (o0, i0, wd) in ((0, 0, T), (T, 3 + T, 128)):
                    kb.op('dve', lambda: nc.vector.tensor_scalar(out=co[b][:, o0:o0 + wd], in0=cb[b][:, i0:i0 + wd], scalar1=wc[:, n, 0:1],
                                                                 scalar2=None, op0=ALU.mult), [('cb', b), 'wc'], [('co', b)])
                    for j in range(1, 4):
                        kb.op('dve', lambda: nc.vector.scalar_tensor_tensor(out=co[b][:, o0:o0 + wd], in0=cb[b][:, i0 + j:i0 + j + wd],
                                                                            scalar=wc[:, n, j:j + 1], in1=co[b][:, o0:o0 + wd],
                                                                            op0=ALU.mult, op1=ALU.add), [('cb', b), 'wc', ('co', b)], [('co', b)])
                kb.op('act', lambda: nc.scalar.activation(out=co[b], in_=co[b], func=AF.Silu), [('co', b)], [('co', b)])
                if part < 2:
                    kb.op('act', lambda: nc.scalar.activation(out=sq[b], in_=co[b], func=AF.Square), [('co', b)], [('sq', b)])
                    for gi, (g0, gs) in enumerate(groups):
                        bank = next_bank()
                        kb.op('pe', lambda: nc.tensor.matmul(PS[bank][:, :gs], lhsT=ones_f, rhs=sq[b][:, g0:g0 + gs], start=True, stop=True),
                              [('sq', b), 'ones_f'], [('ps', bank)])
                        kb.op('act', lambda: nc.scalar.activation(out=rs[b][:, g0:g0 + gs], in_=PS[bank][:, :gs], func=AF.Sqrt, bias=1e-6, scale=1.0),
                              [('ps', bank)], [('rs', b)])
                    kb.op('dve', lambda: nc.vector.reciprocal(out=rs[b], in_=rs[b]), [('rs', b)], [('rs', b)])
                    sc = 128 ** -0.5 if part == 0 else 1.0
                    kb.op('dve', lambda: nc.vector.scalar_tensor_tensor(out=co[b], in0=co[b], scalar=sc, in1=rs[b], op0=ALU.mult, op1=ALU.mult),
                          [('co', b), ('rs', b)], [('co', b)])
                kb.dma('sp', [(dst[h * 128:(h + 1) * 128, :], co[b])], [('co', b)], [dst.name])
        kb.barrier()

    def mix_B_scan(l):
        import os
        LIM = int(os.environ.get('MIXB_LIM', '99'))
        LIMH = int(os.environ.get('LIMH', '0'))
        ar.reset()
        Uf = cmask[:, 1, :]
        NEGI = cmask[:, 2, :]
        STRICT = cmask[:, 3, :]
        vcol = cmask[:, 6, 0:1]
        S = [ar.alloc([128, BH, 128], F32) for _ in range(2)]
        kb.op('dve', lambda: nc.vector.memset(S[0], 0.0), [], [('S', 0, h) for h in range(BH)])
        kb.dma('sp', [(S[1], sdelta[l].rearrange("h k v -> k h v"))], [], [('S', 1, h) for h in range(BH)])
        nalog = ar.alloc([128, BH], F32)
        dtb = ar.alloc([128, BH], F32)
        ogb = ar.alloc([128, 128], F32)
        kb.dma('sp', [(nalog, b_a_log[l:l + 1, :].to_broadcast([128, BH]))], [], ['nalog'])
        kb.dma('sp', [(dtb, b_dt_bias[l:l + 1, :].to_broadcast([128, BH]))], [], ['dtb'])
        kb.dma('sp', [(ogb, b_out_g[l:l + 1, :].to_broadcast([128, 128]))], [], ['ogb'])
        kb.op('act', lambda: nc.scalar.activation(out=nalog, in_=nalog, func=AF.Exp), ['nalog'], ['nalog'])
        kb.op('dve', lambda: nc.vector.tensor_scalar(out=nalog, in0=nalog, scalar1=-1.0, scalar2=None, op0=ALU.mult), ['nalog'], ['nalog'])
        qt_ = [ar.alloc([128, BH, 128], F32) for _ in range(2)]
        kt_ = [ar.alloc([128, BH, 128], F32) for _ in range(2)]
        vt_ = [ar.alloc([128, BH, 128], F32) for _ in range(2)]
        szt = [ar.alloc([128, BH, 128], BF16) for _ in range(2)]
        ab = [ar.alloc([128, 2 * BH], F32) for _ in range(2)]
        g_ = ar.alloc([128, BH], F32)
        beta = ar.alloc([128, BH], F32)
        nbeta = ar.alloc([128, BH], F32)
        egc = ar.alloc([128, BH], F32)
        gendb = ar.alloc([128, BH], F32)
        ekend = ar.alloc([128, BH], F32)
        bg = ar.alloc([128, BH], F32)
        GU = ar.alloc([128, BH, 128], F32)
        GBn = ar.alloc([128, BH, 128], F32)
        oall = ar.alloc([128, BH, 128], F32)
        osq = ar.alloc([128, BH, 128], F32)
        ss = ar.alloc([128, BH], F32)
        ybf = ar.alloc([128, BH, 128], BF16)
        ybT = ar.alloc([128, BH, 128], BF16)
        NSET = min(BH, 6)

        def hset():
            d = {}
            for nm in ('decay', 'P1T', 'attn', 'attnT', 'kend', 'kcdT', 'vn', 'tq'):
                d[nm] = ar.alloc([128, 128], F32)
            d['X'] = ar.alloc([128, 256], F32)
            d['P'] = [ar.alloc([128, 128], F32) for _ in range(7)]
            d['PT'] = [ar.alloc([128, 128], F32) for _ in range(6)]
            return d
        HS = [hset() for _ in range(NSET)]
        qv_ = qnT.rearrange("(h d) t -> d h t", d=128)
        kv_ = knT.rearrange("(h d) t -> d h t", d=128)
        vv_ = vcT.rearrange("(h d) t -> d h t", d=128)
        szv = szTM.rearrange("t (h e) -> t h e", e=128)
        ybv = ymixT[AW:AW + BW, :].rearrange("(h e) t -> e h t", e=128)
        cp = [0]

        def evac(out_ap, ps_ap, reads, writes):
            cp[0] += 1
            if cp[0] % 2:
                kb.op('act', lambda: nc.scalar.copy(out=out_ap, in_=ps_ap), reads, writes)
            else:
                kb.op('dve', lambda: nc.vector.tensor_copy(out=out_ap, in_=ps_ap), reads, writes)

        hctr = [0]
        for tt in range(NTT):
            b = tt % 2
            si = 0 if tt < NTP else 1
            cols = slice(tt * 128, (tt + 1) * 128)
            kb.dma('sp', [(qt_[b], qv_[:, :, cols])], [qnT.name], [('qt', b)])
            kb.dma('sp', [(kt_[b], kv_[:, :, cols])], [knT.name], [('kt', b)])
            kb.dma('sp', [(vt_[b], vv_[:, :, cols])], [vcT.name], [('vt', b)])
            kb.dma('sp', [(szt[b], szv[tt * 128:(tt + 1) * 128])], [szTM.name], [('szt', b)])
            kb.dma('sp', [(ab[b], smallTM[tt * 128:(tt + 1) * 128, 0:2 * BH])], [smallTM.name], [('ab', b)])
            kb.op('act', lambda: nc.scalar.activation(out=beta, in_=ab[b][:, BH:2 * BH], func=AF.Sigmoid), [('ab', b)], ['beta'])
            kb.op('dve', lambda: nc.vector.tensor_tensor(out=g_, in0=ab[b][:, 0:BH], in1=dtb, op=ALU.add), [('ab', b), 'dtb'], ['g'])
            kb.op('act', lambda: nc.scalar.activation(out=g_, in_=g_, func=AF.Exp), ['g'], ['g'])
            kb.op('act', lambda: nc.scalar.activation(out=g_, in_=g_, func=AF.Ln, bias=1.0, scale=1.0), ['g'], ['g'])
            kb.op('dve', lambda: nc.vector.tensor_tensor(out=g_, in0=g_, in1=nalog, op=ALU.mult), ['g', 'nalog'], ['g'])
            if si == 1:
                kb.op('dve', lambda: nc.vector.tensor_scalar(out=g_, in0=g_, scalar1=vcol, scalar2=None, op0=ALU.mult), ['g'], ['g'])
                kb.op('dve', lambda: nc.vector.tensor_scalar(out=beta, in0=beta, scalar1=vcol, scalar2=None, op0=ALU.mult), ['beta'], ['beta'])
            kb.op('dve', lambda: nc.vector.tensor_scalar(out=nbeta, in0=beta, scalar1=-1.0, scalar2=None, op0=ALU.mult), ['beta'], ['nbeta'])
            bk1 = next_bank()
            kb.op('pe', lambda: nc.tensor.matmul(PS[bk1][:, 0:BH], lhsT=Uf, rhs=g_, start=True, stop=True), ['g'], [('ps', bk1)])
            bk2 = next_bank()
            kb.op('pe', lambda: nc.tensor.matmul(PS[bk2][:, 0:BH], lhsT=ones_f, rhs=g_, start=True, stop=True), ['g'], [('ps', bk2)])
            kb.op('act', lambda: nc.scalar.activation(out=egc, in_=PS[bk1][:, 0:BH], func=AF.Exp), [('ps', bk1)], ['egc'])
            kb.op('act', lambda: nc.scalar.activation(out=gendb, in_=PS[bk2][:, 0:BH], func=AF.Exp), [('ps', bk2)], ['gendb'])
            kb.op('dve', lambda: nc.vector.tensor_copy(out=ekend, in_=PS[bk2][:, 0:BH]), [('ps', bk2)], ['ekend'])
            kb.op('dve', lambda: nc.vector.tensor_tensor(out=ekend, in0=ekend, in1=PS[bk1][:, 0:BH], op=ALU.subtract), [('ps', bk1), 'ekend'], ['ekend'])
            kb.op('act', lambda: nc.scalar.activation(out=ekend, in_=ekend, func=AF.Exp), ['ekend'], ['ekend'])
            kb.op('dve', lambda: nc.vector.tensor_tensor(out=bg, in0=beta, in1=egc, op=ALU.mult), ['beta', 'egc'], ['bg'])
            kb.op('dve', lambda: nc.vector.tensor_tensor(out=GU, in0=Uf.unsqueeze(1).to_broadcast([128, BH, 128]),
                                                         in1=g_.unsqueeze(2).to_broadcast([128, BH, 128]), op=ALU.mult), ['g'], ['GU'])
            kb.op('dve', lambda: nc.vector.tensor_scalar(out=GBn, in0=g_.unsqueeze(2).to_broadcast([128, BH, 128]), scalar1=-1.0, scalar2=None,
                                                         op0=ALU.mult), ['g'], ['GBn'])
            if LIM <= 1:
                kb.barrier()
                return
            HBATCH = len(HS)
            for h0 in range(0, BH, HBATCH):
                hs_ = list(range(h0, min(BH, h0 + HBATCH)))
                cx = {h: dict(i=n, H=HS[n]) for n, h in enumerate(hs_)}
                K_ = lambda h, nm: ('h', cx[h]['i'], nm)
                qh = lambda h: qt_[b][:, h, :]
                kh_ = lambda h: kt_[b][:, h, :]
                vh = lambda h: vt_[b][:, h, :]
                Sk = lambda h: ('S', si, h)
                Sh = lambda h: S[si][:, h, :]

                def st_D(h):
                    H = cx[h]['H']
                    bd = cx[h]['bd'] = next_bank()
                    kb.op('pe', lambda: nc.tensor.matmul(PS[bd][:, 0:128], lhsT=GU[:, h, :], rhs=ones_f, start=True, stop=False), ['GU'], [('ps', bd)])
                    kb.op('pe', lambda: nc.tensor.matmul(PS[bd][:, 0:128], lhsT=GBn[:, h, :], rhs=Uf, start=False, stop=False), ['GBn'], [('ps', bd)])
                    kb.op('pe', lambda: nc.tensor.matmul(PS[bd][:, 0:128], lhsT=ident_f, rhs=NEGI, start=False, stop=True), [], [('ps', bd)])

                def st_decay(h):
                    H = cx[h]['H']
                    bd = cx[h]['bd']
                    kb.op('act', lambda: nc.scalar.activation(out=H['decay'], in_=PS[bd][:, 0:128], func=AF.Exp), [('ps', bd)], [K_(h, 'decay')])

                def st_KK(h):
                    bkk = cx[h]['bkk'] = next_bank()
                    kb.op('pe', lambda: nc.tensor.matmul(PS[bkk][:, 0:128], lhsT=kh_(h), rhs=kh_(h), start=True, stop=True), [('kt', b)], [('ps', bkk)])
                    kb.op('pe', lambda: nc.tensor.matmul(PS[bkk][:, 128:256], lhsT=qh(h), rhs=kh_(h), start=True, stop=True), [('kt', b), ('qt', b)], [('ps', bkk)])

                def st_P1T(h):
                    H = cx[h]['H']
                    bkk = cx[h]['bkk']
                    kb.op('dve', lambda: nc.vector.scalar_tensor_tensor(out=H['P1T'], in0=PS[bkk][:, 0:128], scalar=nbeta[:, h:h + 1], in1=H['decay'],
                                                                        op0=ALU.mult, op1=ALU.mult), [('ps', bkk), 'nbeta', K_(h, 'decay')], [K_(h, 'P1T')])
                    kb.op('dve', lambda: nc.vector.tensor_tensor(out=H['attn'], in0=PS[bkk][:, 128:256], in1=H['decay'], op=ALU.mult),
                          [('ps', bkk), K_(h, 'decay')], [K_(h, 'attn')])

                def st_strict(h):
                    H = cx[h]['H']
                    kb.op('pool', lambda: nc.gpsimd.tensor_tensor(out=H['PT'][0], in0=H['P1T'], in1=STRICT, op=ALU.mult), [K_(h, 'P1T')], [K_(h, 'PT0')])

                def st_T12(h):
                    H = cx[h]['H']
                    bt1 = cx[h]['bt1'] = next_bank()
                    kb.op('pe', lambda: nc.tensor.transpose(PS[bt1][:, 0:128], H['PT'][0], ident_f), [K_(h, 'PT0')], [('ps', bt1)])
                    kb.op('pe', lambda: nc.tensor.transpose(PS[bt1][:, 128:256], H['attn'], ident_f), [K_(h, 'attn')], [('ps', bt1)])

                def st_T12e(h):
                    H = cx[h]['H']
                    bt1 = cx[h]['bt1']
                    evac(H['P'][0], PS[bt1][:, 0:128], [('ps', bt1)], [K_(h, 'P0')])
                    evac(H['attnT'], PS[bt1][:, 128:256], [('ps', bt1)], [K_(h, 'attnT')])

                def st_T3(h):
                    bt3 = cx[h]['bt3'] = next_bank()
                    kb.op('pe', lambda: nc.tensor.transpose(PS[bt3][:, 0:128], vh(h), ident_f), [('vt', b)], [('ps', bt3)])
                    kb.op('pe', lambda: nc.tensor.transpose(PS[bt3][:, 128:256], kh_(h), ident_f), [('kt', b)], [('ps', bt3)])

                def st_X(h):
                    H = cx[h]['H']
                    bt3 = cx[h]['bt3']
                    kb.op('dve', lambda: nc.vector.tensor_scalar(out=H['X'][:, 0:128], in0=PS[bt3][:, 0:128], scalar1=beta[:, h:h + 1], scalar2=None,
                                                                 op0=ALU.mult), [('ps', bt3), 'beta'], [K_(h, 'X')])
                    kb.op('dve', lambda: nc.vector.tensor_scalar(out=H['X'][:, 128:256], in0=PS[bt3][:, 128:256], scalar1=bg[:, h:h + 1], scalar2=None,
                                                                 op0=ALU.mult), [('ps', bt3), 'bg'], [K_(h, 'X')])
                    kb.op('dve', lambda: nc.vector.tensor_scalar(out=H['kend'], in0=PS[bt3][:, 128:256], scalar1=ekend[:, h:h + 1], scalar2=None,
                                                                 op0=ALU.mult), [('ps', bt3), 'ekend'], [K_(h, 'kend')])

                def mk_sq(lv):
                    def st_sq(h):
                        H = cx[h]['H']
                        bs1 = cx[h]['bs1'] = next_bank()
                        kb.op('pe', lambda: nc.tensor.matmul(PS[bs1][:, 0:128], lhsT=H['PT'][lv], rhs=H['P'][lv], start=True, stop=True),
                              [K_(h, 'PT%d' % lv), K_(h, 'P%d' % lv)], [('ps', bs1)])
                        if lv < 5:
                            kb.op('pe', lambda: nc.tensor.matmul(PS[bs1][:, 128:256], lhsT=H['P'][lv], rhs=H['PT'][lv], start=True, stop=True),
                                  [K_(h, 'PT%d' % lv), K_(h, 'P%d' % lv)], [('ps', bs1)])

                    def st_sqe(h):
                        H = cx[h]['H']
                        bs1 = cx[h]['bs1']
                        evac(H['P'][lv + 1], PS[bs1][:, 0:128], [('ps', bs1)], [K_(h, 'P%d' % (lv + 1))])
                        if lv < 5:
                            evac(H['PT'][lv + 1], PS[bs1][:, 128:256], [('ps', bs1)], [K_(h, 'PT%d' % (lv + 1))])
                    return st_sq, st_sqe

                def mk_ap(lv):
                    def st_ap(h):
                        H = cx[h]['H']
                        ba = cx[h]['ba'] = next_bank()
                        kb.op('pe', lambda: nc.tensor.matmul(PS[ba][:, 0:256], lhsT=H['P'][lv], rhs=H['X'], start=True, stop=True),
                              [K_(h, 'P%d' % lv), K_(h, 'X')], [('ps', ba)])

                    def st_ape(h):
                        H = cx[h]['H']
                        ba = cx[h]['ba']
                        kb.op('dve', lambda: nc.vector.tensor_tensor(out=H['X'], in0=H['X'], in1=PS[ba][:, 0:256], op=ALU.add),
                              [('ps', ba), K_(h, 'X')], [K_(h, 'X')])
                    return st_ap, st_ape

                def st_T4(h):
                    H = cx[h]['H']
                    bt4 = cx[h]['bt4'] = next_bank()
                    kb.op('pe', lambda: nc.tensor.transpose(PS[bt4][:, 0:128], H['X'][:, 128:256], ident_f), [K_(h, 'X')], [('ps', bt4)])

                def st_T4e(h):
                    H = cx[h]['H']
                    bt4 = cx[h]['bt4']
                    evac(H['kcdT'], PS[bt4][:, 0:128], [('ps', bt4)], [K_(h, 'kcdT')])

                def st_s1(h):
                    H = cx[h]['H']
                    b1 = cx[h]['b1'] = next_bank()
                    kb.op('pe', lambda: nc.tensor.matmul(PS[b1][:, 0:128], lhsT=H['kcdT'], rhs=Sh(h), start=True, stop=True), [K_(h, 'kcdT'), Sk(h)], [('ps', b1)])
                    kb.op('pe', lambda: nc.tensor.matmul(PS[b1][:, 128:256], lhsT=qh(h), rhs=Sh(h), start=True, stop=True), [('qt', b), Sk(h)], [('ps', b1)])

                def st_s1e(h):
                    H = cx[h]['H']
                    b1 = cx[h]['b1']
                    kb.op('dve', lambda: nc.vector.tensor_tensor(out=H['vn'], in0=H['X'][:, 0:128], in1=PS[b1][:, 0:128], op=ALU.subtract),
                          [('ps', b1), K_(h, 'X')], [K_(h, 'vn')])
                    kb.op('dve', lambda: nc.vector.tensor_scalar(out=H['tq'], in0=PS[b1][:, 128:256], scalar1=egc[:, h:h + 1], scalar2=None,
                                                                 op0=ALU.mult), [('ps', b1), 'egc'], [K_(h, 'tq')])

                def st_s2(h):
                    H = cx[h]['H']
                    b2 = cx[h]['b2'] = next_bank()
                    kb.op('pe', lambda: nc.tensor.matmul(PS[b2][:, 0:128], lhsT=H['attnT'], rhs=H['vn'], start=True, stop=True),
                          [K_(h, 'attnT'), K_(h, 'vn')], [('ps', b2)])
                    kb.op('pe', lambda: nc.tensor.matmul(PS[b2][:, 128:256], lhsT=H['kend'], rhs=H['vn'], start=True, stop=True),
                          [K_(h, 'kend'), K_(h, 'vn')], [('ps', b2)])

                def st_s2e(h):
                    H = cx[h]['H']
                    b2 = cx[h]['b2']
                    kb.op('dve', lambda: nc.vector.tensor_tensor(out=oall[:, h, :], in0=H['tq'], in1=PS[b2][:, 0:128], op=ALU.add),
                          [('ps', b2), K_(h, 'tq')], [('oall', h)])
                    kb.op('dve', lambda: nc.vector.scalar_tensor_tensor(out=Sh(h), in0=Sh(h), scalar=gendb[:, h:h + 1], in1=PS[b2][:, 128:256],
                                                                        op0=ALU.mult, op1=ALU.add), [('ps', b2), Sk(h), 'gendb'], [Sk(h)])

                stages = [st_D, st_decay, st_KK, st_P1T, st_strict, st_T12, st_T12e, st_T3, st_X]
                for lv in range(6):
                    stages += list(mk_sq(lv))
                for lv in range(7):
                    stages += list(mk_ap(lv))
                stages += [st_T4, st_T4e, st_s1, st_s1e, st_s2, st_s2e]
                SLIM = int(os.environ.get('STG_LIM', '999'))
                for sn, stg in enumerate(stages):
                    if sn >= SLIM:
                        break
                    for h in hs_:
                        stg(h)
                if SLIM < 999:
                    kb.barrier()
                    return
            oall_k = [('oall', h) for h in range(BH)]
            kb.op('act', lambda: nc.scalar.activation(out=osq, in_=oall, func=AF.Square), oall_k, ['osq'])
            kb.op('dve', lambda: nc.vector.tensor_reduce(out=ss, in_=osq, axis=AX.X, op=ALU.add), ['osq'], ['ss'])
            kb.op('act', lambda: nc.scalar.activation(out=ss, in_=ss, func=AF.Sqrt, scale=1.0 / 128, bias=1e-6), ['ss'], ['ss'])
            kb.op('dve', lambda: nc.vector.reciprocal(out=ss, in_=ss), ['ss'], ['ss'])
            if LIM <= 10:
                kb.barrier()
                return
            kb.op('dve', lambda: nc.vector.tensor_tensor(out=osq, in0=oall, in1=ss.unsqueeze(2).to_broadcast([128, BH, 128]), op=ALU.mult),
                  oall_k + ['ss', 'osq'], ['osq'])
            kb.op('pool', lambda: nc.gpsimd.tensor_tensor(out=osq, in0=osq, in1=ogb.unsqueeze(1).to_broadcast([128, BH, 128]), op=ALU.mult),
                  ['osq', 'ogb'], ['osq'])
            kb.op('dve', lambda: nc.vector.tensor_tensor(out=ybf, in0=osq, in1=szt[b], op=ALU.mult), ['osq', ('szt', b)], ['ybf'])
            if LIM <= 11:
                kb.barrier()
                return
            for h0 in range(0, BH, 8):
                hn = min(8, BH - h0)
                bank = next_bank()
                pbf = PS[bank][:].bitcast(BF16)
                for j in range(hn):
                    kb.op('pe', lambda: nc.tensor.transpose(pbf[:, j * 128:(j + 1) * 128], ybf[:, h0 + j, :], ident_b), ['ybf'], [('ps', bank)])
                kb.op('act', lambda: nc.scalar.copy(out=ybT[:, h0:h0 + hn, :], in_=pbf[:, 0:hn * 128].rearrange("p (a b) -> p a b", a=hn)),
                      [('ps', bank)], ['ybT'])
            kb.dma('sp', [(ybv[:, :, cols], ybT)], ['ybT'], [ymixT.name])
            if LIM <= 12:
                kb.barrier()
                return
            if tt == NTP - 1:
                kb.dma('sp', [(odeltap[l].rearrange("h k v -> k h v"), S[0])], [('S', 0, h) for h in range(BH)], [odeltap.name], is_out=True)
            if tt == NTP:
                kb.dma('sp', [(odeltas[l].rearrange("h k v -> k h v"), S[1])], [('S', 1, h) for h in range(BH)], [odeltas.name], is_out=True)
        kb.barrier()

    candD = dscr("candD", [128, 256], F32)

    def mix_C_sample(l):
        ar.reset()
        P = c['PAGES']
        NCH = 32
        M4 = 128
        CW = 4 * P
        KS = c['KSELS']
        K1 = min(KS, CW)
        G = CH // CKV
        R64 = 4 * IH
        Wide = cmask2[0:R64, 0:256]
        SelM = cmask2[0:R64, 256:260]
        Ind = cmask2[0:4, 260:260 + M4]
        wscale = float(IH ** -0.5 * 128 ** -0.5)
        NPH = c['NPHYS']
        pidx = ar.alloc([P, 1], I32)
        kb.dma('sp', [(pidx, ptab.rearrange("o p -> p o"), dict(allow_slow_non_contiguous=True))], [], ['pidx'])
        if l > 0:
            kb.op('dve', lambda: nc.vector.tensor_scalar(out=pidx, in0=pidx, scalar1=float(l * NPH), scalar2=None, op0=ALU.add), ['pidx'], ['pidx'])
        iqs = ar.alloc([128, IH, 4], BF16)
        qs = ar.alloc([128, CH, 4], BF16)
        ikn = ar.alloc([128, 4], BF16)
        ktn = ar.alloc([128, CKV, 4], BF16)
        vn_ = ar.alloc([4, CKVW], BF16)
        wcol = ar.alloc([R64, 1], F32)
        kb.dma('sp', [(iqs, iqT.rearrange("(h d) t -> d h t", d=128)[:, :, T:T + 4])], [iqT.name], ['iqs'])
        kb.dma('sp', [(qs, qT.rearrange("(h d) t -> d h t", d=128)[:, :, T:T + 4])], [qT.name], ['qs'])
        kb.dma('sp', [(ikn, ikT[:, T:T + 4])], [ikT.name], ['ikn'])
        kb.dma('sp', [(ktn, kT.rearrange("(h d) t -> d h t", d=128)[:, :, T:T + 4])], [kT.name], ['ktn'])
        kb.dma('sp', [(vn_, vTM[T:T + 4, :])], [vTM.name], ['vn_'])
        kb.dma('sp', [(wcol[h * 4:(h + 1) * 4, :], smallTM[T:T + 4, 32 + h:33 + h], dict(allow_slow_non_contiguous=True)) for h in range(IH)], [smallTM.name], ['wcol'])
        kb.op('dve', lambda: nc.vector.tensor_scalar(out=wcol, in0=wcol, scalar1=wscale, scalar2=None, op0=ALU.mult), ['wcol'], ['wcol'])
        iqs2 = iqs.rearrange("p a b -> p (a b)")

        idxI = ar.alloc([P, 8], I32)
        idxK = ar.alloc([P, 16], I32)
        for ch in range(8):
            kb.op('dve', lambda: nc.vector.tensor_scalar(out=idxI[:, ch:ch + 1], in0=pidx, scalar1=8.0, scalar2=float(ch), op0=ALU.mult, op1=ALU.add),
                  ['pidx'], ['idxI'])
        for ch in range(16):
            kb.op('dve', lambda: nc.vector.tensor_scalar(out=idxK[:, ch:ch + 1], in0=pidx, scalar1=16.0, scalar2=float(ch), op0=ALU.mult, op1=ALU.add),
                  ['pidx'], ['idxK'])

        def gather(dst, srcap, nch, ch, idxt, ikey, key):
            def fn():
                flat = srcap.rearrange("l (n c s) w -> (l n c) (s w)", c=nch, s=128 // nch)
                return nc.gpsimd.indirect_dma_start(out=dst.rearrange("p a b -> p (a b)"), out_offset=None, in_=flat,
                                                    in_offset=bass.IndirectOffsetOnAxis(ap=idxt[:, ch:ch + 1], axis=0),
                                                    oob_is_err=False)
            kb.dma_custom('pool', fn, [ikey], [key])

        KI = ar.alloc([P, 128, 128], BF16)
        for s0 in range(0, 128, 16):
            gather(KI[:, s0:s0 + 16, :], cache_kidx, 8, s0 // 16, idxI, 'idxI', 'KI')
        ikc = [ar.alloc([128, CW], BF16) for _ in range(2)]
        rr = [ar.alloc([R64, CW], F32) for _ in range(2)]
        ACC = 7
        for cidx in range(NCH):
            cb_ = cidx % 2
            bank = next_bank(0, 2)
            pbf = PS[bank][:].bitcast(BF16)
            for jb in range(4):
                s = cidx * 4 + jb
                kb.op('pe', lambda: nc.tensor.transpose(pbf[:, jb * P:(jb + 1) * P], KI[:, s, :], ident_b[0:P, 0:P]), ['KI'], [('ps', bank)])
            kb.op('act', lambda: nc.scalar.copy(out=ikc[cb_], in_=pbf[:, 0:CW]), [('ps', bank)], [('ikc', cb_)])
            b2_ = next_bank(2, 4)
            kb.op('pe', lambda: nc.tensor.matmul(PS[b2_][0:R64, 0:CW], lhsT=iqs2, rhs=ikc[cb_], start=True, stop=True), ['iqs', ('ikc', cb_)], [('ps', b2_)])
            kb.op('dve', lambda: nc.vector.tensor_scalar(out=rr[cb_], in0=PS[b2_][0:R64, 0:CW], scalar1=0.0, scalar2=wcol[:, 0:1], op0=ALU.max, op1=ALU.mult),
                  [('ps', b2_), 'wcol'], [('rr', cb_)])
            kb.op('pe', lambda: nc.tensor.matmul(PS[ACC][0:M4, 0:CW], lhsT=Wide[:, NCH - 1 - cidx:NCH - 1 - cidx + M4], rhs=rr[cb_],
                                                 start=(cidx == 0), stop=(cidx == NCH - 1)), [('rr', cb_)], [('ps', ACC)])
        acc = ar.alloc([M4, CW], F32)
        work = ar.alloc([M4, CW], F32)
        kb.op('act', lambda: nc.scalar.copy(out=acc, in_=PS[ACC][0:M4, 0:CW]), [('ps', ACC)], ['acc'])
        kb.op('dve', lambda: nc.vector.tensor_copy(out=work, in_=acc), ['acc'], ['work'])
        rn = ar.alloc([R64, 4], F32)
        accn = ar.alloc([4, 4], F32)
        bn = next_bank(2, 4)
        kb.op('pe', lambda: nc.tensor.matmul(PS[bn][0:R64, 0:4], lhsT=iqs2, rhs=ikn, start=True, stop=True), ['iqs', 'ikn'], [('ps', bn)])
        kb.op('dve', lambda: nc.vector.tensor_scalar(out=rn, in0=PS[bn][0:R64, 0:4], scalar1=0.0, scalar2=wcol[:, 0:1], op0=ALU.max, op1=ALU.mult),
              [('ps', bn), 'wcol'], ['rn'])
        bn2 = next_bank(2, 4)
        kb.op('pe', lambda: nc.tensor.matmul(PS[bn2][0:4, 0:4], lhsT=SelM, rhs=rn, start=True, stop=True), ['rn'], [('ps', bn2)])
        kb.op('dve', lambda: nc.vector.tensor_tensor(out=accn, in0=PS[bn2][0:4, 0:4], in1=cmask[0:4, 4, 0:4], op=ALU.add), [('ps', bn2)], ['accn'])
        cand1 = ar.alloc([M4, K1], F32)
        m8 = ar.alloc([128, 8], F32)
        for r in range(K1 // 8):
            kb.op('dve', lambda: nc.vector.max(out=cand1[:, r * 8:(r + 1) * 8], in_=work), ['work'], ['cand1'])
            if r < K1 // 8 - 1:
                kb.op('dve', lambda: nc.vector.match_replace(out=work, in_to_replace=cand1[:, r * 8:(r + 1) * 8], in_values=work, imm_value=-3.0e38),
                      ['work', 'cand1'], ['work'])
        W2 = NCH * K1 + 8
        cand2 = ar.alloc([4, W2], F32)
        cdv = candD.rearrange("a b -> (a b)")[0:M4 * K1]
        kb.dma('sp', [(cdv.rearrange("(m k) -> m k", k=K1), cand1)], ['cand1'], [candD.name])
        kb.dma('sp', [(cand2[:, 0:NCH * K1], cdv.rearrange("(t x) -> t x", t=4))], [candD.name], ['cand2'])
        kb.op('dve', lambda: nc.vector.tensor_copy(out=cand2[:, NCH * K1:NCH * K1 + 4], in_=accn), ['accn', 'cand2'], ['cand2'])
        kb.op('dve', lambda: nc.vector.memset(cand2[:, NCH * K1 + 4:W2], -3.0e38), ['cand2'], ['cand2'])
        for r in range(KS // 8):
            kb.op('dve', lambda: nc.vector.max(out=m8[0:4, :], in_=cand2), ['cand2'], ['m8'])
            if r < KS // 8 - 1:
                kb.op('dve', lambda: nc.vector.match_replace(out=cand2, in_to_replace=m8[0:4, :], in_values=cand2, imm_value=-3.0e38),
                      ['cand2', 'm8'], ['cand2'])
        thr4 = ar.alloc([4, 2], F32)
        kb.op('dve', lambda: nc.vector.tensor_copy(out=thr4[:, 0:1], in_=m8[0:4, 7:8]), ['m8'], ['thr4'])
        kb.op('dve', lambda: nc.vector.tensor_copy(out=thr4[:, 1:2], in_=m8[0:4, 7:8]), ['m8', 'thr4'], ['thr4'])
        bt_ = next_bank(2, 4)
        kb.op('pe', lambda: nc.tensor.matmul(PS[bt_][0:M4, 0:2], lhsT=Ind, rhs=thr4, start=True, stop=True), ['thr4'], [('ps', bt_)])
        thrtc = ar.alloc([M4, 2], F32)
        kb.op('act', lambda: nc.scalar.copy(out=thrtc, in_=PS[bt_][0:M4, 0:2]), [('ps', bt_)], ['thrtc'])
        seltc = ar.alloc([M4, CW], BF16)
        kb.op('dve', lambda: nc.vector.tensor_scalar(out=seltc, in0=acc, scalar1=thrtc[:, 0:1], scalar2=None, op0=ALU.is_ge), ['acc', 'thrtc'], ['seltc'])
        seln = ar.alloc([4, 4], F32)
        kb.op('dve', lambda: nc.vector.tensor_scalar(out=seln, in0=accn, scalar1=thr4[:, 0:1], scalar2=None, op0=ALU.is_ge), ['accn', 'thr4'], ['seln'])
        selT = ar.alloc([P, 4, M4], BF16)
        bs_ = next_bank(2, 4)
        pbs = PS[bs_][:].bitcast(BF16)
        for jb in range(4):
            kb.op('pe', lambda: nc.tensor.transpose(pbs[0:P, jb * M4:(jb + 1) * M4], seltc[:, jb * P:(jb + 1) * P], ident_b),
                  ['seltc'], [('ps', bs_)])
        kb.op('act', lambda: nc.scalar.copy(out=selT.rearrange("p a b -> p (a b)"), in_=pbs[0:P, 0:4 * M4]), [('ps', bs_)], ['selT'])
        selnT = ar.alloc([4, 4], BF16)
        bs2 = next_bank(2, 4)
        kb.op('pe', lambda: nc.tensor.transpose(PS[bs2][0:4, 0:4], seln, ident_f[0:4, 0:4]), ['seln'], [('ps', bs2)])
        kb.op('act', lambda: nc.scalar.copy(out=selnT, in_=PS[bs2][0:4, 0:4]), [('ps', bs2)], ['selnT'])
        SC = 8
        kpg = [ar.alloc([P, SC, CKVW], BF16) for _ in range(2)]
        vpg = [ar.alloc([P, SC, CKVW], BF16) for _ in range(2)]
        kpf = [ar.alloc([P, SC, CKVW], F32)] * 2
        vpf = [ar.alloc([P, SC, CKVW], F32)] * 2
        ktj = [ar.alloc([128, CKV, P], BF16) for _ in range(2)]
        pT = [ar.alloc([P, CH, 4], BF16) for _ in range(2)]
        pTm = [ar.alloc([P, CH, 4], BF16) for _ in range(2)]
        OT, DEN = 5, 6
        NQ = CH * 4
        qs2 = qs.rearrange("p a b -> p (a b)")
        first = [True]
        for s in range(128):
            cidx, jb = s // 4, s % 4
            gb = (s // SC) % 2
            sl = s % SC
            if sl == 0:
                gather(kpf[gb], cache_k, 16, s // SC, idxK, 'idxK', ('kpf', 0))
                gather(vpf[gb], cache_v, 16, s // SC, idxK, 'idxK', ('vpf', 0))
                kb.op('act', lambda: nc.scalar.copy(out=kpg[gb], in_=kpf[gb]), [('kpf', 0)], [('kpg', gb)])
                kb.op('dve', lambda: nc.vector.tensor_copy(out=vpg[gb], in_=vpf[gb]), [('vpf', 0)], [('vpg', gb)])
            s2 = s % 2
            bank = next_bank(0, 2)
            pbf = PS[bank][:].bitcast(BF16)
            for kh in range(CKV):
                kb.op('pe', lambda: nc.tensor.transpose(pbf[:, kh * P:(kh + 1) * P], kpg[gb][:, sl, kh * 128:(kh + 1) * 128], ident_b[0:P, 0:P]),
                      [('kpg', gb)], [('ps', bank)])
            kb.op('act', lambda: nc.scalar.copy(out=ktj[s2].rearrange("p a b -> p (a b)"), in_=pbf[:, 0:CKV * P]), [('ps', bank)], [('ktj', s2)])
            sb_ = next_bank(2, 4)
            for kh in range(CKV):
                kb.op('pe', lambda: nc.tensor.matmul(PS[sb_][0:P, kh * G * 4:(kh + 1) * G * 4], lhsT=ktj[s2][:, kh, :], rhs=qs2[:, kh * G * 4:(kh + 1) * G * 4],
                                                     start=True, stop=True), [('ktj', s2), 'qs'], [('ps', sb_)])
            kb.op('act', lambda: nc.scalar.activation(out=pT[s2].rearrange("p a b -> p (a b)"), in_=PS[sb_][0:P, 0:NQ], func=AF.Exp), [('ps', sb_)], [('pT', s2)])
            kb.op('dve', lambda: nc.vector.tensor_tensor(out=pTm[s2], in0=pT[s2],
                                                         in1=selT[:, jb, cidx:M4:NCH].unsqueeze(1).to_broadcast([P, CH, 4]), op=ALU.mult),
                  [('pT', s2), 'selT'], [('pTm', s2)])
            pm2 = pTm[s2].rearrange("p a b -> p (a b)")
            for kh in range(CKV):
                kb.op('pe', lambda: nc.tensor.matmul(PS[OT][:, kh * G * 4:(kh + 1) * G * 4], lhsT=vpg[gb][:, sl, kh * 128:(kh + 1) * 128],
                                                     rhs=pm2[:, kh * G * 4:(kh + 1) * G * 4], start=first[0], stop=False, skip_group_check=True),
                      [('vpg', gb), ('pTm', s2)], [('ps', OT)])
                first[0] = False
            kb.op('pe', lambda: nc.tensor.matmul(PS[DEN][:, 0:NQ], lhsT=ones_b[0:P, :], rhs=pm2, start=(s == 0), stop=False, skip_group_check=True),
                  [('pTm', s2)], [('ps', DEN)])
        sbn = next_bank(2, 4)
        for kh in range(CKV):
            kb.op('pe', lambda: nc.tensor.matmul(PS[sbn][0:4, kh * G * 4:(kh + 1) * G * 4], lhsT=ktn[:, kh, :], rhs=qs2[:, kh * G * 4:(kh + 1) * G * 4],
                                                 start=True, stop=True), ['ktn', 'qs'], [('ps', sbn)])
        pTn = ar.alloc([4, CH, 4], BF16)
        pTnm = ar.alloc([4, CH, 4], BF16)
        kb.op('act', lambda: nc.scalar.activation(out=pTn.rearrange("p a b -> p (a b)"), in_=PS[sbn][0:4, 0:NQ], func=AF.Exp), [('ps', sbn)], ['pTn'])
        kb.op('dve', lambda: nc.vector.tensor_tensor(out=pTnm, in0=pTn, in1=selnT.unsqueeze(1).to_broadcast([4, CH, 4]), op=ALU.mult),
              ['pTn', 'selnT'], ['pTnm'])
        pn2 = pTnm.rearrange("p a b -> p (a b)")
        for kh in range(CKV):
            kb.op('pe', lambda: nc.tensor.matmul(PS[OT][:, kh * G * 4:(kh + 1) * G * 4], lhsT=vn_[0:4, kh * 128:(kh + 1) * 128],
                                                 rhs=pn2[:, kh * G * 4:(kh + 1) * G * 4], start=False, stop=(kh == CKV - 1), skip_group_check=True),
                  ['vn_', 'pTnm'], [('ps', OT)])
        kb.op('pe', lambda: nc.tensor.matmul(PS[DEN][:, 0:NQ], lhsT=ones_b[0:4, :], rhs=pn2, start=False, stop=True, skip_group_check=True),
              ['pTnm'], [('ps', DEN)])
        rden = ar.alloc([128, NQ], F32)
        ycs = ar.alloc([128, CH, 128], BF16)
        kb.op('dve', lambda: nc.vector.reciprocal(out=rden, in_=PS[DEN][:, 0:NQ]), [('ps', DEN)], ['rden'])
        kb.op('pool', lambda: nc.gpsimd.memset(ycs, 0.0), [], ['ycs'])
        kb.op('dve', lambda: nc.vector.tensor_tensor(out=ycs[:, :, 0:4], in0=PS[OT][:, 0:NQ].rearrange("p (a b) -> p a b", b=4),
                                                     in1=rden.rearrange("p (a b) -> p a b", b=4), op=ALU.mult), [('ps', OT), 'rden', 'ycs'], ['ycs'])
        ycv = ymixT[AW + BW:MIX, :].rearrange("(h d) t -> d h t", d=128)
        kb.dma('sp', [(ycv[:, :, T:T + 128], ycs)], ['ycs'], [ymixT.name])
        kb.barrier()

    def phase_mixers(l):
        mix_A(l)
        mix_C_prompt(l)
        import os
        if os.environ.get('MIXCS', '1') == '1':
            mix_C_sample(l)
        if os.environ.get('MIXB', 'all') in ('conv', 'all'):
            mix_B_conv(l)
        if os.environ.get('MIXB', 'all') == 'all':
            mix_B_scan(l)


    phase_loadx()
    done = False
    for l in range(DEPTH):
        ar.reset()
        ACTB = alloc_actb()
        mark = ar.off
        phase_norm(xresA, ln1[l:l + 1, :], ACTB)
        kb.barrier()
        ar.off = mark
        phase_win(l, ACTB)
        if stop_after == 'win':
            done = True
            break
        phase_mixers(l)
        if stop_after == 'mix':
            done = True
            break
        phase_wbr(l)
        phase_wo(l, xresA, xresB)
        ar.reset()
        ACTB = alloc_actb()
        mark = ar.off
        phase_norm(xresB, ln2[l:l + 1, :], ACTB)
        kb.barrier()
        ar.off = mark
        phase_w13(l, ACTB)
        phase_w2(l, xresB, xresA)
        if stop_after == 'l0':
            done = True
            break
    if not done:
        phase_final(xresA, xresB)

    scr = dict(xresA=xresA, gauT=gauT, qkvT=qkvT, qT=qT, kT=kT, iqT=iqT, ikT=ikT, gT=gT, avg=avg, szTM=szTM,
               smallTM=smallTM, vTM=vTM, ymixT=ymixT, mT=mT, hT=hT, xresB=xresB)
    if dbg:
        ar.reset()
        for name, ap in dbg.items():
            src = scr[name]
            rows = src.shape[0]
            for r0 in range(0, rows, 128):
                rr = min(128, rows - r0)
                if src.dtype == BF16:
                    tb = ar.alloc([128, src.shape[1]], BF16)
                    tf = ar.alloc([128, src.shape[1]], F32)
                    kb.dma('sp', [(tb[0:rr], src[r0:r0 + rr, :])], [src.name], ['dbg_tb'])
                    kb.op('dve', lambda: nc.vector.tensor_copy(out=tf[0:rr], in_=tb[0:rr]), ['dbg_tb'], ['dbg_tf'])
                    kb.dma('sp', [(ap[r0:r0 + rr, :], tf[0:rr])], ['dbg_tf'], [ap.name])
                    kb.barrier()
                    ar.reset()
                else:
                    kb.dma('sp', [(ap[r0:r0 + rr, :], src[r0:r0 + rr, :])], [src.name], [ap.name])
    kb.finish()
    return nc, c


def make_consts():
    k = np.zeros((8, 128, 128), np.float32)
    i = np.arange(128)
    k[0] = np.eye(128, dtype=np.float32)
    k[1] = (i[:, None] <= i[None, :]).astype(np.float32)
    k[2] = np.where(i[:, None] >= i[None, :], 0.0, NEG)
    k[3] = (i[:, None] > i[None, :]).astype(np.float32)
    k[4] = np.where(i[None, :] <= i[:, None], 0.0, -1e30)
    k[5] = (i[:, None] <= i[None, :]).astype(np.float32)
    k[6] = (i[:, None] < 4).astype(np.float32) * np.ones((1, 128), np.float32)
    k[7] = i[:, None].astype(np.float32) * np.ones((1, 128), np.float32)
    return k


def make_consts2(c):
    NCH = 32
    IH = c['IH']
    k = np.zeros((128, 512), np.float32)
    for r in range(4 * IH):
        t = r % 4
        k[r, t * NCH + NCH - 1] = 1.0
        k[r, 256 + t] = 1.0
    for t in range(4):
        k[t, 260 + t * NCH:260 + (t + 1) * NCH] = 1.0
    return k


def shard_inputs(cfg, inputs, n_cores=8):
    c = derive(cfg)
    NB, NSB = c['NB'], c['NSB']
    f = lambda a: np.ascontiguousarray(np.asarray(a))
    rep = {}
    DEPTH = c['DEPTH']
    rep['cache_k'] = f(inputs['cache_k']).reshape(DEPTH, -1, c['CKVW'])
    rep['cache_v'] = f(inputs['cache_v']).reshape(DEPTH, -1, c['CKVW'])
    rep['cache_kidx'] = f(inputs['cache_kidx']).reshape(DEPTH, -1, 128)
    for k in ('ln1', 'w_in', 'a_ln_g', 'a_ln_b', 'a_ws', 'a_bs', 'b_conv_w', 'b_a_log', 'b_dt_bias', 'b_out_g',
              'w_br', 'w_o', 'ln2', 'ffn_w1', 'ffn_w3', 'ffn_w2'):
        rep[k] = f(inputs[k])
    rep['ln_f'] = f(inputs['ln_f']).reshape(1, -1)
    rep['consts'] = make_consts()
    rep['consts2'] = make_consts2(c)
    maps = []
    for core in range(n_cores):
        m = dict(rep)
        m['xp'] = f(inputs['x_prompt'][core % NB])
        sb = core % NSB
        m['xs'] = f(inputs['x_sample'][sb])
        m['sconv'] = f(inputs['state_conv'][:, sb])
        m['sdelta'] = f(inputs['state_delta'][:, sb])
        m['ptab'] = f(inputs['page_table'][sb:sb + 1]).astype(np.int32)
        maps.append(m)
    return maps


_CACHE = {}


def kernel(**inputs):
    cfg = REAL_CFG
    if 'nc' not in _CACHE:
        _CACHE['nc'] = build(cfg)
    nc, c = _CACHE['nc']
    maps = shard_inputs(cfg, inputs)
    res = run_bass_kernel_spmd(nc, maps, core_ids=list(range(8)))
    return assemble(c, res.results)


def assemble(c, R):
    NB, NSB, DEPTH = c['NB'], c['NSB'], c['DEPTH']
    T, D = c['T'], c['D']
    stackp = lambda name: np.stack([R[b][name] for b in range(NB)], axis=1)
    stacks = lambda name: np.stack([R[b][name] for b in range(NSB)], axis=1)
    y_prompt = np.stack([R[b]['yp'] for b in range(NB)], 0)
    y_sample = np.stack([R[b]['ys'] for b in range(NSB)], 0)
    return (y_prompt, y_sample,
            stackp('okp').reshape(DEPTH, NB, T, c['CKV'], 128), stackp('ovp').reshape(DEPTH, NB, T, c['CKV'], 128),
            stackp('oikp'), stackp('oconvp'), stackp('odeltap'),
            stacks('oks').reshape(DEPTH, NSB, 4, c['CKV'], 128), stacks('ovs').reshape(DEPTH, NSB, 4, c['CKV'], 128),
            stacks('oiks'), stacks('oconvs'), stacks('odeltas'), stacks('ochunk'))
```
